# Optimizing a Trainium2 kernel written in Bass

```python
import math, functools
import jax, jax.numpy as jnp
from jax import lax
import numpy as np

D_MODEL = 1024
BATCH = 8
SEQ = 2048
DEPTH = 1
DEC_BATCH = 32
DEC_SEQ = 1
PAST_LEN = 16384
PAGE_SIZE = 128

HEAD_DIM = 128
MIX_WIDTH = D_MODEL
H_A = MIX_WIDTH // (2 * HEAD_DIM)
H_B = MIX_WIDTH // HEAD_DIM - H_A
GDN_QKV = 3 * H_A * HEAD_DIM
MOBA_QKV = 3 * H_B * HEAD_DIM
SPLITS = [GDN_QKV, GDN_QKV + H_A * HEAD_DIM, GDN_QKV + H_A * HEAD_DIM + H_A, GDN_QKV + H_A * HEAD_DIM + 2 * H_A]
IN_COLS = GDN_QKV + H_A * HEAD_DIM + 2 * H_A + MOBA_QKV
CONV_W = 4
CHUNK = 64
MOBA_BLOCK = 256
MOBA_TOPK = 3
QBLOCK = 32
ROT_DIM = HEAD_DIM // 4
ROPE_THETA = 500000.0
D_FF = -(-8 * D_MODEL // (3 * 256)) * 256
EPS = 1e-6

kernel_name = 'hymba_gdn_moba_step'


def rmsnorm(x, w):
    xf = x.astype(jnp.float32)
    y = xf * lax.rsqrt(jnp.mean(xf * xf, axis=-1, keepdims=True) + EPS)
    return (y * w.astype(jnp.float32)).astype(x.dtype)


def l2norm(x):
    xf = x.astype(jnp.float32)
    return xf * lax.rsqrt(jnp.sum(xf * xf, axis=-1, keepdims=True) + EPS)


def rotary(x, pos):
    half = ROT_DIM // 2
    inv_freq = ROPE_THETA ** (-jnp.arange(half, dtype=jnp.float32) / half)
    ang = pos.astype(jnp.float32)[:, None] * inv_freq[None, :]
    cos = jnp.cos(ang)[None, :, None, :]
    sin = jnp.sin(ang)[None, :, None, :]
    xr = x[..., :ROT_DIM].astype(jnp.float32)
    x1, x2 = xr[..., :half], xr[..., half:]
    rot = jnp.concatenate([x1 * cos - x2 * sin, x2 * cos + x1 * sin], axis=-1)
    return jnp.concatenate([rot.astype(x.dtype), x[..., ROT_DIM:]], axis=-1)


def short_conv(xc, prev, w):
    L = xc.shape[1]
    full = jnp.concatenate([prev.astype(xc.dtype), xc], axis=1)
    y = full[:, 0:L] * w[0]
    for j in range(1, CONV_W):
        y = y + full[:, j:j + L] * w[j]
    return jax.nn.silu(y), full[:, L:]


def gated_delta_rule(q, k, v, g, beta, s0):
    B, L, H, _ = q.shape
    pad = (-L) % CHUNK
    n = (L + pad) // CHUNK

    def chunks(t):
        t = jnp.pad(t.astype(jnp.float32), [(0, 0), (0, pad)] + [(0, 0)] * (t.ndim - 2))
        t = t.reshape((B, n, CHUNK) + t.shape[2:])
        return jnp.swapaxes(jnp.swapaxes(t, 2, 3), 0, 1)

    qc, kc, vc, gc, bc = chunks(q), chunks(k), chunks(v), chunks(g), chunks(beta)
    gcum = jnp.cumsum(gc, axis=-1)
    tri = jnp.tril(jnp.ones((CHUNK, CHUNK), bool))
    strict = jnp.tril(jnp.ones((CHUNK, CHUNK), bool), -1)
    decay = jnp.exp(jnp.where(tri, gcum[..., :, None] - gcum[..., None, :], -jnp.inf))
    kb = kc * bc[..., None]
    a = jnp.where(strict, jnp.einsum('nbhid,nbhjd->nbhij', kb, kc) * decay, 0.0)
    eye = jnp.eye(CHUNK, dtype=jnp.float32)
    tinv = lax.linalg.triangular_solve(eye + a, jnp.broadcast_to(eye, a.shape), left_side=True,
                                       lower=True, unit_diagonal=True)
    u = tinv @ (vc * bc[..., None])
    w = tinv @ (kb * jnp.exp(gcum)[..., None])
    qk = jnp.einsum('nbhid,nbhjd->nbhij', qc, kc) * decay
    q_dec = qc * jnp.exp(gcum)[..., None]
    g_last = gcum[..., -1]
    k_dec = kc * jnp.exp(g_last[..., None] - gcum)[..., None]

    def step(S, xs):
        u_i, w_i, qk_i, qd_i, kd_i, gl_i = xs
        v_new = u_i - w_i @ S
        o = qd_i @ S + qk_i @ v_new
        S = S * jnp.exp(gl_i)[..., None, None] + jnp.swapaxes(kd_i, -1, -2) @ v_new
        return S, o

    S, o = lax.scan(step, s0.astype(jnp.float32), (u, w, qk, q_dec, k_dec, g_last))
    o = jnp.swapaxes(jnp.swapaxes(o, 0, 1), 2, 3).reshape(B, n * CHUNK, H, -1)[:, :L]
    return o, S


def moba_attend(q, q_pos, k_means, fetch):
    nb = k_means.shape[2]
    n_past = q_pos // MOBA_BLOCK
    qf = q.astype(jnp.float32)
    gate = jnp.einsum('bhqd,bhnd->bhqn', qf, k_means.astype(jnp.float32))
    is_past = jnp.arange(nb)[None, :] < n_past[:, None]
    gate = jnp.where(is_past, gate, -jnp.inf)
    if nb < MOBA_TOPK:
        gate = jnp.pad(gate, ((0, 0), (0, 0), (0, 0), (0, MOBA_TOPK - nb)), constant_values=-jnp.inf)
    _, idx = lax.top_k(gate, MOBA_TOPK)
    idx = jnp.minimum(idx, nb - 1)
    slot_valid = jnp.arange(MOBA_TOPK)[None, :] < n_past[:, None]
    own = jnp.broadcast_to(n_past[:, None], idx.shape[:-1] + (1,))
    blocks = jnp.concatenate([idx, own.astype(idx.dtype)], axis=-1)
    pos = blocks[..., None] * MOBA_BLOCK + jnp.arange(MOBA_BLOCK)
    slots = jnp.concatenate([slot_valid, jnp.ones((q_pos.shape[0], 1), bool)], axis=-1)
    mask = slots[:, :, None] & (pos <= q_pos[:, None, None])
    kg, vg = fetch(pos)
    logits = jnp.einsum('bhqd,bhqnkd->bhqnk', qf, kg.astype(jnp.float32)) * (HEAD_DIM ** -0.5)
    logits = jnp.where(mask, logits, -jnp.inf)
    sh = logits.shape
    p = jax.nn.softmax(logits.reshape(sh[:3] + (-1,)), axis=-1).reshape(sh)
    out = jnp.einsum('bhqnk,bhqnkd->bhqd', p, vg.astype(jnp.float32))
    return out.astype(q.dtype)


def moba_prompt(q, k, v):
    B, L, H, D = q.shape
    nb = -(-L // MOBA_BLOCK)
    pad = nb * MOBA_BLOCK - L
    k_log = jnp.pad(k, ((0, 0), (0, pad), (0, 0), (0, 0)))
    v_log = jnp.pad(v, ((0, 0), (0, pad), (0, 0), (0, 0)))
    k_means = jnp.swapaxes(k_log.astype(jnp.float32).reshape(B, nb, MOBA_BLOCK, H, D).mean(axis=2), 1, 2)
    bi = jnp.arange(B)[:, None, None, None, None]
    hi = jnp.arange(H)[None, :, None, None, None]

    def fetch(pos):
        return k_log[bi, pos, hi], v_log[bi, pos, hi]

    nqb = L // QBLOCK
    qblk = q.reshape(B, nqb, QBLOCK, H, D).transpose(1, 0, 3, 2, 4)
    pblk = jnp.arange(L, dtype=jnp.int32).reshape(nqb, QBLOCK)
    out = lax.map(lambda qp: moba_attend(qp[0], qp[1], k_means, fetch), (qblk, pblk))
    return out.transpose(1, 0, 3, 2, 4).reshape(B, L, H * D)


def moba_sample(cache_k, cache_v, page_table, q, k, v):
    DB, LS, H, D = q.shape
    total = PAST_LEN + LS
    nb = -(-total // MOBA_BLOCK)
    ppb = MOBA_BLOCK // PAGE_SIZE
    n_pages = page_table.shape[1]
    page_sums = cache_k[page_table].astype(jnp.float32).sum(axis=2)
    page_sums = jnp.pad(page_sums, ((0, 0), (0, nb * ppb - n_pages), (0, 0), (0, 0)))
    blk = page_sums.reshape(DB, nb, ppb, H, D).sum(axis=2)
    new_pos = PAST_LEN + jnp.arange(LS, dtype=jnp.int32)
    blk = blk.at[:, new_pos // MOBA_BLOCK].add(k.astype(jnp.float32))
    k_means = jnp.swapaxes(blk / MOBA_BLOCK, 1, 2)
    bi = jnp.arange(DB)[:, None, None, None, None]
    hi = jnp.arange(H)[None, :, None, None, None]

    def fetch(pos):
        past_pos = jnp.minimum(pos, PAST_LEN - 1)
        phys = page_table[bi, past_pos // PAGE_SIZE]
        off = past_pos % PAGE_SIZE
        new_idx = jnp.clip(pos - PAST_LEN, 0, LS - 1)
        from_past = (pos < PAST_LEN)[..., None]
        kg = jnp.where(from_past, cache_k[phys, off, hi], k[bi, new_idx, hi])
        vg = jnp.where(from_past, cache_v[phys, off, hi], v[bi, new_idx, hi])
        return kg, vg

    out = moba_attend(jnp.swapaxes(q, 1, 2), new_pos, k_means, fetch)
    return jnp.swapaxes(out, 1, 2).reshape(DB, LS, H * D)


def layer_step(x, pos, gdn_s0, conv_s0, moba_fn, norm1_w, w_in, conv_w, a_log, dt_bias, gdn_norm_w,
               w_out, norm2_w, w_gate, w_up, w_down):
    B, L, _ = x.shape
    proj = rmsnorm(x, norm1_w) @ w_in
    qkv_a, z, a, b, qkv_b = jnp.split(proj, SPLITS, axis=-1)
    conv_out, conv_new = short_conv(qkv_a, conv_s0, conv_w)
    qa, ka, va = [t.reshape(B, L, H_A, HEAD_DIM) for t in jnp.split(conv_out, 3, axis=-1)]
    qa = l2norm(qa) * (HEAD_DIM ** -0.5)
    ka = l2norm(ka)
    g = -jnp.exp(a_log.astype(jnp.float32)) * jax.nn.softplus(a.astype(jnp.float32) + dt_bias.astype(jnp.float32))
    beta = jax.nn.sigmoid(b.astype(jnp.float32))
    oa, s_new = gated_delta_rule(qa, ka, va, g, beta, gdn_s0)
    oa = rmsnorm(oa, gdn_norm_w) * jax.nn.silu(z.reshape(B, L, H_A, HEAD_DIM).astype(jnp.float32))
    oa = oa.reshape(B, L, H_A * HEAD_DIM).astype(x.dtype)
    qb, kb, vb = [t.reshape(B, L, H_B, HEAD_DIM) for t in jnp.split(qkv_b, 3, axis=-1)]
    qb = rotary(qb, pos)
    kb = rotary(kb, pos)
    ob = moba_fn(qb, kb, vb)
    x = x + jnp.concatenate([oa, ob], axis=-1) @ w_out
    h = rmsnorm(x, norm2_w)
    x = x + (jax.nn.silu(h @ w_gate) * (h @ w_up)) @ w_down
    return x, kb, vb, s_new, conv_new


def setup_inputs(seed: int = 0) -> dict:
    key = jax.random.key(seed)
    ks = jax.random.split(key, 20)
    n_pages = PAST_LEN // PAGE_SIZE
    n_used = DEC_BATCH * n_pages
    n_phys = n_used + -(-n_used // 4)
    nrm = jax.random.normal
    page_table = jax.random.permutation(ks[4], n_phys)[:n_used].reshape(DEC_BATCH, n_pages).astype(jnp.int32)
    dt = jnp.exp(jax.random.uniform(ks[9], (DEPTH, H_A)) * (math.log(0.1) - math.log(0.001)) + math.log(0.001))
    return {
        'x_prompt': nrm(ks[0], (BATCH, SEQ, D_MODEL), jnp.float32),
        'x_sample': nrm(ks[1], (DEC_BATCH, DEC_SEQ, D_MODEL), jnp.float32),
        'cache_k': nrm(ks[2], (DEPTH, n_phys, PAGE_SIZE, H_B, HEAD_DIM), jnp.float32),
        'cache_v': nrm(ks[3], (DEPTH, n_phys, PAGE_SIZE, H_B, HEAD_DIM), jnp.float32),
        'page_table': page_table,
        'state_gdn': nrm(ks[5], (DEPTH, DEC_BATCH, H_A, HEAD_DIM, HEAD_DIM), jnp.float32) * HEAD_DIM ** -0.5,
        'state_conv': nrm(ks[6], (DEPTH, DEC_BATCH, CONV_W - 1, GDN_QKV), jnp.float32),
        'norm1_w': 1.0 + 0.1 * nrm(ks[7], (DEPTH, D_MODEL), jnp.float32),
        'w_in': nrm(ks[8], (DEPTH, D_MODEL, IN_COLS), jnp.float32) * D_MODEL ** -0.5,
        'conv_w': nrm(ks[10], (DEPTH, CONV_W, GDN_QKV), jnp.float32) * CONV_W ** -0.5,
        'a_log': jnp.log(jax.random.uniform(ks[11], (DEPTH, H_A), minval=1.0, maxval=16.0)),
        'dt_bias': dt + jnp.log(-jnp.expm1(-dt)),
        'gdn_norm_w': 1.0 + 0.1 * nrm(ks[12], (DEPTH, HEAD_DIM), jnp.float32),
        'w_out': nrm(ks[13], (DEPTH, MIX_WIDTH, D_MODEL), jnp.float32) * MIX_WIDTH ** -0.5,
        'norm2_w': 1.0 + 0.1 * nrm(ks[14], (DEPTH, D_MODEL), jnp.float32),
        'w_gate': nrm(ks[15], (DEPTH, D_MODEL, D_FF), jnp.float32) * D_MODEL ** -0.5,
        'w_up': nrm(ks[16], (DEPTH, D_MODEL, D_FF), jnp.float32) * D_MODEL ** -0.5,
        'w_down': nrm(ks[17], (DEPTH, D_FF, D_MODEL), jnp.float32) * D_FF ** -0.5,
        'final_norm_w': 1.0 + 0.1 * nrm(ks[18], (D_MODEL,), jnp.float32),
    }


def reference(x_prompt, x_sample, cache_k, cache_v, page_table, state_gdn, state_conv, norm1_w, w_in,
              conv_w, a_log, dt_bias, gdn_norm_w, w_out, norm2_w, w_gate, w_up, w_down, final_norm_w):
    pos_p = jnp.arange(x_prompt.shape[1], dtype=jnp.int32)
    pos_s = PAST_LEN + jnp.arange(x_sample.shape[1], dtype=jnp.int32)
    xp, xs = x_prompt, x_sample
    kp, vp, ks, vs, gp, gs, cp, cs = [], [], [], [], [], [], [], []
    for layer in range(DEPTH):
        weights = (norm1_w[layer], w_in[layer], conv_w[layer], a_log[layer], dt_bias[layer], gdn_norm_w[layer],
                   w_out[layer], norm2_w[layer], w_gate[layer], w_up[layer], w_down[layer])
        s0 = jnp.zeros((xp.shape[0], H_A, HEAD_DIM, HEAD_DIM), jnp.float32)
        c0 = jnp.zeros((xp.shape[0], CONV_W - 1, GDN_QKV), xp.dtype)
        xp, k1, v1, s1, c1 = layer_step(xp, pos_p, s0, c0, moba_prompt, *weights)
        sample_mixer = functools.partial(moba_sample, cache_k[layer], cache_v[layer], page_table)
        xs, k2, v2, s2, c2 = layer_step(xs, pos_s, state_gdn[layer], state_conv[layer], sample_mixer, *weights)
        kp.append(k1); vp.append(v1); gp.append(s1); cp.append(c1)
        ks.append(k2); vs.append(v2); gs.append(s2); cs.append(c2)
    y_prompt = rmsnorm(xp, final_norm_w)
    y_sample = rmsnorm(xs, final_norm_w)
    return (y_prompt, y_sample, jnp.stack(kp), jnp.stack(vp), jnp.stack(ks), jnp.stack(vs),
            jnp.stack(gp), jnp.stack(gs), jnp.stack(cp), jnp.stack(cs))
```

```python
import math
import os
import numpy as np
import concourse.bass as bass
import concourse.mybir as mybir
from concourse.bass_utils import run_bass_kernel_spmd

F32 = mybir.dt.float32
BF16 = mybir.dt.bfloat16
I32 = mybir.dt.int32
U32 = mybir.dt.uint32
AF = mybir.ActivationFunctionType
ALU = mybir.AluOpType
AX = mybir.AxisListType

T = 2048
NT = 16
D = 1024
KC = 8
HD = 128
NH = 4
GQ = 1536
INC = 3592
DFF = 2816
FC = 22
ZOFF, AOFF, BOFF, QBOFF, KBOFF, VBOFF = 1536, 2048, 2052, 2056, 2568, 3080
EPS = 1e-6
NEG = -30000.0
SCALE = HD ** -0.5
NS = 4
NCORES = 8
_STOP = os.environ.get("KSTOP", "")
_KNT = int(os.environ.get("KNT", "16"))
_NF32 = int(os.environ.get("KNF32", "1"))
_KSUB = int(os.environ.get("KSUB", "99"))


class Tile:
    def __init__(self, name):
        self.name = name
        self.w = None
        self.r = []
        self.dsem = None
        self.dcnt = 0
        self.excl = False


class Buf:
    def __init__(self, ap, name):
        self.ap = ap
        self.t = Tile(name)


class Sched:
    def __init__(self, nc):
        self.nc = nc
        self.eng = {'pe': nc.tensor, 'act': nc.scalar, 'dve': nc.vector, 'pool': nc.gpsimd, 'sp': nc.sync}
        self.sem = {}
        self.cnt = {}
        for k in ['pe', 'act', 'dve', 'pool']:
            self.sem[k] = nc.alloc_semaphore("sem_" + k)
            self.cnt[k] = 0
        self.waited = {k: {} for k in self.eng}
        self.out_stamps = []
        self.dtiles = []
        self.free_dsems = []
        self.ninst = 0

    def _wait(self, e, deps):
        best = {}
        for (sem, val) in deps:
            key = id(sem)
            if key not in best or best[key][1] < val:
                best[key] = (sem, val)
        for key, (sem, val) in best.items():
            if self.waited[e].get(key, 0) < val:
                self.eng[e].wait_ge(sem, val)
                self.waited[e][key] = val
                self.ninst += 1

    def _deps(self, e, reads, writes, skip_sem=None):
        deps = []
        mysem = self.sem.get(e)
        for t in reads:
            if t.w is not None:
                deps.append(t.w)
            if t.excl:
                for st in t.r:
                    if st[0] is not mysem:
                        deps.append(st)
        for t in writes:
            if t.w is not None and t.w[0] is not mysem and t.w[0] is not skip_sem:
                deps.append(t.w)
            for st in t.r:
                if st[0] is not mysem:
                    deps.append(st)
        return deps

    def op(self, e, fn, reads=(), writes=(), inc=True):
        reads = [b.t for b in reads]
        writes = [b.t for b in writes]
        self._wait(e, self._deps(e, reads, writes))
        inst = fn(self.eng[e])
        self.ninst += 1
        if inc:
            self.cnt[e] += 1
            inst.then_inc(self.sem[e], 1)
            stamp = (self.sem[e], self.cnt[e])
        else:
            stamp = (self.sem[e], self.cnt[e] + 1)
        for t in writes:
            t.w = stamp
            t.r = []
        for t in reads:
            t.r.append(stamp)
        return inst

    def dma(self, q, out, in_, reads=(), writes=(), is_output=False, indirect=None, owner=None, **kw):
        reads = [b.t for b in reads]
        writes = [b.t for b in writes]
        if owner is None:
            owner = writes[0] if writes else reads[0]
        else:
            owner = owner.t
        if owner.dsem is None:
            owner.dsem = self.nc.alloc_semaphore("d%d_%s" % (len(self.dtiles), owner.name))
            self.dtiles.append(owner)
        self._wait(q, self._deps(None, reads, writes, skip_sem=owner.dsem))
        if indirect is not None:
            inst = self.eng[q].indirect_dma_start(out=out, out_offset=None, in_=in_, in_offset=indirect)
        else:
            inst = self.eng[q].dma_start(out=out, in_=in_, **kw)
        self.ninst += 1
        owner.dcnt += 16
        inst.then_inc(owner.dsem, 16)
        stamp = (owner.dsem, owner.dcnt)
        for t in writes:
            t.w = stamp
            t.r = []
        for t in reads:
            t.r.append(stamp)
        if is_output:
            self.out_stamps.append(stamp)
        return inst

    def barrier(self):
        stamps = [(self.sem[k], self.cnt[k]) for k in self.sem if self.cnt[k] > 0]
        stamps += [(t.dsem, t.dcnt) for t in self.dtiles if t.dcnt > 0]
        for e in self.eng:
            self._wait(e, stamps)

    def finish(self):
        self._wait('sp', self.out_stamps)


class Arena:
    def __init__(self, nc, words):
        self.t = nc.alloc_sbuf_tensor("arena", [128, words], F32)
        self.top = 0
        self.words = words
        self.n = 0

    def mark(self):
        return self.top

    def reset(self, m):
        self.top = m

    def _take(self, w):
        o = self.top
        self.top += w
        assert self.top <= self.words, ("SBUF arena overflow", self.top, self.words)
        self.n += 1
        return o

    def f32(self, n, parts=128, name=None):
        o = self._take(n)
        return Buf(self.t[0:parts, o:o + n], name or ("b%d" % self.n))

    def bf(self, n, parts=128, name=None):
        w = (n + 1) // 2
        o = self._take(w)
        return Buf(self.t[0:parts, o:o + w].bitcast(BF16)[:, 0:n], name or ("b%d" % self.n))

    def i32(self, n, parts=128, name=None):
        o = self._take(n)
        return Buf(self.t[0:parts, o:o + n].bitcast(I32), name or ("b%d" % self.n))

    def u32(self, n, parts=128, name=None):
        o = self._take(n)
        return Buf(self.t[0:parts, o:o + n].bitcast(U32), name or ("b%d" % self.n))


def build(PAST):
    NPG = PAST // 128
    NB = NPG // 2
    NPHYS = NS * NCORES * NPG + -(-(NS * NCORES * NPG) // 4)
    nc = bass.Bass("TRN2", target_bir_lowering=False)

    def din(name, shape, dt=F32):
        return nc.dram_tensor(name, list(shape), dt, kind="ExternalInput").ap()

    def dout(name, shape, dt=F32):
        return nc.dram_tensor(name, list(shape), dt, kind="ExternalOutput").ap()

    xp = din("xp", [T, D])
    xs = din("xs", [NS, D])
    ck = din("ck", [NPHYS, 128, NH, HD])
    cv = din("cv", [NPHYS, 128, NH, HD])
    pt = din("pt", [NS, NPG], I32)
    sg = din("sg", [NS, NH, HD, HD])
    scd = din("sc", [NS, 3, GQ])
    n1w = din("n1w", [128, KC])
    n2w = din("n2w", [128, KC])
    fnw = din("fnw", [128, D])
    w_in = din("w_in", [D, INC])
    cwd = din("cw", [128, 12, 4])
    cw4d = din("cw4", [NS, 4, GQ])
    alogd = din("alog", [128, NH])
    dtbd = din("dtb", [128, NH])
    gnwd = din("gnw", [128, HD])
    wod = din("wo", [D, D])
    wgd = din("wg", [D, DFF])
    wud = din("wu", [D, DFF])
    wdd = din("wd", [DFF, D])
    cspd = din("csp", [T, 256])
    cssd = din("css", [NS, 256])

    yp = dout("yp", [T, D])
    ys = dout("ys", [NS, D])
    kp = dout("kp", [T, 512])
    vp = dout("vp", [T, 512])
    kso = dout("ks", [NS, 512])
    vso = dout("vs", [NS, 512])
    gp = dout("gp", [NH, HD, HD])
    gso = dout("gs", [NS * NH, HD, HD])
    cpo = dout("cp", [3, GQ])
    cso = dout("cs", [NS, 3, GQ])

    x2s = nc.dram_tensor("x2s", [T, D], F32, kind="Internal").ap()
    sqd = nc.dram_tensor("sqd", [NS, 1536], F32, kind="Internal").ap()
    soad = nc.dram_tensor("soad", [NS, 512], F32, kind="Internal").ap()

    S = Sched(nc)
    A = Arena(nc, 53200)
    PSB = []
    for i in range(8):
        PSB.append(Buf(nc.alloc_psum_tensor("ps%d" % i, [128, 512], F32)[:, :], "ps%d" % i))
        PSB[-1].t.excl = True
    rr = [0]
    rrl = [list(range(8))]

    def bank():
        b = PSB[rrl[0][rr[0] % len(rrl[0])]]
        rr[0] += 1
        return b

    def pbf(b):
        return b.ap[:, :].bitcast(BF16)

    def mm(out, pairs, reads, writes):
        n = len(pairs)
        for i, (l, r) in enumerate(pairs):
            S.op('pe', lambda e, l=l, r=r, i=i: e.matmul(out, lhsT=l, rhs=r, start=(i == 0), stop=(i == n - 1)),
                 reads=reads, writes=writes, inc=(i == n - 1))

    def tr(out, in_, ident, reads, writes, inc=True):
        S.op('pe', lambda e: e.transpose(out=out, in_=in_, identity=ident),
             reads=list(reads) + [ident_b if ident.dtype == BF16 else ident_f], writes=writes, inc=inc)

    def act(out, in_, func, reads, writes, **kw):
        S.op('act', lambda e: e.activation(out=out, in_=in_, func=func, **kw), reads=reads, writes=writes)

    def tt(out, in0, in1, op, reads, writes, e='dve'):
        S.op(e, lambda en: en.tensor_tensor(out=out, in0=in0, in1=in1, op=op), reads=reads, writes=writes)

    def ts(out, in0, s1, s2, op0, op1, reads, writes, e='dve'):
        if op1 is None:
            S.op(e, lambda en: en.tensor_scalar(out=out, in0=in0, scalar1=s1, scalar2=None, op0=op0), reads=reads, writes=writes)
        else:
            S.op(e, lambda en: en.tensor_scalar(out=out, in0=in0, scalar1=s1, scalar2=s2, op0=op0, op1=op1), reads=reads, writes=writes)

    def stt(out, in0, sc, in1, op0, op1, reads, writes):
        S.op('dve', lambda en: en.scalar_tensor_tensor(out=out, in0=in0, scalar=sc, in1=in1, op0=op0, op1=op1), reads=reads, writes=writes)

    def cp(out, in_, reads, writes, e='dve'):
        if e == 'act':
            S.op('act', lambda en: en.activation(out=out, in_=in_, func=AF.Copy), reads=reads, writes=writes)
        else:
            S.op(e, lambda en: en.tensor_copy(out=out, in_=in_), reads=reads, writes=writes)

    def red(out, in_, op, reads, writes):
        S.op('dve', lambda en: en.tensor_reduce(out=out, in_=in_, axis=AX.X, op=op), reads=reads, writes=writes)

    def mset(ap, val, writes, e='dve'):
        S.op(e, lambda en: en.memset(ap, val), writes=writes)

    def recip(ap, bufs):
        S.op('dve', lambda en: en.reciprocal(out=ap, in_=ap), reads=bufs, writes=bufs)

    def rsqrt(out, in_, scale, reads, writes):
        act(out, in_, AF.Sqrt, list(reads) + [epsb], writes, scale=scale, bias=epsb.ap[0:out.shape[0], :])
        recip(out, writes)

    def softplus(dst, tmp, src_bufs, n):
        stt(tmp, dst, -1.0, dst, ALU.mult, ALU.max, src_bufs, src_bufs)
        act(tmp, tmp, AF.Exp, src_bufs, src_bufs, scale=-1.0)
        act(tmp, tmp, AF.Ln, src_bufs + [ones_f], src_bufs, bias=ones_f.ap[0:n, 0:1])
        stt(dst, dst, 0.0, tmp, ALU.max, ALU.add, src_bufs, src_bufs)

    def bc(ap, shape):
        return ap.to_broadcast(list(shape))

    NBX = max(NB, 8)
    ident_f = A.f32(128, name="ident_f")
    ident_b = A.bf(128, name="ident_b")
    ones_f = A.f32(128, name="ones_f")
    ones_b = A.bf(128, name="ones_b")
    triU = A.f32(128, name="triU")
    maskU = A.f32(128, name="maskU")
    strictU = A.f32(128, name="strictU")
    cmask = A.f32(128, name="cmask")
    epsb = A.f32(1, name="epsb")
    negrow = A.f32(1, name="negrow")
    onehot4 = A.f32(NS * 4, name="onehot4")
    iotaNB = A.f32(NBX, parts=16, name="iotaNB")
    rowoff = A.f32(96, name="rowoff")
    pairsel = A.f32(NBX, name="pairsel")
    n1 = A.f32(KC, name="n1")
    n2 = A.f32(KC, name="n2")
    fnb = A.f32(D, name="fnb")
    cwt = A.f32(48, name="cwt")
    negA = A.f32(NH, name="negA")
    dtb = A.f32(NH, name="dtb")
    gnb = A.f32(HD, name="gnb")
    catTs = A.bf(KC * NS, name="catTs")
    assert A.top <= 2500, A.top

    def asel(buf, pattern, op, fill, base, cm):
        S.op('pool', lambda e: e.affine_select(out=buf.ap, in_=buf.ap, pattern=pattern, compare_op=op, fill=fill,
                                               base=base, channel_multiplier=cm), reads=[buf], writes=[buf])

    mset(ident_f.ap, 1.0, [ident_f], e='pool')
    asel(ident_f, [[-1, 128]], ALU.is_equal, 0.0, 0, 1)
    cp(ident_b.ap, ident_f.ap, [ident_f], [ident_b], e='pool')
    mset(ones_f.ap, 1.0, [ones_f], e='pool')
    mset(ones_b.ap, 1.0, [ones_b], e='pool')
    mset(triU.ap, 1.0, [triU], e='pool')
    asel(triU, [[1, 128]], ALU.is_ge, 0.0, 0, -1)
    mset(maskU.ap, 0.0, [maskU], e='pool')
    asel(maskU, [[1, 128]], ALU.is_ge, NEG, 0, -1)
    mset(strictU.ap, 1.0, [strictU], e='pool')
    asel(strictU, [[1, 128]], ALU.is_gt, 0.0, 0, -1)
    mset(cmask.ap, 0.0, [cmask], e='pool')
    asel(cmask, [[-1, 128]], ALU.is_ge, NEG, 0, 1)
    mset(epsb.ap, EPS, [epsb], e='pool')
    mset(negrow.ap, 0.0, [negrow], e='pool')
    asel(negrow, [[0, 1]], ALU.is_ge, NEG, 0, -1)
    mset(onehot4.ap, 1.0, [onehot4], e='pool')
    asel(onehot4, [[1, NS], [-1, 4]], ALU.is_equal, 0.0, 0, 0)
    S.op('pool', lambda e: e.iota(iotaNB.ap, pattern=[[1, NBX]], base=0, channel_multiplier=0,
                                  allow_small_or_imprecise_dtypes=True), writes=[iotaNB])
    S.op('pool', lambda e: e.iota(rowoff.ap, pattern=[[0, NS], [1, NH], [0, 6]], base=0, channel_multiplier=4,
                                  allow_small_or_imprecise_dtypes=True), writes=[rowoff])
    mset(pairsel.ap, 1.0, [pairsel], e='pool')
    asel(pairsel, [[-2, NBX]], ALU.is_ge, 0.0, 0, 1)
    asel(pairsel, [[2, NBX]], ALU.is_ge, 0.0, 1, -1)

    constb = Buf(None, "constgrp")
    cl = [(n1, n1w), (n2, n2w), (fnb, fnw), (cwt, cwd.rearrange("p c j -> p (c j)")), (negA, alogd), (dtb, dtbd), (gnb, gnwd)]
    for (b_, d_) in cl:
        S.dma('sp', b_.ap, d_, writes=[b_], owner=constb)
    for (b_, _) in cl:
        b_.t.w = (constb.t.dsem, constb.t.dcnt)
    act(negA.ap, negA.ap, AF.Exp, [negA], [negA])
    ts(negA.ap, negA.ap, -1.0, None, ALU.mult, None, [negA], [negA])

    O_WI, O_WORK, O_P2, O_OBT, O_OAT = 2500, 16900, 36100, 44900, 49000
    A.top = O_WI
    wi = A.bf(KC * INC, name="wi")
    assert A.top <= O_WORK
    wi3 = wi.ap.rearrange("p (c n) -> p c n", c=KC)
    for c in range(KC):
        for (a0, a1) in ((0, 1796), (1796, INC)):
            S.dma('pool', wi3[:, c, a0:a1], w_in[c * 128:(c + 1) * 128, a0:a1], writes=[wi])

    A.top = O_WORK
    P4 = NS
    xs_sb = A.f32(D, parts=P4, name="xs_sb")
    css = A.f32(256, parts=P4, name="css")
    prj = A.f32(INC, parts=P4, name="prj")
    cats = A.f32(D, parts=P4, name="cats")
    catb = A.bf(D, parts=P4, name="catb")
    s1 = A.f32(GQ, parts=P4, name="s1")
    s2 = A.f32(GQ, parts=P4, name="s2")
    sm = A.f32(96, parts=P4, name="sm")
    mark_sk = A.mark()
    bufA = A.f32(GQ, parts=P4, name="bufA")
    bufB = A.f32(GQ, parts=P4, name="bufB")
    qkvs = A.f32(GQ, parts=P4, name="qkvs")
    xns = A.bf(D, parts=P4, name="xns")
    xnTs = A.bf(KC * NS, name="xnTs")
    xnTs3 = xnTs.ap.rearrange("p (c b) -> p c b", c=KC)
    Ss = A.f32(16 * 128, name="Ss")
    Ss3 = Ss.ap.rearrange("p (g v) -> p g v", g=16)
    Sout = A.f32(16 * 128, name="Sout")
    Sout3 = Sout.ap.rearrange("p (g v) -> p g v", g=16)
    qkTs = A.f32(32, name="qkTs")
    qkTs4 = qkTs.ap.rearrange("p (s h b) -> p s h b", s=2, h=NH)
    kqS = A.f32(32, name="kqS")
    kqS3 = kqS.ap.rearrange("p (g s) -> p g s", g=16)
    kqtm = A.f32(1024, parts=P4, name="kqtm")
    kqtm4 = kqtm.ap.rearrange("p (s h v) -> p s h v", s=2, h=NH)
    vn = A.f32(512, parts=P4, name="vn")
    os_ = A.f32(512, parts=P4, name="os_")
    knm = A.f32(NS * 512, parts=P4, name="knm")
    Rg = A.f32(16, parts=P4, name="Rg")
    egbc = A.f32(16, name="egbc")

    S.dma('sp', xs_sb.ap, xs[:, :], writes=[xs_sb])
    S.dma('sp', css.ap, cssd[:, :], writes=[css])
    S.dma('sp', Ss3, sg.rearrange("b h k v -> k (b h) v"), writes=[Ss])
    ddum = Buf(None, "ddum")
    S.dma('sp', cso[:, 0:2, :], scd[:, 1:3, :], owner=ddum, is_output=True)
    act(s1.ap[:, 0:D], xs_sb.ap, AF.Square, [xs_sb], [s1, sm], accum_out=sm.ap[:, 0:1])
    rsqrt(sm.ap[:, 0:1], sm.ap[:, 0:1], 1.0 / D, [sm], [sm])
    ts(xns.ap, xs_sb.ap, sm.ap[:, 0:1], None, ALU.mult, None, [xs_sb, sm], [xns])
    pb = bank()
    for c in range(KC):
        tr(pbf(pb)[:, c * NS:(c + 1) * NS], xns.ap[:, c * 128:(c + 1) * 128], ident_b.ap[0:P4, 0:P4], [xns], [pb], inc=(c == KC - 1))
    tt(xnTs3, pbf(pb)[:, 0:KC * NS].rearrange("p (c b) -> p c b", c=KC), bc(n1.ap.unsqueeze(2), [128, KC, NS]), ALU.mult, [pb, n1], [xnTs])
    for j in range(8):
        c0 = j * 512
        w = min(512, INC - c0)
        pj = bank()
        mm(pj.ap[0:P4, 0:w], [(xnTs3[:, c, :], wi3[:, c, c0:c0 + w]) for c in range(KC)], [wi, xnTs], [pj])
        cp(prj.ap[:, c0:c0 + w], pj.ap[0:P4, 0:w], [pj], [prj], e=('act' if j % 2 == 0 else 'dve'))
    S.dma('pool', cso[:, 2, :], prj.ap[:, 0:GQ], reads=[prj], is_output=True)
    S.dma('sp', bufB.ap, cw4d[:, 3, :], writes=[bufB])
    tt(s1.ap, prj.ap[:, 0:GQ], bufB.ap, ALU.mult, [prj, bufB], [s1])
    for j in range(3):
        S.dma('sp', bufA.ap, scd[:, j, :], writes=[bufA])
        S.dma('sp', bufB.ap, cw4d[:, j, :], writes=[bufB])
        tt(s2.ap, bufA.ap, bufB.ap, ALU.mult, [bufA, bufB], [s2])
        tt(s1.ap, s1.ap, s2.ap, ALU.add, [s1, s2], [s1])
    act(qkvs.ap, s1.ap, AF.Silu, [s1], [qkvs])
    tt(s2.ap[:, 0:1024], qkvs.ap[:, 0:1024], qkvs.ap[:, 0:1024], ALU.mult, [qkvs], [s2])
    red(sm.ap[:, 8:16], s2.ap[:, 0:1024].rearrange("p (g d) -> p g d", g=8), ALU.add, [s2], [sm])
    rsqrt(sm.ap[:, 8:16], sm.ap[:, 8:16], 1.0, [sm], [sm])
    ts(sm.ap[:, 8:12], sm.ap[:, 8:12], SCALE, None, ALU.mult, None, [sm], [sm])
    qk8 = qkvs.ap[:, 0:1024].rearrange("p (g d) -> p g d", g=8)
    tt(qk8, qk8, bc(sm.ap[:, 8:16].unsqueeze(2), [P4, 8, 128]), ALU.mult, [qkvs, sm], [qkvs])
    qs3 = qkvs.ap[:, 0:512].rearrange("p (h d) -> p h d", h=NH)
    ks3 = qkvs.ap[:, 512:1024].rearrange("p (h d) -> p h d", h=NH)
    vs3 = qkvs.ap[:, 1024:1536].rearrange("p (h d) -> p h d", h=NH)
    sg0, sg1, sgg, sbeta, seg, sqk = (sm.ap[:, 16:20], sm.ap[:, 20:24], sm.ap[:, 24:28], sm.ap[:, 28:32],
                                      sm.ap[:, 32:36], sm.ap[:, 36:40])
    tt(sg0, prj.ap[:, AOFF:AOFF + 4], dtb.ap[0:P4, :], ALU.add, [prj, dtb], [sm])
    softplus(sg0, sg1, [sm], P4)
    tt(sgg, sg0, negA.ap[0:P4, :], ALU.mult, [sm, negA], [sm])
    act(sbeta, prj.ap[:, BOFF:BOFF + 4], AF.Sigmoid, [prj], [sm])
    act(seg, sgg, AF.Exp, [sm], [sm])
    pq = bank()
    for s_, src in enumerate((ks3, qs3)):
        for h in range(NH):
            tr(pq.ap[:, (s_ * NH + h) * NS:(s_ * NH + h + 1) * NS], src[:, h, :], ident_f.ap[0:P4, 0:P4], [qkvs], [pq],
               inc=(s_ == 1 and h == NH - 1))
    cp(qkTs.ap, pq.ap[:, 0:32], [pq], [qkTs])
    pq = bank()
    for b in range(NS):
        for h in range(NH):
            g = b * NH + h
            mm(pq.ap[:, g * 2:g * 2 + 2], [(Ss3[:, g, :], qkTs4[:, :, h, b])], [Ss, qkTs], [pq])
    cp(kqS.ap, pq.ap[:, 0:32], [pq], [kqS])
    kqS4 = kqS.ap.rearrange("p (b h s) -> p b h s", b=NS, h=NH)
    pk2 = [bank(), bank()]
    for s_ in range(2):
        for h in range(NH):
            tr(pk2[s_].ap[0:P4, h * 128:(h + 1) * 128], kqS4[:, :, h, s_], ident_f.ap, [kqS], [pk2[s_]], inc=(h == NH - 1))
        cp(kqtm.ap[:, s_ * 512:(s_ + 1) * 512], pk2[s_].ap[0:P4, :], [pk2[s_]], [kqtm], e=('act' if s_ == 0 else 'dve'))
    kS, qS = kqtm4[:, 0], kqtm4[:, 1]
    vn3 = vn.ap.rearrange("p (h v) -> p h v", h=NH)
    os3 = os_.ap.rearrange("p (h v) -> p h v", h=NH)

    def b4(ap):
        return bc(ap.unsqueeze(2), [P4, NH, 128])
    tt(vn3, kS, b4(seg), ALU.mult, [kqtm, sm], [vn])
    tt(vn3, vs3, vn3, ALU.subtract, [qkvs, vn], [vn])
    tt(vn3, vn3, b4(sbeta), ALU.mult, [vn, sm], [vn])
    s23 = s2.ap[:, 0:512].rearrange("p (h d) -> p h d", h=NH)
    tt(s23, qs3, ks3, ALU.mult, [qkvs], [s2])
    red(sqk, s23, ALU.add, [s2], [sm])
    tt(os3, qS, b4(seg), ALU.mult, [kqtm, sm], [os_])
    tt(s23, vn3, b4(sqk), ALU.mult, [vn, sm], [s2])
    tt(os_.ap, os_.ap, s2.ap[:, 0:512], ALU.add, [os_, s2], [os_])
    R3 = Rg.ap.rearrange("p (b h) -> p b h", b=NS)
    oh = onehot4.ap.rearrange("p (b c) -> p b c", b=NS)
    for b in range(NS):
        ts(R3[:, b, :], seg, ident_f.ap[0:P4, b:b + 1], None, ALU.mult, None, [sm, ident_f], [Rg])
        ts(knm.ap[:, b * 512:(b + 1) * 512], qkvs.ap[:, 512:1024], ident_f.ap[0:P4, b:b + 1], None, ALU.mult, None, [qkvs, ident_f], [knm])
    pq = bank()
    mm(pq.ap[:, 0:16], [(ones_f.ap[0:P4, :], Rg.ap)], [ones_f, Rg], [pq])
    cp(egbc.ap, pq.ap[:, 0:16], [pq], [egbc])
    for b in range(NS):
        pq = bank()
        for h in range(NH):
            mm(pq.ap[:, h * 128:(h + 1) * 128], [(knm.ap[:, b * 512 + h * 128:b * 512 + (h + 1) * 128], vn3[:, h, :])], [knm, vn], [pq])
        for h in range(NH):
            g = b * NH + h
            stt(Sout3[:, g, :], Ss3[:, g, :], egbc.ap[:, g:g + 1], pq.ap[:, h * 128:(h + 1) * 128], ALU.mult, ALU.add, [Ss, egbc, pq], [Sout])
    S.dma('pool', gso.rearrange("g k v -> k g v"), Sout3, reads=[Sout], is_output=True)
    tt(s23, os3, os3, ALU.mult, [os_], [s2])
    red(sm.ap[:, 40:44], s23, ALU.add, [s2], [sm])
    rsqrt(sm.ap[:, 40:44], sm.ap[:, 40:44], 1.0 / HD, [sm], [sm])
    tt(os3, os3, b4(sm.ap[:, 40:44]), ALU.mult, [os_, sm], [os_])
    tt(os3, os3, bc(gnb.ap[0:P4, :].unsqueeze(1), [P4, NH, 128]), ALU.mult, [os_, gnb], [os_])
    act(s2.ap[:, 0:512], prj.ap[:, ZOFF:ZOFF + 512], AF.Silu, [prj], [s2])
    tt(cats.ap[:, 0:512], os_.ap, s2.ap[:, 0:512], ALU.mult, [os_, s2], [cats])
    qb8 = prj.ap[:, QBOFF:QBOFF + 1024].rearrange("p (g d) -> p g d", g=8)
    css4 = css.ap.rearrange("p (s g f) -> p s g f", s=2, g=8)
    rts = s1.ap[:, 0:512].rearrange("p (k g f) -> p k g f", k=4, g=8)
    x1, x2 = qb8[:, :, 0:16], qb8[:, :, 16:32]
    tt(rts[:, 0], x1, css4[:, 0], ALU.mult, [prj, css], [s1])
    tt(rts[:, 1], x2, css4[:, 1], ALU.mult, [prj, css], [s1])
    tt(rts[:, 2], x2, css4[:, 0], ALU.mult, [prj, css], [s1])
    tt(rts[:, 3], x1, css4[:, 1], ALU.mult, [prj, css], [s1])
    tt(x1, rts[:, 0], rts[:, 1], ALU.subtract, [s1], [prj])
    tt(x2, rts[:, 2], rts[:, 3], ALU.add, [s1], [prj])
    S.dma('pool', kso[:, :], prj.ap[:, KBOFF:KBOFF + 512], reads=[prj], is_output=True)
    S.dma('pool', vso[:, :], prj.ap[:, VBOFF:VBOFF + 512], reads=[prj], is_output=True)

    if _STOP == "Sa":
        S.finish()
        return nc
    S.dma('sp', sqd[:, :], prj.ap[:, QBOFF:QBOFF + 1536], reads=[prj])
    S.dma('sp', soad[:, :], cats.ap[:, 0:512], reads=[cats])
    S.barrier()
    GR = 1
    NCH = 128 // GR
    O_PSI = 34810
    A.top = O_PSI
    ptc = A.i32(NS, name="ptc")
    ptcf = A.f32(NS, name="ptcf")
    iotaC = A.f32(NCH, name="iotaC")
    idxgf = A.f32(NS * NCH, name="idxgf")
    idxg = A.i32(NS * NCH, name="idxg")
    assert A.top <= 36100, A.top
    A.top = 44900
    acc = A.f32(NS * 512, name="acc")
    Gb = [A.f32(512, name="G%d" % i_) for i_ in range(4)]
    assert A.top <= 49000, A.top
    with nc.allow_non_contiguous_dma(reason="tiny page-table transpose"):
        S.dma('sp', ptc.ap[0:NPG, :], pt.rearrange("b n -> n b"), writes=[ptc])
    ckp = ck.rearrange("n (c r) h d -> (n c) (r h d)", r=GR)
    S.op('pool', lambda e: e.iota(iotaC.ap, pattern=[[1, NCH]], base=0, channel_multiplier=0,
                                  allow_small_or_imprecise_dtypes=True), writes=[iotaC])
    cp(ptcf.ap[0:NPG, :], ptc.ap[0:NPG, :], [ptc], [ptcf])
    stt(idxgf.ap[0:NPG, :].rearrange("p (b c) -> p b c", b=NS), bc(ptcf.ap[0:NPG, :].unsqueeze(2), [NPG, NS, NCH]), float(NCH),
        bc(iotaC.ap[0:NPG, :].unsqueeze(1), [NPG, NS, NCH]), ALU.mult, ALU.add, [ptcf, iotaC], [idxgf])
    ts(idxgf.ap[0:NPG, :], idxgf.ap[0:NPG, :], 0.0, float(NPHYS * NCH - 1), ALU.max, ALU.min, [idxgf], [idxgf])
    cp(idxg.ap[0:NPG, :], idxgf.ap[0:NPG, :], [idxgf], [idxg])
    mset(acc.ap, 0.0, [acc])
    acc3 = acc.ap.rearrange("p (b c) -> p b c", b=NS)
    ps_state = [0]

    def pagesum_chunks(n):
        for _ in range(n):
            k = ps_state[0]
            if k >= NS * NCH:
                return
            ps_state[0] += 1
            b = k // NCH
            G = Gb[k % 4]
            S.dma('pool', G.ap[0:NPG, :], ckp[:, :], writes=[G], reads=[idxg],
                  indirect=bass.IndirectOffsetOnAxis(ap=idxg.ap[0:NPG, k:k + 1].bitcast(U32), axis=0))
            tt(acc3[0:NPG, b, :], acc3[0:NPG, b, :], G.ap[0:NPG, :], ALU.add, [acc, G], [acc], e='pool')

    if _STOP == "Sb":
        S.finish()
        return nc
    S.barrier()
    A.top = O_OAT
    oaT = A.bf(NH * T, name="oaT")
    A.top = O_OBT
    obT = A.bf(NH * T, name="obT")
    A.top = O_P2
    qbT = A.bf(NH * T, name="qbT")
    kbT = A.bf(NH * T, name="kbT")
    sel30 = A.f32(NT * 32, name="sel30")
    assert A.top <= O_OBT
    oaT3 = oaT.ap.rearrange("p (h t) -> p h t", h=NH)
    obT3 = obT.ap.rearrange("p (h t) -> p h t", h=NH)
    qbT3 = qbT.ap.rearrange("p (h t) -> p h t", h=NH)
    kbT3 = kbT.ap.rearrange("p (h t) -> p h t", h=NH)
    A.top = O_WORK
    ss = A.f32(1, name="ss")
    rstd = A.f32(1, name="rstd")
    xn = A.bf(D, name="xn")
    xnT = A.bf(D, name="xnT")
    xnT3 = xnT.ap.rearrange("p (c t) -> p c t", c=KC)
    raw = A.f32(12 * 131, name="raw")
    raw3 = raw.ap.rearrange("p (c t) -> p c t", c=12)
    cacc = A.f32(GQ, name="cacc")
    scr = A.f32(GQ, name="scr")
    xt = scr
    zs = A.bf(512, name="zs")
    tb = A.f32(GQ, name="tb")
    cst = A.f32(256, name="cst")
    absb = A.f32(8, name="absb")
    qkb = A.bf(1024, name="qkb")
    kmT = A.f32(NH * 8, name="kmT")
    kmTb = A.bf(NH * 8, name="kmTb")
    gsb = A.f32(32, name="gsb")
    top8 = A.f32(8, name="top8")
    gsm = A.f32(64, name="gsm")
    qkv = tb
    ssqk = A.f32(8, name="ssqk")
    ctab = A.f32(7 * 4, name="ctab")
    varb = A.bf(3 * 512, name="varb")
    varf = A.f32(4 * 512, name="varf")
    varTb = A.bf(3 * 512, name="varTb")
    qdTf = A.f32(512, name="qdTf")
    varb4 = varb.ap.rearrange("p (v h d) -> p v h d", v=3, h=NH)
    varf4 = varf.ap.rearrange("p (v h d) -> p v h d", v=4, h=NH)
    varTb4 = varTb.ap.rearrange("p (v h t) -> p v h t", v=3, h=NH)
    qdT3 = qdTf.ap.rearrange("p (h t) -> p h t", h=NH)
    qkT = A.f32(NH * 128, name="qkT")
    qkT3 = qkT.ap.rearrange("p (h t) -> p h t", h=NH)
    _mk = A.f32 if _NF32 else A.bf
    hsl = []
    for si in range(2):
        hsl.append({
            'dg': A.f32(128, name="dg%d" % si), 'decT': A.f32(128, name="decT%d" % si), 'decS': A.f32(128, name="decS%d" % si),
            'NTf': A.f32(128, name="NTf%d" % si),
            'Mm': [_mk(128, name="Mm0_%d" % si), _mk(128, name="Mm1_%d" % si)],
            'MT': [_mk(128, name="MT0_%d" % si), _mk(128, name="MT1_%d" % si)],
            'Qb': [_mk(128, name="Qb0_%d" % si), _mk(128, name="Qb1_%d" % si)],
            'banks': [PSB[3 * si + j_] for j_ in range(3)],
        })
    uu = A.f32(NH * 128, name="uu")
    wT = A.f32(NH * 128, name="wT")
    wT3 = wT.ap.rearrange("p (h t) -> p h t", h=NH)
    Sst = A.f32(NH * 128, name="Sst")
    vnew = A.f32(NH * 128, name="vnew")
    oss = A.f32(4, name="oss")
    print("phaseA top", A.top, O_P2)
    assert A.top <= O_P2, A.top

    mset(raw.ap, 0.0, [raw])
    mset(Sst.ap, 0.0, [Sst])
    mset(kmT.ap, 0.0, [kmT])
    mset(kmTb.ap, 0.0, [kmTb])
    mset(sel30.ap, 0.0, [sel30])
    cwt3 = cwt.ap.rearrange("p (c j) -> p c j", c=12)
    sel304 = sel30.ap.rearrange("p (m h n) -> p m h n", m=NT, h=NH)
    kmT3 = kmT.ap.rearrange("p (h n) -> p h n", h=NH)
    kmTb3 = kmTb.ap.rearrange("p (h n) -> p h n", h=NH)
    gsb3 = gsb.ap.rearrange("p (h n) -> p h n", h=NH)
    rrl[0] = [0, 1, 2, 3, 4, 5]
    pu, pw = PSB[6], PSB[7]
    c12 = lambda b_: b_.ap.rearrange("p (c t) -> p c t", c=12)

    for m in range(min(NT, _KNT)):
        tsl = slice(m * 128, (m + 1) * 128)
        xv = scr.ap[:, 0:D]
        S.dma('sp', xv, xp[tsl, :], writes=[scr])
        S.dma('sp', cst.ap, cspd[tsl, :], writes=[cst])
        act(xn.ap, xv, AF.Square, [scr], [xn, ss], accum_out=ss.ap)
        rsqrt(rstd.ap, ss.ap, 1.0 / D, [ss], [rstd])
        ts(xn.ap, xv, rstd.ap, None, ALU.mult, None, [scr, rstd], [xn])
        pb = bank()
        for c in range(KC):
            tr(pbf(pb)[:, c * 128:(c + 1) * 128], xn.ap[:, c * 128:(c + 1) * 128], ident_b.ap, [xn], [pb], inc=(c == KC - 1))
        tt(xnT3, pbf(pb).rearrange("p (c t) -> p c t", c=KC), bc(n1.ap.unsqueeze(2), [128, KC, 128]), ALU.mult, [pb, n1], [xnT])
        pagesum_chunks((NS * NCH + NT - 1) // NT)
        for g in range(3):
            pq = bank()
            for j in range(4):
                cc = g * 4 + j
                mm(pq.ap[:, j * 128:(j + 1) * 128], [(wi3[:, c, cc * 128:(cc + 1) * 128], xnT3[:, c, :]) for c in range(KC)], [wi, xnT], [pq])
            cp(raw3[:, g * 4:(g + 1) * 4, 3:131], pq.ap.rearrange("p (c t) -> p c t", c=4), [pq], [raw], e='act')
        pz = bank()
        mm(pz.ap, [(xnT3[:, c, :], wi3[:, c, ZOFF:ZOFF + 512]) for c in range(KC)], [wi, xnT], [pz])
        act(zs.ap, pz.ap, AF.Silu, [pz], [zs])
        for j, off in enumerate((QBOFF, KBOFF, VBOFF)):
            pj = bank()
            mm(pj.ap, [(xnT3[:, c, :], wi3[:, c, off:off + 512]) for c in range(KC)], [wi, xnT], [pj])
            cp(tb.ap[:, j * 512:(j + 1) * 512], pj.ap, [pj], [tb], e=('act' if j % 2 == 0 else 'dve'))
        pa = bank()
        mm(pa.ap[:, 0:8], [(xnT3[:, c, :], wi3[:, c, AOFF:AOFF + 8]) for c in range(KC)], [wi, xnT], [pa])
        cp(absb.ap, pa.ap[:, 0:8], [pa], [absb])
        if _KSUB <= 1:
            continue
        tt(c12(cacc), raw3[:, :, 3:131], bc(cwt3[:, :, 3:4], [128, 12, 128]), ALU.mult, [raw, cwt], [cacc])
        for j in range(3):
            tt(c12(scr), raw3[:, :, j:j + 128], bc(cwt3[:, :, j:j + 1], [128, 12, 128]), ALU.mult, [raw, cwt], [scr])
            tt(cacc.ap, cacc.ap, scr.ap, ALU.add, [cacc, scr], [cacc])
        cp(raw3[:, :, 0:3], raw3[:, :, 128:131], [raw], [raw])
        act(cacc.ap, cacc.ap, AF.Silu, [cacc], [cacc])
        if _KSUB <= 2:
            continue
        tb3 = tb.ap[:, 0:1024].rearrange("p (g d) -> p g d", g=8)
        cs_m = cst.ap.rearrange("p (s g f) -> p s g f", s=2, g=8)
        rt3 = scr.ap[:, 0:512].rearrange("p (k g f) -> p k g f", k=4, g=8)
        x1, x2 = tb3[:, :, 0:16], tb3[:, :, 16:32]
        tt(rt3[:, 0], x1, cs_m[:, 0], ALU.mult, [tb, cst], [scr])
        tt(rt3[:, 1], x2, cs_m[:, 1], ALU.mult, [tb, cst], [scr])
        tt(rt3[:, 2], x2, cs_m[:, 0], ALU.mult, [tb, cst], [scr])
        tt(rt3[:, 3], x1, cs_m[:, 1], ALU.mult, [tb, cst], [scr])
        tt(x1, rt3[:, 0], rt3[:, 1], ALU.subtract, [scr], [tb])
        tt(x2, rt3[:, 2], rt3[:, 3], ALU.add, [scr], [tb])
        S.dma('pool', kp[tsl, :], tb.ap[:, 512:1024], reads=[tb], is_output=True)
        S.dma('pool', vp[tsl, :], tb.ap[:, 1024:1536], reads=[tb], is_output=True)
        cp(qkb.ap, tb.ap[:, 0:1024], [tb], [qkb])
        pt_ = bank()
        for g in range(8):
            tr(pbf(pt_)[:, g * 128:(g + 1) * 128], qkb.ap[:, g * 128:(g + 1) * 128], ident_b.ap, [qkb], [pt_], inc=(g == 7))
        ptv = pbf(pt_).rearrange("p (g t) -> p g t", g=8)
        cp(qbT3[:, :, tsl], ptv[:, 0:4, :], [pt_], [qbT])
        cp(kbT3[:, :, tsl], ptv[:, 4:8, :], [pt_], [kbT], e='act')
        if _KSUB <= 3:
            continue
        nbk = m // 2
        pk = bank()
        for h in range(NH):
            mm(pk.ap[:, h:h + 1], [(qkb.ap[:, 512 + h * 128:512 + (h + 1) * 128], ones_b.ap[:, 0:1])], [qkb, ones_b], [pk])
        if m % 2 == 0:
            ts(kmT3[:, :, nbk], pk.ap[:, 0:4], 1.0 / 256, None, ALU.mult, None, [pk], [kmT])
        else:
            stt(kmT3[:, :, nbk], pk.ap[:, 0:4], 1.0 / 256, kmT3[:, :, nbk], ALU.mult, ALU.add, [pk, kmT], [kmT])
        if nbk >= 1:
            pg = bank()
            for h in range(NH):
                mm(pg.ap[:, h * 8:(h + 1) * 8], [(qbT3[:, h, tsl], kmTb3[:, h, :])], [qbT, kmTb], [pg])
            cp(gsb.ap, pg.ap[:, 0:32], [pg], [gsb])
            if nbk < 8:
                mset(gsb3[:, :, nbk:8], NEG, [gsb])
            for h in range(NH):
                S.op('dve', lambda e, h=h: e.max(out=top8.ap, in_=gsb3[:, h, :]), reads=[gsb], writes=[top8])
                ts(sel304[:, m, h, :], gsb3[:, h, :], top8.ap[:, 2:3], -NEG, ALU.is_ge, ALU.mult, [gsb, top8], [sel30])
        if m % 2 == 1:
            cp(kmTb.ap, kmT.ap, [kmT], [kmTb])
        if _KSUB <= 4:
            continue
        a_, b_ = absb.ap[:, 0:4], absb.ap[:, 4:8]
        g0 = gsm.ap[:, 0:4]
        g1 = gsm.ap[:, 4:8]
        gg = gsm.ap[:, 8:12]
        beta = gsm.ap[:, 12:16]
        gcl = gsm.ap[:, 16:24]
        eg = gsm.ap[:, 24:28]
        egl = gsm.ap[:, 28:32]
        eglast = gsm.ap[:, 32:36]
        tt(g0, a_, dtb.ap, ALU.add, [absb, dtb], [gsm])
        softplus(g0, g1, [gsm], 128)
        tt(gg, g0, negA.ap, ALU.mult, [gsm, negA], [gsm])
        act(beta, b_, AF.Sigmoid, [absb], [gsm])
        pgc = bank()
        mm(pgc.ap[:, 0:4], [(triU.ap, gg)], [triU, gsm], [pgc])
        mm(pgc.ap[:, 4:8], [(ones_f.ap, gg)], [ones_f, gsm], [pgc])
        cp(gcl, pgc.ap[:, 0:8], [pgc], [gsm])
        gcum, glast = gcl[:, 0:4], gcl[:, 4:8]
        act(eg, gcum, AF.Exp, [gsm], [gsm])
        tt(egl, glast, gcum, ALU.subtract, [gsm], [gsm])
        act(egl, egl, AF.Exp, [gsm], [gsm])
        act(eglast, glast, AF.Exp, [gsm], [gsm])
        cacc3 = c12(cacc)
        for g in range(3):
            pq = bank()
            for j in range(4):
                tr(pq.ap[:, j * 128:(j + 1) * 128], cacc3[:, g * 4 + j, :], ident_f.ap, [cacc], [pq], inc=(j == 3))
            cp(qkv.ap[:, g * 512:(g + 1) * 512], pq.ap, [pq], [qkv], e=('act' if g % 2 == 0 else 'dve'))
        if m == NT - 1:
            for j in range(3):
                pc = bank()
                mm(pc.ap[0:3, :], [(xnT3[:, c, 125:128], wi3[:, c, j * 512:(j + 1) * 512]) for c in range(KC)], [wi, xnT], [pc])
                cp(cacc.ap[0:3, j * 512:(j + 1) * 512], pc.ap[0:3, :], [pc], [cacc])
            S.dma('pool', cpo[:, :], cacc.ap[0:3, :], reads=[cacc], is_output=True)
        sqb = scr.ap[:, 0:1024]
        tt(sqb, qkv.ap[:, 0:1024], qkv.ap[:, 0:1024], ALU.mult, [qkv], [scr])
        red(ssqk.ap, sqb.rearrange("p (g d) -> p g d", g=8), ALU.add, [scr], [ssqk])
        rsqrt(ssqk.ap, ssqk.ap, 1.0, [ssqk], [ssqk])
        rq, rk = ssqk.ap[:, 0:4], ssqk.ap[:, 4:8]
        ct = ctab.ap.rearrange("p (v h) -> p v h", v=7)
        ts(ct[:, 0], rq, SCALE, None, ALU.mult, None, [ssqk], [ctab])
        tt(ct[:, 1], ct[:, 0], eg, ALU.mult, [ctab, gsm], [ctab])
        cp(ct[:, 2], rk, [ssqk], [ctab])
        tt(ct[:, 3], rk, beta, ALU.mult, [ssqk, gsm], [ctab])
        tt(ct[:, 4], ct[:, 3], eg, ALU.mult, [ctab, gsm], [ctab])
        tt(ct[:, 5], rk, egl, ALU.mult, [ssqk, gsm], [ctab])
        cp(ct[:, 6], beta, [gsm], [ctab])
        q3 = qkv.ap[:, 0:512].rearrange("p (h d) -> p h d", h=NH)
        k3 = qkv.ap[:, 512:1024].rearrange("p (h d) -> p h d", h=NH)
        v3 = qkv.ap[:, 1024:1536].rearrange("p (h d) -> p h d", h=NH)

        def cb(i):
            return bc(ct[:, i].unsqueeze(2), [128, NH, 128])
        tt(varb4[:, 0], q3, cb(0), ALU.mult, [qkv, ctab], [varb])
        tt(varb4[:, 1], k3, cb(2), ALU.mult, [qkv, ctab], [varb])
        tt(varb4[:, 2], k3, cb(3), ALU.mult, [qkv, ctab], [varb])
        tt(varf4[:, 0], q3, cb(1), ALU.mult, [qkv, ctab], [varf])
        tt(varf4[:, 1], k3, cb(4), ALU.mult, [qkv, ctab], [varf])
        tt(varf4[:, 2], k3, cb(5), ALU.mult, [qkv, ctab], [varf])
        tt(varf4[:, 3], v3, cb(6), ALU.mult, [qkv, ctab], [varf])
        for g in range(2):
            pq = bank()
            n8 = 8 if g == 0 else 4
            for j in range(n8):
                v_, h = (g * 8 + j) // 4, (g * 8 + j) % 4
                tr(pbf(pq)[:, j * 128:(j + 1) * 128], varb4[:, v_, h, :], ident_b.ap, [varb], [pq], inc=(j == n8 - 1))
            cp(varTb.ap[:, g * 1024:g * 1024 + n8 * 128], pbf(pq)[:, 0:n8 * 128], [pq], [varTb], e=('act' if g == 0 else 'dve'))
        pq = bank()
        for h in range(NH):
            tr(pq.ap[:, h * 128:(h + 1) * 128], varf4[:, 0, h, :], ident_f.ap, [varf], [pq], inc=(h == NH - 1))
        cp(qdTf.ap, pq.ap, [pq], [qdTf], e='act')
        qnT, knT, kbTv = varTb4[:, 0], varTb4[:, 1], varTb4[:, 2]
        kbg, kdv, vbv = varf4[:, 1], varf4[:, 2], varf4[:, 3]
        if _KSUB <= 6:
            continue
        def chain(h, sl):
            dg, decT, decS, NTf, Mm, MT, Qb = sl['dg'], sl['decT'], sl['decS'], sl['NTf'], sl['Mm'], sl['MT'], sl['Qb']
            pj = sl['banks'][0]
            pns = sl['banks'][1:3]
            ts(dg.ap, ident_f.ap, gcum[:, h:h + 1], None, ALU.mult, None, [ident_f, gsm], [dg])
            mm(pj.ap[:, 0:128], [(ones_f.ap, dg.ap)], [ones_f, dg], [pj])
            yield
            stt(dg.ap, pj.ap[:, 0:128], gcum[:, h:h + 1], maskU.ap, ALU.subtract, ALU.min, [pj, gsm, maskU], [dg])
            act(decT.ap, dg.ap, AF.Exp, [dg], [decT])
            tt(decS.ap, decT.ap, strictU.ap, ALU.mult, [decT, strictU], [decS])
            mm(pj.ap[:, 128:256], [(knT[:, h, :], qnT[:, h, :])], [varTb], [pj])
            mm(pj.ap[:, 256:384], [(knT[:, h, :], kbTv[:, h, :])], [varTb], [pj])
            yield
            tt(qkT3[:, h, :], pj.ap[:, 128:256], decT.ap, ALU.mult, [pj, decT], [qkT])
            stt(NTf.ap, pj.ap[:, 256:384], -1.0, decS.ap, ALU.mult, ALU.mult, [pj, decS], [NTf])
            cp(MT[0].ap, NTf.ap, [NTf], [MT[0]])
            pn = pns[0]
            tr(pn.ap[:, 0:128], NTf.ap, ident_f.ap, [NTf], [pn])
            yield
            cp(Mm[0].ap, pn.ap[:, 0:128], [pn], [Mm[0]], e='act')
            tt(NTf.ap, NTf.ap, ident_f.ap, ALU.add, [NTf, ident_f], [NTf])
            cp(Qb[0].ap, NTf.ap, [NTf], [Qb[0]])
            yield
            cur = 0
            for lv in range(6):
                nx = 1 - cur
                pn = pns[(lv + 1) % 2]
                mm(pn.ap[:, 0:128], [(MT[cur].ap, Mm[cur].ap)], [MT[cur], Mm[cur]], [pn])
                mm(pn.ap[:, 128:256], [(Mm[cur].ap, MT[cur].ap)], [MT[cur], Mm[cur]], [pn])
                yield
                cp(Mm[nx].ap, pn.ap[:, 0:128], [pn], [Mm[nx]], e='act')
                if lv < 5:
                    cp(MT[nx].ap, pn.ap[:, 128:256], [pn], [MT[nx]])
                yield
                mm(pn.ap[:, 256:384], [(Mm[nx].ap, Qb[cur].ap)], [Mm[nx], Qb[cur]], [pn])
                yield
                tt(NTf.ap, NTf.ap, pn.ap[:, 256:384], ALU.add, [NTf, pn], [NTf])
                if lv < 5:
                    cp(Qb[nx].ap, NTf.ap, [NTf], [Qb[nx]])
                yield
                cur = nx
            mm(pu.ap[:, h * 128:(h + 1) * 128], [(NTf.ap, vbv[:, h, :])], [NTf, varf], [pu])
            mm(pw.ap[:, h * 128:(h + 1) * 128], [(kbg[:, h, :], NTf.ap)], [NTf, varf], [pw])

        for hp in ((0, 1), (2, 3)):
            gens = [chain(hp[0], hsl[0]), chain(hp[1], hsl[1])]
            alive = [True, True]
            while any(alive):
                for gi_ in range(2):
                    if alive[gi_]:
                        try:
                            next(gens[gi_])
                        except StopIteration:
                            alive[gi_] = False
        cp(uu.ap, pu.ap, [pu], [uu], e='act')
        cp(wT.ap, pw.ap, [pw], [wT])
        if _KSUB <= 7:
            continue
        pv = bank()
        for h in range(NH):
            mm(pv.ap[:, h * 128:(h + 1) * 128], [(wT3[:, h, :], Sst.ap[:, h * 128:(h + 1) * 128])], [wT, Sst], [pv])
        tt(vnew.ap, uu.ap, pv.ap, ALU.subtract, [uu, pv], [vnew])
        po = bank()
        for h in range(NH):
            mm(po.ap[:, h * 128:(h + 1) * 128], [(qdT3[:, h, :], Sst.ap[:, h * 128:(h + 1) * 128]),
                                                 (qkT3[:, h, :], vnew.ap[:, h * 128:(h + 1) * 128])], [qdTf, Sst, qkT, vnew], [po])
        psn = bank()
        for h in range(NH):
            mm(psn.ap[:, h * 128:(h + 1) * 128], [(kdv[:, h, :], vnew.ap[:, h * 128:(h + 1) * 128])], [varf, vnew], [psn])
        Sst3 = Sst.ap.rearrange("p (h d) -> p h d", h=NH)
        tt(Sst3, Sst3, bc(eglast.unsqueeze(2), [128, NH, 128]), ALU.mult, [Sst, gsm], [Sst])
        tt(Sst.ap, Sst.ap, psn.ap, ALU.add, [Sst, psn], [Sst])
        if _KSUB <= 8:
            continue
        osb = scr.ap[:, 0:512]
        osq = scr.ap[:, 512:1024]
        oab = qkb.ap[:, 0:512]
        cp(osb, po.ap, [po], [scr], e='act')
        tt(osq, osb, osb, ALU.mult, [scr], [scr])
        red(oss.ap, osq.rearrange("p (h d) -> p h d", h=NH), ALU.add, [scr], [oss])
        rsqrt(oss.ap, oss.ap, 1.0 / HD, [oss], [oss])
        osb3 = osb.rearrange("p (h d) -> p h d", h=NH)
        tt(osb3, osb3, bc(oss.ap.unsqueeze(2), [128, NH, 128]), ALU.mult, [scr, oss], [scr])
        tt(osb3, osb3, bc(gnb.ap.unsqueeze(1), [128, NH, 128]), ALU.mult, [scr, gnb], [scr])
        tt(oab, osb, zs.ap, ALU.mult, [scr, zs], [qkb])
        pq = bank()
        for h in range(NH):
            tr(pbf(pq)[:, h * 128:(h + 1) * 128], oab[:, h * 128:(h + 1) * 128], ident_b.ap, [qkb], [pq], inc=(h == NH - 1))
        cp(oaT3[:, :, tsl], pbf(pq)[:, 0:512].rearrange("p (h t) -> p h t", h=NH), [pq], [oaT])
    S.dma('pool', gp.rearrange("h k v -> k h v"), Sst.ap.rearrange("p (h v) -> p h v", h=NH), reads=[Sst], is_output=True)

    pagesum_chunks(NS * NCH)
    S.barrier()
    rrl[0] = list(range(8))
    A.top = O_WORK
    P4 = NS
    prq = A.f32(1536, parts=P4, name="prq")
    cats = A.f32(D, parts=P4, name="cats")
    catb = A.bf(D, parts=P4, name="catb")
    S.dma('sp', prq.ap, sqd[:, :], writes=[prq])
    S.dma('sp', cats.ap[:, 0:512], soad[:, :], writes=[cats])
    pti = A.i32(NPG, parts=16, name="pti")
    ptf = A.f32(NPG, parts=16, name="ptf")
    qbc = A.f32(NS * 512, name="qbc")
    pr = A.f32(NH * 7 * 128, name="pr")
    pgs = A.f32(16, name="pgs")
    gT = A.f32(NBX, parts=16, name="gT")
    t8 = A.f32(8, parts=16, name="t8")
    i8 = A.u32(8, parts=16, name="i8")
    i8f = A.f32(8, parts=16, name="i8f")
    eqb = A.f32(NBX, parts=16, name="eqb")
    eq2 = A.f32(NBX, parts=16, name="eq2")
    physf = A.f32(6, parts=16, name="physf")
    R16 = A.f32(96, parts=16, name="R16")
    idxf = A.f32(96, name="idxf")
    idxi = A.i32(96, name="idxi")
    Kg = A.f32(NH * 7 * 128, name="Kg")
    Vg = A.f32(NH * 7 * 128, name="Vg")
    Kg4 = Kg.ap.rearrange("p (h g d) -> p h g d", h=NH, g=7)
    Vg4 = Vg.ap.rearrange("p (h g d) -> p h g d", h=NH, g=7)
    pr4 = pr.ap.rearrange("p (h g d) -> p h g d", h=NH, g=7)
    sl = A.f32(28, name="sl")
    sl3 = sl.ap.rearrange("p (h g) -> p h g", h=NH)
    pmx = A.f32(4, name="pmx")
    m4 = A.f32(1, parts=4, name="m4")
    Rm = A.f32(4, parts=4, name="Rm")
    ngm = A.f32(4, name="ngm")
    Ps = A.f32(28, name="Ps")
    Ps3 = Ps.ap.rearrange("p (h g) -> p h g", h=NH)
    PZ = A.f32(NS * 28 * 4, name="PZ")
    PZ5 = PZ.ap.rearrange("p (b h g c) -> p b h g c", b=NS, h=NH, g=7)
    psr = A.f32(4, name="psr")
    obacc = A.f32(512, parts=P4, name="obacc")
    denacc = A.f32(4, parts=P4, name="denacc")

    assert A.top <= 36100, A.top
    for b in range(NS):
        S.dma('sp', pti.ap[b * NH:(b + 1) * NH, :], pt[b:b + 1, :].partition_broadcast(NH), writes=[pti])
    cp(ptf.ap, pti.ap, [pti], [ptf])
    ckr = ck.rearrange("n r h d -> (n r h) d")
    cvr = cv.rearrange("n r h d -> (n r h) d")
    for b in range(NS):
        ts(pr.ap[0:P4, b * 512:(b + 1) * 512], prq.ap[:, 0:512], ident_f.ap[0:P4, b:b + 1], None, ALU.mult, None, [prq, ident_f], [pr])
        pq = bank()
        mm(pq.ap, [(ones_f.ap[0:P4, :], pr.ap[0:P4, b * 512:(b + 1) * 512])], [ones_f, pr], [pq])
        cp(qbc.ap[:, b * 512:(b + 1) * 512], pq.ap, [pq], [qbc], e='act')
    tt(pr.ap[0:NPG, 0:2048], acc.ap[0:NPG, :], qbc.ap[0:NPG, :], ALU.mult, [acc, qbc], [pr])
    red(pgs.ap[0:NPG, :], pr.ap[0:NPG, 0:2048].rearrange("p (g d) -> p g d", g=16), ALU.add, [pr], [pgs])
    pq = bank()
    mm(pq.ap[0:16, 0:NB], [(pgs.ap[0:NPG, :], pairsel.ap[0:NPG, 0:NB])], [pgs, pairsel], [pq])
    cp(gT.ap[:, 0:NB], pq.ap[0:16, 0:NB], [pq], [gT])
    S.op('dve', lambda e: e.max(out=t8.ap, in_=gT.ap[:, 0:NB]), reads=[gT], writes=[t8])
    S.op('dve', lambda e: e.max_index(out=i8.ap, in_max=t8.ap, in_values=gT.ap[:, 0:NB]), reads=[gT, t8], writes=[i8])
    cp(i8f.ap, i8.ap, [i8], [i8f])
    ptf3 = ptf.ap.rearrange("p (n j) -> p n j", j=2)
    for k in range(3):
        ts(eqb.ap[:, 0:NB], iotaNB.ap[:, 0:NB], i8f.ap[:, k:k + 1], None, ALU.is_equal, None, [iotaNB, i8f], [eqb])
        for j in range(2):
            tt(eq2.ap[:, 0:NB], eqb.ap[:, 0:NB], ptf3[:, :, j], ALU.mult, [eqb, ptf], [eq2])
            red(physf.ap[:, k * 2 + j:k * 2 + j + 1], eq2.ap[:, 0:NB], ALU.add, [eq2], [physf])
    R163 = R16.ap.rearrange("p (g c) -> p g c", g=16)
    for g in range(16):
        ts(R163[:, g, :], physf.ap, ident_f.ap[0:16, g:g + 1], None, ALU.mult, None, [physf, ident_f], [R16])
    pq = bank()
    mm(pq.ap[:, 0:96], [(ones_f.ap[0:16, :], R16.ap)], [ones_f, R16], [pq])
    stt(idxf.ap, pq.ap[:, 0:96], 512.0, rowoff.ap, ALU.mult, ALU.add, [pq, rowoff], [idxf])
    ts(idxf.ap, idxf.ap, 0.0, float(NPHYS * 512 - 1), ALU.max, ALU.min, [idxf], [idxf])
    cp(idxi.ap, idxf.ap, [idxf], [idxi])
    mset(PZ.ap, 0.0, [PZ])
    for b in range(NS):
        mset(Kg4[:, :, 6, :], 0.0, [Kg])
        mset(Vg4[:, :, 6, :], 0.0, [Vg])
        S.dma('pool', Kg4[0:1, :, 6, :], prq.ap[b:b + 1, 512:1024].rearrange("p (h d) -> p h d", h=NH), reads=[prq], writes=[Kg])
        S.dma('pool', Vg4[0:1, :, 6, :], prq.ap[b:b + 1, 1024:1536].rearrange("p (h d) -> p h d", h=NH), reads=[prq], writes=[Vg])
        for h in range(NH):
            for kj in range(6):
                col = (b * NH + h) * 6 + kj
                S.dma('pool', Kg4[:, h, kj, :], ckr[:, :], writes=[Kg], reads=[idxi],
                      indirect=bass.IndirectOffsetOnAxis(ap=idxi.ap[:, col:col + 1].bitcast(U32), axis=0))
                S.dma('pool', Vg4[:, h, kj, :], cvr[:, :], writes=[Vg], reads=[idxi],
                      indirect=bass.IndirectOffsetOnAxis(ap=idxi.ap[:, col:col + 1].bitcast(U32), axis=0))
        qb_b = qbc.ap[:, b * 512:(b + 1) * 512].rearrange("p (h d) -> p h d", h=NH)
        for h in range(NH):
            tt(pr4[:, h], Kg4[:, h], bc(qb_b[:, h, :].unsqueeze(1), [128, 7, 128]), ALU.mult, [Kg, qbc], [pr])
        red(sl.ap, pr.ap.rearrange("p (g d) -> p g d", g=28), ALU.add, [pr], [sl])
        ts(sl3[:, :, 6], sl3[:, :, 6], negrow.ap, None, ALU.add, None, [sl, negrow], [sl])
        red(pmx.ap, sl3, ALU.max, [sl], [pmx])
        pq = bank()
        tr(pq.ap[0:4, 0:128], pmx.ap, ident_f.ap, [pmx], [pq])
        red(m4.ap, pq.ap[0:4, 0:128], ALU.max, [pq], [m4])
        ts(Rm.ap, ident_f.ap[0:4, 0:4], m4.ap, None, ALU.mult, None, [ident_f, m4], [Rm])
        pq = bank()
        mm(pq.ap[:, 0:4], [(ones_f.ap[0:4, :], Rm.ap)], [ones_f, Rm], [pq])
        ts(ngm.ap, pq.ap[:, 0:4], -SCALE, None, ALU.mult, None, [pq], [ngm])
        for h in range(NH):
            act(Ps3[:, h, :], sl3[:, h, :], AF.Exp, [sl, ngm], [Ps], scale=SCALE, bias=ngm.ap[:, h:h + 1])
        cp(PZ5[:, b, :, :, b], Ps3, [Ps], [PZ])
        red(psr.ap, Ps3, ALU.add, [Ps], [psr])
        pa = bank()
        for h in range(NH):
            mm(pa.ap[0:P4, h * 128:(h + 1) * 128], [(PZ5[:, b, h, g, :], Vg4[:, h, g, :]) for g in range(7)], [PZ, Vg], [pa])
        pd = bank()
        mm(pd.ap[0:P4, 0:4], [(oh[:, b, :], psr.ap)], [onehot4, psr], [pd])
        if b == 0:
            cp(obacc.ap, pa.ap[0:P4, :], [pa], [obacc])
            cp(denacc.ap, pd.ap[0:P4, 0:4], [pd], [denacc])
        else:
            tt(obacc.ap, obacc.ap, pa.ap[0:P4, :], ALU.add, [obacc, pa], [obacc])
            tt(denacc.ap, denacc.ap, pd.ap[0:P4, 0:4], ALU.add, [denacc, pd], [denacc])
    recip(denacc.ap, [denacc])
    tt(cats.ap[:, 512:1024].rearrange("p (h d) -> p h d", h=NH), obacc.ap.rearrange("p (h d) -> p h d", h=NH),
       bc(denacc.ap.unsqueeze(2), [P4, NH, 128]), ALU.mult, [obacc, denacc], [cats])
    cp(catb.ap, cats.ap, [cats], [catb])
    pb = bank()
    for c in range(KC):
        tr(pbf(pb)[:, c * NS:(c + 1) * NS], catb.ap[:, c * 128:(c + 1) * 128], ident_b.ap[0:P4, 0:P4], [catb], [pb], inc=(c == KC - 1))
    cp(catTs.ap, pbf(pb)[:, 0:KC * NS], [pb], [catTs])
    if _STOP == "A":
        S.finish()
        return nc
    S.barrier()
    rrl[0] = list(range(8))
    O_WO, O_WG, O_WU, O_VBT, O_BW, O_CW, O_WD = 2500, 6600, 17900, 29400, 17900, 29400, 36200
    A.top = O_VBT
    vbt = A.bf(NT * 512, name="vbt")
    vbt3 = vbt.ap.rearrange("p (m c) -> p m c", m=NT)
    assert A.top <= O_P2, A.top
    A.top = O_BW
    NSLOT = 2
    slots = []
    for si in range(NSLOT):
        d_ = {}
        d_['Pm'] = A.bf(T, name="Pm%d" % si)
        d_['PT'] = A.bf(NT * 128, name="PT%d" % si)
        d_['mx'] = A.f32(8, name="mx%d" % si)
        d_['negm'] = A.f32(1, name="negm%d" % si)
        d_['pbias'] = A.f32(8, name="pbias%d" % si)
        d_['rs'] = A.f32(12, name="rs%d" % si)
        d_['rinv'] = A.f32(1, name="rinv%d" % si)
        d_['dsb'] = A.f32(128, name="dsb%d" % si)
        d_['obb'] = A.bf(128, name="obb%d" % si)
        d_['banks'] = [PSB[4 * si + j] for j in range(4)]
        d_['rr'] = 0
        slots.append(d_)
    assert A.top <= O_VBT, A.top
    for m in range(NT):
        S.dma('pool', vbt3[:, m, :], vp[m * 128:(m + 1) * 128, :], writes=[vbt])
    A.top = O_WO
    wo = A.bf(KC * D, name="wo")
    A.top = O_WG
    wg = A.bf(KC * DFF, name="wg")
    assert A.top <= O_WU, A.top
    wo3 = wo.ap.rearrange("p (c n) -> p c n", c=KC)
    wg3 = wg.ap.rearrange("p (c n) -> p c n", c=KC)
    for c in range(KC):
        S.dma('pool', wo3[:, c, :], wod[c * 128:(c + 1) * 128, :], writes=[wo])
    for c in range(KC):
        for (a0, a1) in ((0, 1408), (1408, DFF)):
            S.dma('pool', wg3[:, c, a0:a1], wgd[c * 128:(c + 1) * 128, a0:a1], writes=[wg])

    def unitB(h, i, sl):
        Pm, PT_, mx, negm, pbias, rs, rinv, dsb, obb = (sl['Pm'], sl['PT'], sl['mx'], sl['negm'], sl['pbias'], sl['rs'],
                                                        sl['rinv'], sl['dsb'], sl['obb'])
        PT3 = PT_.ap.rearrange("p (j q) -> p j q", j=NT)

        def sbank():
            b_ = sl['banks'][sl['rr'] % 4]
            sl['rr'] += 1
            return b_
        tsl = slice(i * 128, (i + 1) * 128)
        nk = i + 1
        nbk = i // 2
        ng = (nk + 3) // 4
        lb = []
        for g in range(ng):
            w = min(512, nk * 128 - g * 512)
            pl = sbank()
            mm(pl.ap[:, 0:w], [(qbT3[:, h, tsl], kbT3[:, h, g * 512:g * 512 + w])], [qbT, kbT], [pl])
            lb.append((pl, w))
        yield
        for g, (pl, w) in enumerate(lb):
            red(mx.ap[:, g:g + 1], pl.ap[:, 0:w], ALU.max, [pl], [mx])
        red(negm.ap, mx.ap[:, 0:ng], ALU.max, [mx], [negm])
        ts(negm.ap, negm.ap, -SCALE, None, ALU.mult, None, [negm], [negm])
        if nbk > 0:
            ts(pbias.ap[:, 0:nbk], sel304[:, i, h, 0:nbk], negm.ap, NEG, ALU.add, ALU.add, [sel30, negm], [pbias])
        pl = lb[i // 4][0]
        c0 = (i % 4) * 128
        tt(dsb.ap, pl.ap[:, c0:c0 + 128], cmask.ap, ALU.add, [pl, cmask], [dsb])
        yield
        ncol = 0
        for n in range(nbk):
            pl = lb[n // 2][0]
            c0 = (n % 2) * 256
            act(Pm.ap[:, n * 256:(n + 1) * 256], pl.ap[:, c0:c0 + 256], AF.Exp, [pl, pbias], [Pm, rs],
                scale=SCALE, bias=pbias.ap[:, n:n + 1], accum_out=rs.ap[:, ncol:ncol + 1])
            ncol += 1
        if i % 2 == 1:
            j = i - 1
            pl = lb[j // 4][0]
            c0 = (j % 4) * 128
            act(Pm.ap[:, j * 128:(j + 1) * 128], pl.ap[:, c0:c0 + 128], AF.Exp, [pl, negm], [Pm, rs],
                scale=SCALE, bias=negm.ap, accum_out=rs.ap[:, ncol:ncol + 1])
            ncol += 1
        act(Pm.ap[:, i * 128:(i + 1) * 128], dsb.ap, AF.Exp, [dsb, negm], [Pm, rs],
            scale=SCALE, bias=negm.ap, accum_out=rs.ap[:, ncol:ncol + 1])
        ncol += 1
        yield
        red(rinv.ap, rs.ap[:, 0:ncol], ALU.add, [rs], [rinv])
        recip(rinv.ap, [rinv])
        for g in range((nk + 7) // 8):
            n8 = min(8, nk - g * 8)
            pq = sbank()
            for j in range(n8):
                jj = g * 8 + j
                tr(pbf(pq)[:, j * 128:(j + 1) * 128], Pm.ap[:, jj * 128:(jj + 1) * 128], ident_b.ap, [Pm], [pq], inc=(j == n8 - 1))
            cp(PT_.ap[:, g * 1024:g * 1024 + n8 * 128], pbf(pq)[:, 0:n8 * 128], [pq], [PT_], e=('act' if g == 0 else 'dve'))
            yield
        po = sbank()
        mm(po.ap[:, 0:128], [(PT3[:, j, :], vbt3[:, j, h * 128:(h + 1) * 128]) for j in range(nk)], [PT_, vbt], [po])
        yield
        ts(obb.ap, po.ap[:, 0:128], rinv.ap, None, ALU.mult, None, [po, rinv], [obb])
        pq = sbank()
        tr(pbf(pq)[:, 0:128], obb.ap, ident_b.ap, [obb], [pq])
        yield
        cp(obT3[:, h, tsl], pbf(pq)[:, 0:128], [pq], [obT], e='act')

    units = [(h, i) for i in range(NT) for h in range(NH)]
    order = []
    lo, hi = 0, len(units) - 1
    while lo <= hi:
        order.append(units[hi])
        if lo != hi:
            order.append(units[lo])
        lo += 1
        hi -= 1
    pend = list(order)
    live = [None] * NSLOT
    while pend or any(g_ is not None for g_ in live):
        for si in range(NSLOT):
            if live[si] is None and pend:
                h_, i_ = pend.pop(0)
                live[si] = unitB(h_, i_, slots[si])
            if live[si] is not None:
                try:
                    next(live[si])
                except StopIteration:
                    live[si] = None

    if _STOP == "B":
        S.finish()
        return nc
    S.barrier()
    rrl[0] = list(range(8))
    A.top = O_WU
    wu = A.bf(KC * DFF, name="wu")
    assert A.top <= O_CW, A.top
    wu3 = wu.ap.rearrange("p (c n) -> p c n", c=KC)
    for c in range(KC):
        for (a0, a1) in ((0, 1408), (1408, DFF)):
            S.dma('pool', wu3[:, c, a0:a1], wud[c * 128:(c + 1) * 128, a0:a1], writes=[wu])
    A.top = O_CW
    xr = [A.f32(D, name="xr0"), A.f32(D, name="xr1")]
    x2b = A.f32(D, name="x2b")
    hn = A.bf(D, name="hn")
    hT = A.bf(D, name="hT")
    sgt = A.bf(512, name="sgt")
    actT = A.bf(FC * 128, name="actT")
    yb = A.f32(D, name="yb")
    st = A.f32(4, name="st")
    assert A.top <= O_WD, A.top
    catTs3 = catTs.ap.rearrange("p (c b) -> p c b", c=KC)

    def tailA(x_, nt, catfn, dst):
        for n in range(2):
            pj = bank()
            mm(pj.ap[0:nt, :], [(catfn(c)[0], wo3[:, c, n * 512:(n + 1) * 512]) for c in range(KC)], [wo] + catfn(0)[1], [pj])
            tt(dst.ap[0:nt, n * 512:(n + 1) * 512], x_.ap[0:nt, n * 512:(n + 1) * 512], pj.ap[0:nt, :], ALU.add, [x_, pj], [dst])

    def tailB(x2_, nt, ydst):
        act(hn.ap[0:nt, :], x2_.ap[0:nt, :], AF.Square, [x2_], [hn, st], accum_out=st.ap[0:nt, 0:1])
        rsqrt(st.ap[0:nt, 0:1], st.ap[0:nt, 0:1], 1.0 / D, [st], [st])
        ts(hn.ap[0:nt, :], x2_.ap[0:nt, :], st.ap[0:nt, 0:1], None, ALU.mult, None, [x2_, st], [hn])
        pb_ = bank()
        for c in range(KC):
            tr(pbf(pb_)[:, c * nt:(c + 1) * nt], hn.ap[0:nt, c * 128:(c + 1) * 128], ident_b.ap[0:nt, 0:nt], [hn], [pb_], inc=(c == KC - 1))
        hTv = hT.ap[:, 0:KC * nt].rearrange("p (c t) -> p c t", c=KC)
        tt(hTv, pbf(pb_)[:, 0:KC * nt].rearrange("p (c t) -> p c t", c=KC), bc(n2.ap.unsqueeze(2), [128, KC, nt]), ALU.mult, [pb_, n2], [hT])
        aTv = actT.ap[:, 0:FC * nt].rearrange("p (f t) -> p f t", f=FC)
        for f0 in range(0, FC, 4):
            nf = min(4, FC - f0)
            pg_ = bank()
            pu_ = bank()
            for j in range(nf):
                f = f0 + j
                mm(pg_.ap[:, j * nt:(j + 1) * nt], [(wg3[:, c, f * 128:(f + 1) * 128], hTv[:, c, :]) for c in range(KC)], [wg, hT], [pg_])
                mm(pu_.ap[:, j * nt:(j + 1) * nt], [(wu3[:, c, f * 128:(f + 1) * 128], hTv[:, c, :]) for c in range(KC)], [wu, hT], [pu_])
            act(sgt.ap[:, 0:nf * nt], pg_.ap[:, 0:nf * nt], AF.Silu, [pg_], [sgt])
            tt(actT.ap[:, f0 * nt:(f0 + nf) * nt], sgt.ap[:, 0:nf * nt], pu_.ap[:, 0:nf * nt], ALU.mult, [sgt, pu_], [actT])
        for n in range(2):
            pj = bank()
            mm(pj.ap[0:nt, :], [(aTv[:, f, :], wd3[:, f, n * 512:(n + 1) * 512]) for f in range(FC)], [wd, actT], [pj])
            tt(x2_.ap[0:nt, n * 512:(n + 1) * 512], x2_.ap[0:nt, n * 512:(n + 1) * 512], pj.ap[0:nt, :], ALU.add, [x2_, pj], [x2_])
        act(hn.ap[0:nt, :], x2_.ap[0:nt, :], AF.Square, [x2_], [hn, st], accum_out=st.ap[0:nt, 1:2])
        rsqrt(st.ap[0:nt, 1:2], st.ap[0:nt, 1:2], 1.0 / D, [st], [st])
        stt(yb.ap[0:nt, :], x2_.ap[0:nt, :], st.ap[0:nt, 1:2], fnb.ap[0:nt, :], ALU.mult, ALU.mult, [x2_, st, fnb], [yb])
        S.dma('sp', ydst, yb.ap[0:nt, :], reads=[yb], is_output=True)

    for m in range(NT):
        tsl = slice(m * 128, (m + 1) * 128)
        xb = xr[m % 2]
        S.dma('sp', xb.ap, xp[tsl, :], writes=[xb])

        def catfn(c, tsl=tsl):
            if c < 4:
                return (oaT3[:, c, tsl], [oaT])
            return (obT3[:, c - 4, tsl], [obT])
        tailA(xb, 128, catfn, x2b)
        S.dma('sp', x2s[tsl, :], x2b.ap, reads=[x2b])
    S.barrier()
    A.top = O_WD
    wd = A.bf(FC * D, name="wd")
    xsb2 = A.f32(D, parts=NS, name="xsb2")
    x2sm = A.f32(D, parts=NS, name="x2sm")
    assert A.top <= 53200
    wd3 = wd.ap.rearrange("p (c n) -> p c n", c=FC)
    for c in range(FC):
        S.dma('pool', wd3[:, c, :], wdd[c * 128:(c + 1) * 128, :], writes=[wd])
    for m in range(NT):
        tsl = slice(m * 128, (m + 1) * 128)
        xb = xr[m % 2]
        S.dma('sp', xb.ap, x2s[tsl, :], writes=[xb])
        tailB(xb, 128, yp[tsl, :])
    S.dma('sp', xsb2.ap, xs[:, :], writes=[xsb2])
    tailA(xsb2, NS, lambda c: (catTs3[:, c, :], [catTs]), x2sm)
    tailB(x2sm, NS, ys[:, :])
    S.finish()
    return nc


_CACHE = {}


def _rot_tables(pos):
    half = 16
    inv = (np.float32(500000.0) ** (-(np.arange(half, dtype=np.float32)) / np.float32(half))).astype(np.float32)
    ang = pos.astype(np.float32)[:, None] * inv[None, :]
    cos = np.cos(ang).astype(np.float32)
    sin = np.sin(ang).astype(np.float32)
    tab = np.stack([np.broadcast_to(cos[:, None, :], (len(pos), 8, half)),
                    np.broadcast_to(sin[:, None, :], (len(pos), 8, half))], axis=1)
    return np.ascontiguousarray(tab.reshape(len(pos), 256), dtype=np.float32)


def kernel(x_prompt, x_sample, cache_k, cache_v, page_table, state_gdn, state_conv, norm1_w, w_in,
           conv_w, a_log, dt_bias, gdn_norm_w, w_out, norm2_w, w_gate, w_up, w_down, final_norm_w):
    f = lambda a: np.ascontiguousarray(np.asarray(a), dtype=np.float32)
    NPG = page_table.shape[1]
    PAST = NPG * 128
    if PAST not in _CACHE:
        _CACHE[PAST] = build(PAST)
    nc = _CACHE[PAST]
    ck = f(cache_k[0])
    cv = f(cache_v[0])
    shared = {
        "ck": ck, "cv": cv,
        "n1w": f(np.asarray(norm1_w[0]).reshape(KC, 128).T),
        "n2w": f(np.asarray(norm2_w[0]).reshape(KC, 128).T),
        "fnw": f(np.broadcast_to(np.asarray(final_norm_w)[None, :], (128, D))),
        "w_in": f(w_in[0]),
        "cw": f(np.asarray(conv_w[0]).T.reshape(12, 128, 4).transpose(1, 0, 2)),
        "cw4": f(np.broadcast_to(np.asarray(conv_w[0])[None], (NS, 4, GQ))),
        "alog": f(np.broadcast_to(np.asarray(a_log[0])[None, :], (128, NH))),
        "dtb": f(np.broadcast_to(np.asarray(dt_bias[0])[None, :], (128, NH))),
        "gnw": f(np.broadcast_to(np.asarray(gdn_norm_w[0])[None, :], (128, HD))),
        "wo": f(w_out[0]), "wg": f(w_gate[0]), "wu": f(w_up[0]), "wd": f(w_down[0]),
        "csp": _rot_tables(np.arange(T)),
        "css": _rot_tables(np.full((NS,), PAST)),
    }
    in_maps = []
    for c in range(NCORES):
        m = dict(shared)
        m["xp"] = f(x_prompt[c])
        m["xs"] = f(x_sample[NS * c:NS * (c + 1), 0, :])
        m["pt"] = np.ascontiguousarray(np.asarray(page_table[NS * c:NS * (c + 1)]), dtype=np.int32)
        m["sg"] = f(state_gdn[0, NS * c:NS * (c + 1)])
        m["sc"] = f(state_conv[0, NS * c:NS * (c + 1)])
        in_maps.append(m)
    res = run_bass_kernel_spmd(nc, in_maps, core_ids=list(range(NCORES)))
    R = res.results
    cat = lambda k: np.stack([np.asarray(R[c][k]) for c in range(NCORES)], axis=0)
    y_prompt = cat("yp")
    y_sample = cat("ys").reshape(NS * NCORES, 1, D)
    k_prompt = cat("kp").reshape(1, NCORES, T, NH, HD)
    v_prompt = cat("vp").reshape(1, NCORES, T, NH, HD)
    k_sample = cat("ks").reshape(1, NS * NCORES, 1, NH, HD)
    v_sample = cat("vs").reshape(1, NS * NCORES, 1, NH, HD)
    gdn_prompt = cat("gp").reshape(1, NCORES, NH, HD, HD)
    gdn_sample = cat("gs").reshape(1, NS * NCORES, NH, HD, HD)
    conv_prompt = cat("cp").reshape(1, NCORES, 3, GQ)
    conv_sample = cat("cs").reshape(1, NS * NCORES, 3, GQ)
    return (y_prompt, y_sample, k_prompt, v_prompt, k_sample, v_sample, gdn_prompt, gdn_sample, conv_prompt, conv_sample)
```

```python
import math
import os
import numpy as np
import concourse.bass as bass
import concourse.mybir as mybir
from concourse.bass_utils import run_bass_kernel_spmd

F32 = mybir.dt.float32
BF16 = mybir.dt.bfloat16
I32 = mybir.dt.int32
U32 = mybir.dt.uint32
AF = mybir.ActivationFunctionType
ALU = mybir.AluOpType
AX = mybir.AxisListType

T = 2048
NT = 16
D = 1024
KC = 8
HD = 128
NH = 4
GQ = 1536
INC = 3592
DFF = 2816
FC = 22
ZOFF, AOFF, BOFF, QBOFF, KBOFF, VBOFF = 1536, 2048, 2052, 2056, 2568, 3080
EPS = 1e-6
NEG = -30000.0
SCALE = HD ** -0.5
NS = 4
NCORES = 8
_STOP = os.environ.get("KSTOP", "")
_KNT = int(os.environ.get("KNT", "16"))
_NF32 = int(os.environ.get("KNF32", "1"))
_KSUB = int(os.environ.get("KSUB", "99"))


class Tile:
    def __init__(self, name):
        self.name = name
        self.w = None
        self.r = []
        self.dsem = None
        self.dcnt = 0
        self.excl = False


class Buf:
    def __init__(self, ap, name):
        self.ap = ap
        self.t = Tile(name)


class Sched:
    def __init__(self, nc):
        self.nc = nc
        self.eng = {'pe': nc.tensor, 'act': nc.scalar, 'dve': nc.vector, 'pool': nc.gpsimd, 'sp': nc.sync}
        self.sem = {}
        self.cnt = {}
        for k in ['pe', 'act', 'dve', 'pool']:
            self.sem[k] = nc.alloc_semaphore("sem_" + k)
            self.cnt[k] = 0
        self.waited = {k: {} for k in self.eng}
        self.out_stamps = []
        self.dtiles = []
        self.free_dsems = []
        self.ninst = 0

    def _wait(self, e, deps):
        best = {}
        for (sem, val) in deps:
            key = id(sem)
            if key not in best or best[key][1] < val:
                best[key] = (sem, val)
        for key, (sem, val) in best.items():
            if self.waited[e].get(key, 0) < val:
                self.eng[e].wait_ge(sem, val)
                self.waited[e][key] = val
                self.ninst += 1

    def _deps(self, e, reads, writes, skip_sem=None):
        deps = []
        mysem = self.sem.get(e)
        for t in reads:
            if t.w is not None:
                deps.append(t.w)
            if t.excl:
                for st in t.r:
                    if st[0] is not mysem:
                        deps.append(st)
        for t in writes:
            if t.w is not None and t.w[0] is not mysem and t.w[0] is not skip_sem:
                deps.append(t.w)
            for st in t.r:
                if st[0] is not mysem:
                    deps.append(st)
        return deps

    def op(self, e, fn, reads=(), writes=(), inc=True):
        reads = [b.t for b in reads]
        writes = [b.t for b in writes]
        self._wait(e, self._deps(e, reads, writes))
        inst = fn(self.eng[e])
        self.ninst += 1
        if inc:
            self.cnt[e] += 1
            inst.then_inc(self.sem[e], 1)
            stamp = (self.sem[e], self.cnt[e])
        else:
            stamp = (self.sem[e], self.cnt[e] + 1)
        for t in writes:
            t.w = stamp
            t.r = []
        for t in reads:
            t.r.append(stamp)
        return inst

    def dma(self, q, out, in_, reads=(), writes=(), is_output=False, indirect=None, owner=None, **kw):
        reads = [b.t for b in reads]
        writes = [b.t for b in writes]
        if owner is None:
            owner = writes[0] if writes else reads[0]
        else:
            owner = owner.t
        if owner.dsem is None:
            owner.dsem = self.nc.alloc_semaphore("d%d_%s" % (len(self.dtiles), owner.name))
            self.dtiles.append(owner)
        self._wait(q, self._deps(None, reads, writes, skip_sem=owner.dsem))
        if indirect is not None:
            inst = self.eng[q].indirect_dma_start(out=out, out_offset=None, in_=in_, in_offset=indirect)
        else:
            inst = self.eng[q].dma_start(out=out, in_=in_, **kw)
        self.ninst += 1
        owner.dcnt += 16
        inst.then_inc(owner.dsem, 16)
        stamp = (owner.dsem, owner.dcnt)
        for t in writes:
            t.w = stamp
            t.r = []
        for t in reads:
            t.r.append(stamp)
        if is_output:
            self.out_stamps.append(stamp)
        return inst

    def barrier(self):
        stamps = [(self.sem[k], self.cnt[k]) for k in self.sem if self.cnt[k] > 0]
        stamps += [(t.dsem, t.dcnt) for t in self.dtiles if t.dcnt > 0]
        for e in self.eng:
            self._wait(e, stamps)

    def finish(self):
        self._wait('sp', self.out_stamps)


class Arena:
    def __init__(self, nc, words):
        self.t = nc.alloc_sbuf_tensor("arena", [128, words], F32)
        self.top = 0
        self.words = words
        self.n = 0

    def mark(self):
        return self.top

    def reset(self, m):
        self.top = m

    def _take(self, w):
        o = self.top
        self.top += w
        assert self.top <= self.words, ("SBUF arena overflow", self.top, self.words)
        self.n += 1
        return o

    def f32(self, n, parts=128, name=None):
        o = self._take(n)
        return Buf(self.t[0:parts, o:o + n], name or ("b%d" % self.n))

    def bf(self, n, parts=128, name=None):
        w = (n + 1) // 2
        o = self._take(w)
        return Buf(self.t[0:parts, o:o + w].bitcast(BF16)[:, 0:n], name or ("b%d" % self.n))

    def i32(self, n, parts=128, name=None):
        o = self._take(n)
        return Buf(self.t[0:parts, o:o + n].bitcast(I32), name or ("b%d" % self.n))

    def u32(self, n, parts=128, name=None):
        o = self._take(n)
        return Buf(self.t[0:parts, o:o + n].bitcast(U32), name or ("b%d" % self.n))


def build(PAST):
    NPG = PAST // 128
    NB = NPG // 2
    NPHYS = NS * NCORES * NPG + -(-(NS * NCORES * NPG) // 4)
    nc = bass.Bass("TRN2", target_bir_lowering=False)

    def din(name, shape, dt=F32):
        return nc.dram_tensor(name, list(shape), dt, kind="ExternalInput").ap()

    def dout(name, shape, dt=F32):
        return nc.dram_tensor(name, list(shape), dt, kind="ExternalOutput").ap()

    xp = din("xp", [T, D])
    xs = din("xs", [NS, D])
    ck = din("ck", [NPHYS, 128, NH, HD])
    cv = din("cv", [NPHYS, 128, NH, HD])
    pt = din("pt", [NS, NPG], I32)
    sg = din("sg", [NS, NH, HD, HD])
    scd = din("sc", [NS, 3, GQ])
    n1w = din("n1w", [128, KC])
    n2w = din("n2w", [128, KC])
    fnw = din("fnw", [128, D])
    w_in = din("w_in", [D, INC])
    cwd = din("cw", [128, 12, 4])
    cw4d = din("cw4", [NS, 4, GQ])
    alogd = din("alog", [128, NH])
    dtbd = din("dtb", [128, NH])
    gnwd = din("gnw", [128, HD])
    wod = din("wo", [D, D])
    wgd = din("wg", [D, DFF])
    wud = din("wu", [D, DFF])
    wdd = din("wd", [DFF, D])
    cspd = din("csp", [T, 256])
    cssd = din("css", [NS, 256])

    yp = dout("yp", [T, D])
    ys = dout("ys", [NS, D])
    kp = dout("kp", [T, 512])
    vp = dout("vp", [T, 512])
    kso = dout("ks", [NS, 512])
    vso = dout("vs", [NS, 512])
    gp = dout("gp", [NH, HD, HD])
    gso = dout("gs", [NS * NH, HD, HD])
    cpo = dout("cp", [3, GQ])
    cso = dout("cs", [NS, 3, GQ])

    x2s = nc.dram_tensor("x2s", [T, D], F32, kind="Internal").ap()
    sqd = nc.dram_tensor("sqd", [NS, 1536], F32, kind="Internal").ap()
    soad = nc.dram_tensor("soad", [NS, 512], F32, kind="Internal").ap()

    S = Sched(nc)
    A = Arena(nc, 53200)
    PSB = []
    for i in range(8):
        PSB.append(Buf(nc.alloc_psum_tensor("ps%d" % i, [128, 512], F32)[:, :], "ps%d" % i))
        PSB[-1].t.excl = True
    rr = [0]
    rrl = [list(range(8))]

    def bank():
        b = PSB[rrl[0][rr[0] % len(rrl[0])]]
        rr[0] += 1
        return b

    def pbf(b):
        return b.ap[:, :].bitcast(BF16)

    def mm(out, pairs, reads, writes):
        n = len(pairs)
        for i, (l, r) in enumerate(pairs):
            S.op('pe', lambda e, l=l, r=r, i=i: e.matmul(out, lhsT=l, rhs=r, start=(i == 0), stop=(i == n - 1)),
                 reads=reads, writes=writes, inc=(i == n - 1))

    def tr(out, in_, ident, reads, writes, inc=True):
        S.op('pe', lambda e: e.transpose(out=out, in_=in_, identity=ident),
             reads=list(reads) + [ident_b if ident.dtype == BF16 else ident_f], writes=writes, inc=inc)

    def act(out, in_, func, reads, writes, **kw):
        S.op('act', lambda e: e.activation(out=out, in_=in_, func=func, **kw), reads=reads, writes=writes)

    def tt(out, in0, in1, op, reads, writes, e='dve'):
        S.op(e, lambda en: en.tensor_tensor(out=out, in0=in0, in1=in1, op=op), reads=reads, writes=writes)

    def ts(out, in0, s1, s2, op0, op1, reads, writes, e='dve'):
        if op1 is None:
            S.op(e, lambda en: en.tensor_scalar(out=out, in0=in0, scalar1=s1, scalar2=None, op0=op0), reads=reads, writes=writes)
        else:
            S.op(e, lambda en: en.tensor_scalar(out=out, in0=in0, scalar1=s1, scalar2=s2, op0=op0, op1=op1), reads=reads, writes=writes)

    def stt(out, in0, sc, in1, op0, op1, reads, writes):
        S.op('dve', lambda en: en.scalar_tensor_tensor(out=out, in0=in0, scalar=sc, in1=in1, op0=op0, op1=op1), reads=reads, writes=writes)

    def cp(out, in_, reads, writes, e='dve'):
        if e == 'act':
            S.op('act', lambda en: en.activation(out=out, in_=in_, func=AF.Copy), reads=reads, writes=writes)
        else:
            S.op(e, lambda en: en.tensor_copy(out=out, in_=in_), reads=reads, writes=writes)

    def red(out, in_, op, reads, writes):
        S.op('dve', lambda en: en.tensor_reduce(out=out, in_=in_, axis=AX.X, op=op), reads=reads, writes=writes)

    def mset(ap, val, writes, e='dve'):
        S.op(e, lambda en: en.memset(ap, val), writes=writes)

    def recip(ap, bufs):
        S.op('dve', lambda en: en.reciprocal(out=ap, in_=ap), reads=bufs, writes=bufs)

    def rsqrt(out, in_, scale, reads, writes):
        act(out, in_, AF.Sqrt, list(reads) + [epsb], writes, scale=scale, bias=epsb.ap[0:out.shape[0], :])
        recip(out, writes)

    def softplus(dst, tmp, src_bufs, n):
        stt(tmp, dst, -1.0, dst, ALU.mult, ALU.max, src_bufs, src_bufs)
        act(tmp, tmp, AF.Exp, src_bufs, src_bufs, scale=-1.0)
        act(tmp, tmp, AF.Ln, src_bufs + [ones_f], src_bufs, bias=ones_f.ap[0:n, 0:1])
        stt(dst, dst, 0.0, tmp, ALU.max, ALU.add, src_bufs, src_bufs)

    def bc(ap, shape):
        return ap.to_broadcast(list(shape))

    NBX = max(NB, 8)
    ident_f = A.f32(128, name="ident_f")
    ident_b = A.bf(128, name="ident_b")
    ones_f = A.f32(128, name="ones_f")
    ones_b = A.bf(128, name="ones_b")
    triU = A.f32(128, name="triU")
    maskU = A.f32(128, name="maskU")
    strictU = A.f32(128, name="strictU")
    cmask = A.f32(128, name="cmask")
    epsb = A.f32(1, name="epsb")
    negrow = A.f32(1, name="negrow")
    onehot4 = A.f32(NS * 4, name="onehot4")
    iotaNB = A.f32(NBX, parts=16, name="iotaNB")
    rowoff = A.f32(96, name="rowoff")
    pairsel = A.f32(NBX, name="pairsel")
    n1 = A.f32(KC, name="n1")
    n2 = A.f32(KC, name="n2")
    fnb = A.f32(D, name="fnb")
    cwt = A.f32(48, name="cwt")
    negA = A.f32(NH, name="negA")
    dtb = A.f32(NH, name="dtb")
    gnb = A.f32(HD, name="gnb")
    catTs = A.bf(KC * NS, name="catTs")
    assert A.top <= 2500, A.top

    def asel(buf, pattern, op, fill, base, cm):
        S.op('pool', lambda e: e.affine_select(out=buf.ap, in_=buf.ap, pattern=pattern, compare_op=op, fill=fill,
                                               base=base, channel_multiplier=cm), reads=[buf], writes=[buf])

    mset(ident_f.ap, 1.0, [ident_f], e='pool')
    asel(ident_f, [[-1, 128]], ALU.is_equal, 0.0, 0, 1)
    cp(ident_b.ap, ident_f.ap, [ident_f], [ident_b], e='pool')
    mset(ones_f.ap, 1.0, [ones_f], e='pool')
    mset(ones_b.ap, 1.0, [ones_b], e='pool')
    mset(triU.ap, 1.0, [triU], e='pool')
    asel(triU, [[1, 128]], ALU.is_ge, 0.0, 0, -1)
    mset(maskU.ap, 0.0, [maskU], e='pool')
    asel(maskU, [[1, 128]], ALU.is_ge, NEG, 0, -1)
    mset(strictU.ap, 1.0, [strictU], e='pool')
    asel(strictU, [[1, 128]], ALU.is_gt, 0.0, 0, -1)
    mset(cmask.ap, 0.0, [cmask], e='pool')
    asel(cmask, [[-1, 128]], ALU.is_ge, NEG, 0, 1)
    mset(epsb.ap, EPS, [epsb], e='pool')
    mset(negrow.ap, 0.0, [negrow], e='pool')
    asel(negrow, [[0, 1]], ALU.is_ge, NEG, 0, -1)
    mset(onehot4.ap, 1.0, [onehot4], e='pool')
    asel(onehot4, [[1, NS], [-1, 4]], ALU.is_equal, 0.0, 0, 0)
    S.op('pool', lambda e: e.iota(iotaNB.ap, pattern=[[1, NBX]], base=0, channel_multiplier=0,
                                  allow_small_or_imprecise_dtypes=True), writes=[iotaNB])
    S.op('pool', lambda e: e.iota(rowoff.ap, pattern=[[0, NS], [1, NH], [0, 6]], base=0, channel_multiplier=4,
                                  allow_small_or_imprecise_dtypes=True), writes=[rowoff])
    mset(pairsel.ap, 1.0, [pairsel], e='pool')
    asel(pairsel, [[-2, NBX]], ALU.is_ge, 0.0, 0, 1)
    asel(pairsel, [[2, NBX]], ALU.is_ge, 0.0, 1, -1)

    constb = Buf(None, "constgrp")
    cl = [(n1, n1w), (n2, n2w), (fnb, fnw), (cwt, cwd.rearrange("p c j -> p (c j)")), (negA, alogd), (dtb, dtbd), (gnb, gnwd)]
    for (b_, d_) in cl:
        S.dma('sp', b_.ap, d_, writes=[b_], owner=constb)
    for (b_, _) in cl:
        b_.t.w = (constb.t.dsem, constb.t.dcnt)
    act(negA.ap, negA.ap, AF.Exp, [negA], [negA])
    ts(negA.ap, negA.ap, -1.0, None, ALU.mult, None, [negA], [negA])

    O_WI, O_WORK, O_P2, O_OBT, O_OAT = 2500, 16900, 36100, 44900, 49000
    A.top = O_WI
    wi = A.bf(KC * INC, name="wi")
    assert A.top <= O_WORK
    wi3 = wi.ap.rearrange("p (c n) -> p c n", c=KC)
    for c in range(KC):
        for (a0, a1) in ((0, 1796), (1796, INC)):
            S.dma('pool', wi3[:, c, a0:a1], w_in[c * 128:(c + 1) * 128, a0:a1], writes=[wi])

    A.top = O_WORK
    P4 = NS
    xs_sb = A.f32(D, parts=P4, name="xs_sb")
    css = A.f32(256, parts=P4, name="css")
    prj = A.f32(INC, parts=P4, name="prj")
    cats = A.f32(D, parts=P4, name="cats")
    catb = A.bf(D, parts=P4, name="catb")
    s1 = A.f32(GQ, parts=P4, name="s1")
    s2 = A.f32(GQ, parts=P4, name="s2")
    sm = A.f32(96, parts=P4, name="sm")
    mark_sk = A.mark()
    bufA = A.f32(GQ, parts=P4, name="bufA")
    bufB = A.f32(GQ, parts=P4, name="bufB")
    qkvs = A.f32(GQ, parts=P4, name="qkvs")
    xns = A.bf(D, parts=P4, name="xns")
    xnTs = A.bf(KC * NS, name="xnTs")
    xnTs3 = xnTs.ap.rearrange("p (c b) -> p c b", c=KC)
    Ss = A.f32(16 * 128, name="Ss")
    Ss3 = Ss.ap.rearrange("p (g v) -> p g v", g=16)
    Sout = A.f32(16 * 128, name="Sout")
    Sout3 = Sout.ap.rearrange("p (g v) -> p g v", g=16)
    qkTs = A.f32(32, name="qkTs")
    qkTs4 = qkTs.ap.rearrange("p (s h b) -> p s h b", s=2, h=NH)
    kqS = A.f32(32, name="kqS")
    kqS3 = kqS.ap.rearrange("p (g s) -> p g s", g=16)
    kqtm = A.f32(1024, parts=P4, name="kqtm")
    kqtm4 = kqtm.ap.rearrange("p (s h v) -> p s h v", s=2, h=NH)
    vn = A.f32(512, parts=P4, name="vn")
    os_ = A.f32(512, parts=P4, name="os_")
    knm = A.f32(NS * 512, parts=P4, name="knm")
    Rg = A.f32(16, parts=P4, name="Rg")
    egbc = A.f32(16, name="egbc")

    S.dma('sp', xs_sb.ap, xs[:, :], writes=[xs_sb])
    S.dma('sp', css.ap, cssd[:, :], writes=[css])
    S.dma('sp', Ss3, sg.rearrange("b h k v -> k (b h) v"), writes=[Ss])
    ddum = Buf(None, "ddum")
    S.dma('sp', cso[:, 0:2, :], scd[:, 1:3, :], owner=ddum, is_output=True)
    act(s1.ap[:, 0:D], xs_sb.ap, AF.Square, [xs_sb], [s1, sm], accum_out=sm.ap[:, 0:1])
    rsqrt(sm.ap[:, 0:1], sm.ap[:, 0:1], 1.0 / D, [sm], [sm])
    ts(xns.ap, xs_sb.ap, sm.ap[:, 0:1], None, ALU.mult, None, [xs_sb, sm], [xns])
    pb = bank()
    for c in range(KC):
        tr(pbf(pb)[:, c * NS:(c + 1) * NS], xns.ap[:, c * 128:(c + 1) * 128], ident_b.ap[0:P4, 0:P4], [xns], [pb], inc=(c == KC - 1))
    tt(xnTs3, pbf(pb)[:, 0:KC * NS].rearrange("p (c b) -> p c b", c=KC), bc(n1.ap.unsqueeze(2), [128, KC, NS]), ALU.mult, [pb, n1], [xnTs])
    for j in range(8):
        c0 = j * 512
        w = min(512, INC - c0)
        pj = bank()
        mm(pj.ap[0:P4, 0:w], [(xnTs3[:, c, :], wi3[:, c, c0:c0 + w]) for c in range(KC)], [wi, xnTs], [pj])
        cp(prj.ap[:, c0:c0 + w], pj.ap[0:P4, 0:w], [pj], [prj], e=('act' if j % 2 == 0 else 'dve'))
    S.dma('pool', cso[:, 2, :], prj.ap[:, 0:GQ], reads=[prj], is_output=True)
    S.dma('sp', bufB.ap, cw4d[:, 3, :], writes=[bufB])
    tt(s1.ap, prj.ap[:, 0:GQ], bufB.ap, ALU.mult, [prj, bufB], [s1])
    for j in range(3):
        S.dma('sp', bufA.ap, scd[:, j, :], writes=[bufA])
        S.dma('sp', bufB.ap, cw4d[:, j, :], writes=[bufB])
        tt(s2.ap, bufA.ap, bufB.ap, ALU.mult, [bufA, bufB], [s2])
        tt(s1.ap, s1.ap, s2.ap, ALU.add, [s1, s2], [s1])
    act(qkvs.ap, s1.ap, AF.Silu, [s1], [qkvs])
    tt(s2.ap[:, 0:1024], qkvs.ap[:, 0:1024], qkvs.ap[:, 0:1024], ALU.mult, [qkvs], [s2])
    red(sm.ap[:, 8:16], s2.ap[:, 0:1024].rearrange("p (g d) -> p g d", g=8), ALU.add, [s2], [sm])
    rsqrt(sm.ap[:, 8:16], sm.ap[:, 8:16], 1.0, [sm], [sm])
    ts(sm.ap[:, 8:12], sm.ap[:, 8:12], SCALE, None, ALU.mult, None, [sm], [sm])
    qk8 = qkvs.ap[:, 0:1024].rearrange("p (g d) -> p g d", g=8)
    tt(qk8, qk8, bc(sm.ap[:, 8:16].unsqueeze(2), [P4, 8, 128]), ALU.mult, [qkvs, sm], [qkvs])
    qs3 = qkvs.ap[:, 0:512].rearrange("p (h d) -> p h d", h=NH)
    ks3 = qkvs.ap[:, 512:1024].rearrange("p (h d) -> p h d", h=NH)
    vs3 = qkvs.ap[:, 1024:1536].rearrange("p (h d) -> p h d", h=NH)
    sg0, sg1, sgg, sbeta, seg, sqk = (sm.ap[:, 16:20], sm.ap[:, 20:24], sm.ap[:, 24:28], sm.ap[:, 28:32],
                                      sm.ap[:, 32:36], sm.ap[:, 36:40])
    tt(sg0, prj.ap[:, AOFF:AOFF + 4], dtb.ap[0:P4, :], ALU.add, [prj, dtb], [sm])
    softplus(sg0, sg1, [sm], P4)
    tt(sgg, sg0, negA.ap[0:P4, :], ALU.mult, [sm, negA], [sm])
    act(sbeta, prj.ap[:, BOFF:BOFF + 4], AF.Sigmoid, [prj], [sm])
    act(seg, sgg, AF.Exp, [sm], [sm])
    pq = bank()
    for s_, src in enumerate((ks3, qs3)):
        for h in range(NH):
            tr(pq.ap[:, (s_ * NH + h) * NS:(s_ * NH + h + 1) * NS], src[:, h, :], ident_f.ap[0:P4, 0:P4], [qkvs], [pq],
               inc=(s_ == 1 and h == NH - 1))
    cp(qkTs.ap, pq.ap[:, 0:32], [pq], [qkTs])
    pq = bank()
    for b in range(NS):
        for h in range(NH):
            g = b * NH + h
            mm(pq.ap[:, g * 2:g * 2 + 2], [(Ss3[:, g, :], qkTs4[:, :, h, b])], [Ss, qkTs], [pq])
    cp(kqS.ap, pq.ap[:, 0:32], [pq], [kqS])
    kqS4 = kqS.ap.rearrange("p (b h s) -> p b h s", b=NS, h=NH)
    pk2 = [bank(), bank()]
    for s_ in range(2):
        for h in range(NH):
            tr(pk2[s_].ap[0:P4, h * 128:(h + 1) * 128], kqS4[:, :, h, s_], ident_f.ap, [kqS], [pk2[s_]], inc=(h == NH - 1))
        cp(kqtm.ap[:, s_ * 512:(s_ + 1) * 512], pk2[s_].ap[0:P4, :], [pk2[s_]], [kqtm], e=('act' if s_ == 0 else 'dve'))
    kS, qS = kqtm4[:, 0], kqtm4[:, 1]
    vn3 = vn.ap.rearrange("p (h v) -> p h v", h=NH)
    os3 = os_.ap.rearrange("p (h v) -> p h v", h=NH)

    def b4(ap):
        return bc(ap.unsqueeze(2), [P4, NH, 128])
    tt(vn3, kS, b4(seg), ALU.mult, [kqtm, sm], [vn])
    tt(vn3, vs3, vn3, ALU.subtract, [qkvs, vn], [vn])
    tt(vn3, vn3, b4(sbeta), ALU.mult, [vn, sm], [vn])
    s23 = s2.ap[:, 0:512].rearrange("p (h d) -> p h d", h=NH)
    tt(s23, qs3, ks3, ALU.mult, [qkvs], [s2])
    red(sqk, s23, ALU.add, [s2], [sm])
    tt(os3, qS, b4(seg), ALU.mult, [kqtm, sm], [os_])
    tt(s23, vn3, b4(sqk), ALU.mult, [vn, sm], [s2])
    tt(os_.ap, os_.ap, s2.ap[:, 0:512], ALU.add, [os_, s2], [os_])
    R3 = Rg.ap.rearrange("p (b h) -> p b h", b=NS)
    oh = onehot4.ap.rearrange("p (b c) -> p b c", b=NS)
    for b in range(NS):
        ts(R3[:, b, :], seg, ident_f.ap[0:P4, b:b + 1], None, ALU.mult, None, [sm, ident_f], [Rg])
        ts(knm.ap[:, b * 512:(b + 1) * 512], qkvs.ap[:, 512:1024], ident_f.ap[0:P4, b:b + 1], None, ALU.mult, None, [qkvs, ident_f], [knm])
    pq = bank()
    mm(pq.ap[:, 0:16], [(ones_f.ap[0:P4, :], Rg.ap)], [ones_f, Rg], [pq])
    cp(egbc.ap, pq.ap[:, 0:16], [pq], [egbc])
    for b in range(NS):
        pq = bank()
        for h in range(NH):
            mm(pq.ap[:, h * 128:(h + 1) * 128], [(knm.ap[:, b * 512 + h * 128:b * 512 + (h + 1) * 128], vn3[:, h, :])], [knm, vn], [pq])
        for h in range(NH):
            g = b * NH + h
            stt(Sout3[:, g, :], Ss3[:, g, :], egbc.ap[:, g:g + 1], pq.ap[:, h * 128:(h + 1) * 128], ALU.mult, ALU.add, [Ss, egbc, pq], [Sout])
    S.dma('pool', gso.rearrange("g k v -> k g v"), Sout3, reads=[Sout], is_output=True)
    tt(s23, os3, os3, ALU.mult, [os_], [s2])
    red(sm.ap[:, 40:44], s23, ALU.add, [s2], [sm])
    rsqrt(sm.ap[:, 40:44], sm.ap[:, 40:44], 1.0 / HD, [sm], [sm])
    tt(os3, os3, b4(sm.ap[:, 40:44]), ALU.mult, [os_, sm], [os_])
    tt(os3, os3, bc(gnb.ap[0:P4, :].unsqueeze(1), [P4, NH, 128]), ALU.mult, [os_, gnb], [os_])
    act(s2.ap[:, 0:512], prj.ap[:, ZOFF:ZOFF + 512], AF.Silu, [prj], [s2])
    tt(cats.ap[:, 0:512], os_.ap, s2.ap[:, 0:512], ALU.mult, [os_, s2], [cats])
    qb8 = prj.ap[:, QBOFF:QBOFF + 1024].rearrange("p (g d) -> p g d", g=8)
    css4 = css.ap.rearrange("p (s g f) -> p s g f", s=2, g=8)
    rts = s1.ap[:, 0:512].rearrange("p (k g f) -> p k g f", k=4, g=8)
    x1, x2 = qb8[:, :, 0:16], qb8[:, :, 16:32]
    tt(rts[:, 0], x1, css4[:, 0], ALU.mult, [prj, css], [s1])
    tt(rts[:, 1], x2, css4[:, 1], ALU.mult, [prj, css], [s1])
    tt(rts[:, 2], x2, css4[:, 0], ALU.mult, [prj, css], [s1])
    tt(rts[:, 3], x1, css4[:, 1], ALU.mult, [prj, css], [s1])
    tt(x1, rts[:, 0], rts[:, 1], ALU.subtract, [s1], [prj])
    tt(x2, rts[:, 2], rts[:, 3], ALU.add, [s1], [prj])
    S.dma('pool', kso[:, :], prj.ap[:, KBOFF:KBOFF + 512], reads=[prj], is_output=True)
    S.dma('pool', vso[:, :], prj.ap[:, VBOFF:VBOFF + 512], reads=[prj], is_output=True)

    if _STOP == "Sa":
        S.finish()
        return nc
    S.dma('sp', sqd[:, :], prj.ap[:, QBOFF:QBOFF + 1536], reads=[prj])
    S.dma('sp', soad[:, :], cats.ap[:, 0:512], reads=[cats])
    S.barrier()
    GR = 1
    NCH = 128 // GR
    O_PSI = 34810
    A.top = O_PSI
    ptc = A.i32(NS, name="ptc")
    ptcf = A.f32(NS, name="ptcf")
    iotaC = A.f32(NCH, name="iotaC")
    idxgf = A.f32(NS * NCH, name="idxgf")
    idxg = A.i32(NS * NCH, name="idxg")
    assert A.top <= 36100, A.top
    A.top = 44900
    acc = A.f32(NS * 512, name="acc")
    Gb = [A.f32(512, name="G%d" % i_) for i_ in range(4)]
    assert A.top <= 49000, A.top
    with nc.allow_non_contiguous_dma(reason="tiny page-table transpose"):
        S.dma('sp', ptc.ap[0:NPG, :], pt.rearrange("b n -> n b"), writes=[ptc])
    ckp = ck.rearrange("n (c r) h d -> (n c) (r h d)", r=GR)
    S.op('pool', lambda e: e.iota(iotaC.ap, pattern=[[1, NCH]], base=0, channel_multiplier=0,
                                  allow_small_or_imprecise_dtypes=True), writes=[iotaC])
    cp(ptcf.ap[0:NPG, :], ptc.ap[0:NPG, :], [ptc], [ptcf])
    stt(idxgf.ap[0:NPG, :].rearrange("p (b c) -> p b c", b=NS), bc(ptcf.ap[0:NPG, :].unsqueeze(2), [NPG, NS, NCH]), float(NCH),
        bc(iotaC.ap[0:NPG, :].unsqueeze(1), [NPG, NS, NCH]), ALU.mult, ALU.add, [ptcf, iotaC], [idxgf])
    ts(idxgf.ap[0:NPG, :], idxgf.ap[0:NPG, :], 0.0, float(NPHYS * NCH - 1), ALU.max, ALU.min, [idxgf], [idxgf])
    cp(idxg.ap[0:NPG, :], idxgf.ap[0:NPG, :], [idxgf], [idxg])
    mset(acc.ap, 0.0, [acc])
    acc3 = acc.ap.rearrange("p (b c) -> p b c", b=NS)
    ps_state = [0]

    ps_issued = [0]
    PS_TOT = NS * NCH

    def pagesum_chunks(n):
        for _ in range(n):
            k = ps_state[0]
            if k >= PS_TOT:
                return
            while ps_issued[0] < min(PS_TOT, k + 4):
                j_ = ps_issued[0]
                ps_issued[0] += 1
                Gj = Gb[j_ % 4]
                S.dma('pool', Gj.ap[0:NPG, :], ckp[:, :], writes=[Gj], reads=[idxg],
                      indirect=bass.IndirectOffsetOnAxis(ap=idxg.ap[0:NPG, j_:j_ + 1].bitcast(U32), axis=0))
            ps_state[0] += 1
            b = k // NCH
            G = Gb[k % 4]
            tt(acc3[0:NPG, b, :], acc3[0:NPG, b, :], G.ap[0:NPG, :], ALU.add, [acc, G], [acc], e='pool')

    if _STOP == "Sb":
        S.finish()
        return nc
    S.barrier()
    A.top = O_OAT
    oaT = A.bf(NH * T, name="oaT")
    A.top = O_OBT
    obT = A.bf(NH * T, name="obT")
    A.top = O_P2
    qbT = A.bf(NH * T, name="qbT")
    kbT = A.bf(NH * T, name="kbT")
    sel30 = A.f32(NT * 32, name="sel30")
    assert A.top <= O_OBT
    oaT3 = oaT.ap.rearrange("p (h t) -> p h t", h=NH)
    obT3 = obT.ap.rearrange("p (h t) -> p h t", h=NH)
    qbT3 = qbT.ap.rearrange("p (h t) -> p h t", h=NH)
    kbT3 = kbT.ap.rearrange("p (h t) -> p h t", h=NH)
    A.top = O_WORK
    ss = A.f32(1, name="ss")
    rstd = A.f32(1, name="rstd")
    xn = A.bf(D, name="xn")
    xnT = A.bf(D, name="xnT")
    xnT3 = xnT.ap.rearrange("p (c t) -> p c t", c=KC)
    raw = A.f32(12 * 131, name="raw")
    raw3 = raw.ap.rearrange("p (c t) -> p c t", c=12)
    cacc = A.f32(GQ, name="cacc")
    scr = A.f32(GQ, name="scr")
    xt = scr
    zs = A.bf(512, name="zs")
    tb = A.f32(GQ, name="tb")
    cst = A.f32(256, name="cst")
    absb = A.f32(8, name="absb")
    qkb = A.bf(1024, name="qkb")
    kmT = A.f32(NH * 8, name="kmT")
    kmTb = A.bf(NH * 8, name="kmTb")
    gsb = A.f32(32, name="gsb")
    top8 = A.f32(8, name="top8")
    gsm = A.f32(64, name="gsm")
    qkv = tb
    ssqk = A.f32(8, name="ssqk")
    ctab = A.f32(7 * 4, name="ctab")
    varb = A.bf(3 * 512, name="varb")
    varf = A.f32(4 * 512, name="varf")
    varTb = A.bf(3 * 512, name="varTb")
    qdTf = A.f32(512, name="qdTf")
    varb4 = varb.ap.rearrange("p (v h d) -> p v h d", v=3, h=NH)
    varf4 = varf.ap.rearrange("p (v h d) -> p v h d", v=4, h=NH)
    varTb4 = varTb.ap.rearrange("p (v h t) -> p v h t", v=3, h=NH)
    qdT3 = qdTf.ap.rearrange("p (h t) -> p h t", h=NH)
    qkT = A.f32(NH * 128, name="qkT")
    qkT3 = qkT.ap.rearrange("p (h t) -> p h t", h=NH)
    _mk = A.f32 if _NF32 else A.bf
    hsl = []
    for si in range(2):
        hsl.append({
            'dg': A.f32(128, name="dg%d" % si), 'decT': A.f32(128, name="decT%d" % si), 'decS': A.f32(128, name="decS%d" % si),
            'NTf': A.f32(128, name="NTf%d" % si),
            'Mm': [_mk(128, name="Mm0_%d" % si), _mk(128, name="Mm1_%d" % si)],
            'MT': [_mk(128, name="MT0_%d" % si), _mk(128, name="MT1_%d" % si)],
            'Qb': [_mk(128, name="Qb0_%d" % si), _mk(128, name="Qb1_%d" % si)],
            'banks': [PSB[3 * si + j_] for j_ in range(3)],
        })
    uu = A.f32(NH * 128, name="uu")
    wT = A.f32(NH * 128, name="wT")
    wT3 = wT.ap.rearrange("p (h t) -> p h t", h=NH)
    Sst = A.f32(NH * 128, name="Sst")
    vnew = A.f32(NH * 128, name="vnew")
    oss = A.f32(4, name="oss")
    print("phaseA top", A.top, O_P2)
    assert A.top <= O_P2, A.top

    mset(raw.ap, 0.0, [raw])
    mset(Sst.ap, 0.0, [Sst])
    mset(kmT.ap, 0.0, [kmT])
    mset(kmTb.ap, 0.0, [kmTb])
    mset(sel30.ap, 0.0, [sel30])
    cwt3 = cwt.ap.rearrange("p (c j) -> p c j", c=12)
    sel304 = sel30.ap.rearrange("p (m h n) -> p m h n", m=NT, h=NH)
    kmT3 = kmT.ap.rearrange("p (h n) -> p h n", h=NH)
    kmTb3 = kmTb.ap.rearrange("p (h n) -> p h n", h=NH)
    gsb3 = gsb.ap.rearrange("p (h n) -> p h n", h=NH)
    rrl[0] = [0, 1, 2, 3, 4, 5]
    pu, pw = PSB[6], PSB[7]
    c12 = lambda b_: b_.ap.rearrange("p (c t) -> p c t", c=12)

    for m in range(min(NT, _KNT)):
        tsl = slice(m * 128, (m + 1) * 128)
        xv = scr.ap[:, 0:D]
        S.dma('sp', xv, xp[tsl, :], writes=[scr])
        S.dma('sp', cst.ap, cspd[tsl, :], writes=[cst])
        act(xn.ap, xv, AF.Square, [scr], [xn, ss], accum_out=ss.ap)
        rsqrt(rstd.ap, ss.ap, 1.0 / D, [ss], [rstd])
        ts(xn.ap, xv, rstd.ap, None, ALU.mult, None, [scr, rstd], [xn])
        pb = bank()
        for c in range(KC):
            tr(pbf(pb)[:, c * 128:(c + 1) * 128], xn.ap[:, c * 128:(c + 1) * 128], ident_b.ap, [xn], [pb], inc=(c == KC - 1))
        tt(xnT3, pbf(pb).rearrange("p (c t) -> p c t", c=KC), bc(n1.ap.unsqueeze(2), [128, KC, 128]), ALU.mult, [pb, n1], [xnT])
        pagesum_chunks((NS * NCH + NT - 1) // NT)
        for g in range(3):
            pq = bank()
            for j in range(4):
                cc = g * 4 + j
                mm(pq.ap[:, j * 128:(j + 1) * 128], [(wi3[:, c, cc * 128:(cc + 1) * 128], xnT3[:, c, :]) for c in range(KC)], [wi, xnT], [pq])
            cp(raw3[:, g * 4:(g + 1) * 4, 3:131], pq.ap.rearrange("p (c t) -> p c t", c=4), [pq], [raw], e='act')
        pz = bank()
        mm(pz.ap, [(xnT3[:, c, :], wi3[:, c, ZOFF:ZOFF + 512]) for c in range(KC)], [wi, xnT], [pz])
        act(zs.ap, pz.ap, AF.Silu, [pz], [zs])
        for j, off in enumerate((QBOFF, KBOFF, VBOFF)):
            pj = bank()
            mm(pj.ap, [(xnT3[:, c, :], wi3[:, c, off:off + 512]) for c in range(KC)], [wi, xnT], [pj])
            cp(tb.ap[:, j * 512:(j + 1) * 512], pj.ap, [pj], [tb], e=('act' if j % 2 == 0 else 'dve'))
        pa = bank()
        mm(pa.ap[:, 0:8], [(xnT3[:, c, :], wi3[:, c, AOFF:AOFF + 8]) for c in range(KC)], [wi, xnT], [pa])
        cp(absb.ap, pa.ap[:, 0:8], [pa], [absb])
        if _KSUB <= 1:
            continue
        tt(c12(cacc), raw3[:, :, 3:131], bc(cwt3[:, :, 3:4], [128, 12, 128]), ALU.mult, [raw, cwt], [cacc])
        for j in range(3):
            tt(c12(scr), raw3[:, :, j:j + 128], bc(cwt3[:, :, j:j + 1], [128, 12, 128]), ALU.mult, [raw, cwt], [scr])
            tt(cacc.ap, cacc.ap, scr.ap, ALU.add, [cacc, scr], [cacc])
        cp(raw3[:, :, 0:3], raw3[:, :, 128:131], [raw], [raw])
        act(cacc.ap, cacc.ap, AF.Silu, [cacc], [cacc])
        if _KSUB <= 2:
            continue
        tb3 = tb.ap[:, 0:1024].rearrange("p (g d) -> p g d", g=8)
        cs_m = cst.ap.rearrange("p (s g f) -> p s g f", s=2, g=8)
        rt3 = scr.ap[:, 0:512].rearrange("p (k g f) -> p k g f", k=4, g=8)
        x1, x2 = tb3[:, :, 0:16], tb3[:, :, 16:32]
        tt(rt3[:, 0], x1, cs_m[:, 0], ALU.mult, [tb, cst], [scr])
        tt(rt3[:, 1], x2, cs_m[:, 1], ALU.mult, [tb, cst], [scr])
        tt(rt3[:, 2], x2, cs_m[:, 0], ALU.mult, [tb, cst], [scr])
        tt(rt3[:, 3], x1, cs_m[:, 1], ALU.mult, [tb, cst], [scr])
        tt(x1, rt3[:, 0], rt3[:, 1], ALU.subtract, [scr], [tb])
        tt(x2, rt3[:, 2], rt3[:, 3], ALU.add, [scr], [tb])
        S.dma('sp', kp[tsl, :], tb.ap[:, 512:1024], reads=[tb], is_output=True)
        S.dma('sp', vp[tsl, :], tb.ap[:, 1024:1536], reads=[tb], is_output=True)
        cp(qkb.ap, tb.ap[:, 0:1024], [tb], [qkb])
        pt_ = bank()
        for g in range(8):
            tr(pbf(pt_)[:, g * 128:(g + 1) * 128], qkb.ap[:, g * 128:(g + 1) * 128], ident_b.ap, [qkb], [pt_], inc=(g == 7))
        ptv = pbf(pt_).rearrange("p (g t) -> p g t", g=8)
        cp(qbT3[:, :, tsl], ptv[:, 0:4, :], [pt_], [qbT])
        cp(kbT3[:, :, tsl], ptv[:, 4:8, :], [pt_], [kbT], e='act')
        if _KSUB <= 3:
            continue
        nbk = m // 2
        pk = bank()
        for h in range(NH):
            mm(pk.ap[:, h:h + 1], [(qkb.ap[:, 512 + h * 128:512 + (h + 1) * 128], ones_b.ap[:, 0:1])], [qkb, ones_b], [pk])
        if m % 2 == 0:
            ts(kmT3[:, :, nbk], pk.ap[:, 0:4], 1.0 / 256, None, ALU.mult, None, [pk], [kmT])
        else:
            stt(kmT3[:, :, nbk], pk.ap[:, 0:4], 1.0 / 256, kmT3[:, :, nbk], ALU.mult, ALU.add, [pk, kmT], [kmT])
        if nbk >= 1:
            pg = bank()
            for h in range(NH):
                mm(pg.ap[:, h * 8:(h + 1) * 8], [(qbT3[:, h, tsl], kmTb3[:, h, :])], [qbT, kmTb], [pg])
            cp(gsb.ap, pg.ap[:, 0:32], [pg], [gsb])
            if nbk < 8:
                mset(gsb3[:, :, nbk:8], NEG, [gsb])
            for h in range(NH):
                S.op('dve', lambda e, h=h: e.max(out=top8.ap, in_=gsb3[:, h, :]), reads=[gsb], writes=[top8])
                ts(sel304[:, m, h, :], gsb3[:, h, :], top8.ap[:, 2:3], -NEG, ALU.is_ge, ALU.mult, [gsb, top8], [sel30])
        if m % 2 == 1:
            cp(kmTb.ap, kmT.ap, [kmT], [kmTb])
        if _KSUB <= 4:
            continue
        a_, b_ = absb.ap[:, 0:4], absb.ap[:, 4:8]
        g0 = gsm.ap[:, 0:4]
        g1 = gsm.ap[:, 4:8]
        gg = gsm.ap[:, 8:12]
        beta = gsm.ap[:, 12:16]
        gcl = gsm.ap[:, 16:24]
        eg = gsm.ap[:, 24:28]
        egl = gsm.ap[:, 28:32]
        eglast = gsm.ap[:, 32:36]
        tt(g0, a_, dtb.ap, ALU.add, [absb, dtb], [gsm])
        softplus(g0, g1, [gsm], 128)
        tt(gg, g0, negA.ap, ALU.mult, [gsm, negA], [gsm])
        act(beta, b_, AF.Sigmoid, [absb], [gsm])
        pgc = bank()
        mm(pgc.ap[:, 0:4], [(triU.ap, gg)], [triU, gsm], [pgc])
        mm(pgc.ap[:, 4:8], [(ones_f.ap, gg)], [ones_f, gsm], [pgc])
        cp(gcl, pgc.ap[:, 0:8], [pgc], [gsm])
        gcum, glast = gcl[:, 0:4], gcl[:, 4:8]
        act(eg, gcum, AF.Exp, [gsm], [gsm])
        tt(egl, glast, gcum, ALU.subtract, [gsm], [gsm])
        act(egl, egl, AF.Exp, [gsm], [gsm])
        act(eglast, glast, AF.Exp, [gsm], [gsm])
        cacc3 = c12(cacc)
        for g in range(3):
            pq = bank()
            for j in range(4):
                tr(pq.ap[:, j * 128:(j + 1) * 128], cacc3[:, g * 4 + j, :], ident_f.ap, [cacc], [pq], inc=(j == 3))
            cp(qkv.ap[:, g * 512:(g + 1) * 512], pq.ap, [pq], [qkv], e=('act' if g % 2 == 0 else 'dve'))
        if m == NT - 1:
            for j in range(3):
                pc = bank()
                mm(pc.ap[0:3, :], [(xnT3[:, c, 125:128], wi3[:, c, j * 512:(j + 1) * 512]) for c in range(KC)], [wi, xnT], [pc])
                cp(cacc.ap[0:3, j * 512:(j + 1) * 512], pc.ap[0:3, :], [pc], [cacc])
            S.dma('sp', cpo[:, :], cacc.ap[0:3, :], reads=[cacc], is_output=True)
        sqb = scr.ap[:, 0:1024]
        tt(sqb, qkv.ap[:, 0:1024], qkv.ap[:, 0:1024], ALU.mult, [qkv], [scr])
        red(ssqk.ap, sqb.rearrange("p (g d) -> p g d", g=8), ALU.add, [scr], [ssqk])
        rsqrt(ssqk.ap, ssqk.ap, 1.0, [ssqk], [ssqk])
        rq, rk = ssqk.ap[:, 0:4], ssqk.ap[:, 4:8]
        ct = ctab.ap.rearrange("p (v h) -> p v h", v=7)
        ts(ct[:, 0], rq, SCALE, None, ALU.mult, None, [ssqk], [ctab])
        tt(ct[:, 1], ct[:, 0], eg, ALU.mult, [ctab, gsm], [ctab])
        cp(ct[:, 2], rk, [ssqk], [ctab])
        tt(ct[:, 3], rk, beta, ALU.mult, [ssqk, gsm], [ctab])
        tt(ct[:, 4], ct[:, 3], eg, ALU.mult, [ctab, gsm], [ctab])
        tt(ct[:, 5], rk, egl, ALU.mult, [ssqk, gsm], [ctab])
        cp(ct[:, 6], beta, [gsm], [ctab])
        q3 = qkv.ap[:, 0:512].rearrange("p (h d) -> p h d", h=NH)
        k3 = qkv.ap[:, 512:1024].rearrange("p (h d) -> p h d", h=NH)
        v3 = qkv.ap[:, 1024:1536].rearrange("p (h d) -> p h d", h=NH)

        def cb(i):
            return bc(ct[:, i].unsqueeze(2), [128, NH, 128])
        tt(varb4[:, 0], q3, cb(0), ALU.mult, [qkv, ctab], [varb])
        tt(varb4[:, 1], k3, cb(2), ALU.mult, [qkv, ctab], [varb])
        tt(varb4[:, 2], k3, cb(3), ALU.mult, [qkv, ctab], [varb])
        tt(varf4[:, 0], q3, cb(1), ALU.mult, [qkv, ctab], [varf])
        tt(varf4[:, 1], k3, cb(4), ALU.mult, [qkv, ctab], [varf])
        tt(varf4[:, 2], k3, cb(5), ALU.mult, [qkv, ctab], [varf])
        tt(varf4[:, 3], v3, cb(6), ALU.mult, [qkv, ctab], [varf])
        for g in range(2):
            pq = bank()
            n8 = 8 if g == 0 else 4
            for j in range(n8):
                v_, h = (g * 8 + j) // 4, (g * 8 + j) % 4
                tr(pbf(pq)[:, j * 128:(j + 1) * 128], varb4[:, v_, h, :], ident_b.ap, [varb], [pq], inc=(j == n8 - 1))
            cp(varTb.ap[:, g * 1024:g * 1024 + n8 * 128], pbf(pq)[:, 0:n8 * 128], [pq], [varTb], e=('act' if g == 0 else 'dve'))
        pq = bank()
        for h in range(NH):
            tr(pq.ap[:, h * 128:(h + 1) * 128], varf4[:, 0, h, :], ident_f.ap, [varf], [pq], inc=(h == NH - 1))
        cp(qdTf.ap, pq.ap, [pq], [qdTf], e='act')
        qnT, knT, kbTv = varTb4[:, 0], varTb4[:, 1], varTb4[:, 2]
        kbg, kdv, vbv = varf4[:, 1], varf4[:, 2], varf4[:, 3]
        if _KSUB <= 6:
            continue
        def chain(h, sl):
            dg, decT, decS, NTf, Mm, MT, Qb = sl['dg'], sl['decT'], sl['decS'], sl['NTf'], sl['Mm'], sl['MT'], sl['Qb']
            pj = sl['banks'][0]
            pns = sl['banks'][1:3]
            ts(dg.ap, ident_f.ap, gcum[:, h:h + 1], None, ALU.mult, None, [ident_f, gsm], [dg])
            mm(pj.ap[:, 0:128], [(ones_f.ap, dg.ap)], [ones_f, dg], [pj])
            yield
            stt(dg.ap, pj.ap[:, 0:128], gcum[:, h:h + 1], maskU.ap, ALU.subtract, ALU.min, [pj, gsm, maskU], [dg])
            act(decT.ap, dg.ap, AF.Exp, [dg], [decT])
            tt(decS.ap, decT.ap, strictU.ap, ALU.mult, [decT, strictU], [decS])
            mm(pj.ap[:, 128:256], [(knT[:, h, :], qnT[:, h, :])], [varTb], [pj])
            mm(pj.ap[:, 256:384], [(knT[:, h, :], kbTv[:, h, :])], [varTb], [pj])
            yield
            tt(qkT3[:, h, :], pj.ap[:, 128:256], decT.ap, ALU.mult, [pj, decT], [qkT])
            stt(NTf.ap, pj.ap[:, 256:384], -1.0, decS.ap, ALU.mult, ALU.mult, [pj, decS], [NTf])
            cp(MT[0].ap, NTf.ap, [NTf], [MT[0]])
            pn = pns[0]
            tr(pn.ap[:, 0:128], NTf.ap, ident_f.ap, [NTf], [pn])
            yield
            cp(Mm[0].ap, pn.ap[:, 0:128], [pn], [Mm[0]], e='act')
            tt(NTf.ap, NTf.ap, ident_f.ap, ALU.add, [NTf, ident_f], [NTf])
            cp(Qb[0].ap, NTf.ap, [NTf], [Qb[0]])
            yield
            cur = 0
            for lv in range(6):
                nx = 1 - cur
                pn = pns[(lv + 1) % 2]
                mm(pn.ap[:, 0:128], [(MT[cur].ap, Mm[cur].ap)], [MT[cur], Mm[cur]], [pn])
                mm(pn.ap[:, 128:256], [(Mm[cur].ap, MT[cur].ap)], [MT[cur], Mm[cur]], [pn])
                yield
                cp(Mm[nx].ap, pn.ap[:, 0:128], [pn], [Mm[nx]], e='act')
                if lv < 5:
                    cp(MT[nx].ap, pn.ap[:, 128:256], [pn], [MT[nx]])
                yield
                mm(pn.ap[:, 256:384], [(Mm[nx].ap, Qb[cur].ap)], [Mm[nx], Qb[cur]], [pn])
                yield
                tt(NTf.ap, NTf.ap, pn.ap[:, 256:384], ALU.add, [NTf, pn], [NTf])
                if lv < 5:
                    cp(Qb[nx].ap, NTf.ap, [NTf], [Qb[nx]])
                yield
                cur = nx
            mm(pu.ap[:, h * 128:(h + 1) * 128], [(NTf.ap, vbv[:, h, :])], [NTf, varf], [pu])
            mm(pw.ap[:, h * 128:(h + 1) * 128], [(kbg[:, h, :], NTf.ap)], [NTf, varf], [pw])

        for hp in ((0, 1), (2, 3)):
            gens = [chain(hp[0], hsl[0]), chain(hp[1], hsl[1])]
            alive = [True, True]
            while any(alive):
                for gi_ in range(2):
                    if alive[gi_]:
                        try:
                            next(gens[gi_])
                        except StopIteration:
                            alive[gi_] = False
        cp(uu.ap, pu.ap, [pu], [uu], e='act')
        cp(wT.ap, pw.ap, [pw], [wT])
        if _KSUB <= 7:
            continue
        pv = bank()
        for h in range(NH):
            mm(pv.ap[:, h * 128:(h + 1) * 128], [(wT3[:, h, :], Sst.ap[:, h * 128:(h + 1) * 128])], [wT, Sst], [pv])
        tt(vnew.ap, uu.ap, pv.ap, ALU.subtract, [uu, pv], [vnew])
        po = bank()
        for h in range(NH):
            mm(po.ap[:, h * 128:(h + 1) * 128], [(qdT3[:, h, :], Sst.ap[:, h * 128:(h + 1) * 128]),
                                                 (qkT3[:, h, :], vnew.ap[:, h * 128:(h + 1) * 128])], [qdTf, Sst, qkT, vnew], [po])
        psn = bank()
        for h in range(NH):
            mm(psn.ap[:, h * 128:(h + 1) * 128], [(kdv[:, h, :], vnew.ap[:, h * 128:(h + 1) * 128])], [varf, vnew], [psn])
        Sst3 = Sst.ap.rearrange("p (h d) -> p h d", h=NH)
        tt(Sst3, Sst3, bc(eglast.unsqueeze(2), [128, NH, 128]), ALU.mult, [Sst, gsm], [Sst])
        tt(Sst.ap, Sst.ap, psn.ap, ALU.add, [Sst, psn], [Sst])
        if _KSUB <= 8:
            continue
        osb = scr.ap[:, 0:512]
        osq = scr.ap[:, 512:1024]
        oab = qkb.ap[:, 0:512]
        cp(osb, po.ap, [po], [scr], e='act')
        tt(osq, osb, osb, ALU.mult, [scr], [scr])
        red(oss.ap, osq.rearrange("p (h d) -> p h d", h=NH), ALU.add, [scr], [oss])
        rsqrt(oss.ap, oss.ap, 1.0 / HD, [oss], [oss])
        osb3 = osb.rearrange("p (h d) -> p h d", h=NH)
        tt(osb3, osb3, bc(oss.ap.unsqueeze(2), [128, NH, 128]), ALU.mult, [scr, oss], [scr])
        tt(osb3, osb3, bc(gnb.ap.unsqueeze(1), [128, NH, 128]), ALU.mult, [scr, gnb], [scr])
        tt(oab, osb, zs.ap, ALU.mult, [scr, zs], [qkb])
        pq = bank()
        for h in range(NH):
            tr(pbf(pq)[:, h * 128:(h + 1) * 128], oab[:, h * 128:(h + 1) * 128], ident_b.ap, [qkb], [pq], inc=(h == NH - 1))
        cp(oaT3[:, :, tsl], pbf(pq)[:, 0:512].rearrange("p (h t) -> p h t", h=NH), [pq], [oaT])
    S.dma('sp', gp.rearrange("h k v -> k h v"), Sst.ap.rearrange("p (h v) -> p h v", h=NH), reads=[Sst], is_output=True)

    pagesum_chunks(NS * NCH)
    S.barrier()
    rrl[0] = list(range(8))
    A.top = O_WORK
    P4 = NS
    prq = A.f32(1536, parts=P4, name="prq")
    cats = A.f32(D, parts=P4, name="cats")
    catb = A.bf(D, parts=P4, name="catb")
    S.dma('sp', prq.ap, sqd[:, :], writes=[prq])
    S.dma('sp', cats.ap[:, 0:512], soad[:, :], writes=[cats])
    pti = A.i32(NPG, parts=16, name="pti")
    ptf = A.f32(NPG, parts=16, name="ptf")
    qbc = A.f32(NS * 512, name="qbc")
    pr = A.f32(NH * 7 * 128, name="pr")
    pgs = A.f32(16, name="pgs")
    gT = A.f32(NBX, parts=16, name="gT")
    t8 = A.f32(8, parts=16, name="t8")
    i8 = A.u32(8, parts=16, name="i8")
    i8f = A.f32(8, parts=16, name="i8f")
    eqb = A.f32(NBX, parts=16, name="eqb")
    eq2 = A.f32(NBX, parts=16, name="eq2")
    physf = A.f32(6, parts=16, name="physf")
    R16 = A.f32(96, parts=16, name="R16")
    idxf = A.f32(96, name="idxf")
    idxi = A.i32(96, name="idxi")
    Kg = A.f32(NH * 7 * 128, name="Kg")
    Vg = A.f32(NH * 7 * 128, name="Vg")
    Kg4 = Kg.ap.rearrange("p (h g d) -> p h g d", h=NH, g=7)
    Vg4 = Vg.ap.rearrange("p (h g d) -> p h g d", h=NH, g=7)
    pr4 = pr.ap.rearrange("p (h g d) -> p h g d", h=NH, g=7)
    sl = A.f32(28, name="sl")
    sl3 = sl.ap.rearrange("p (h g) -> p h g", h=NH)
    pmx = A.f32(4, name="pmx")
    m4 = A.f32(1, parts=4, name="m4")
    Rm = A.f32(4, parts=4, name="Rm")
    ngm = A.f32(4, name="ngm")
    Ps = A.f32(28, name="Ps")
    Ps3 = Ps.ap.rearrange("p (h g) -> p h g", h=NH)
    PZ = A.f32(NS * 28 * 4, name="PZ")
    PZ5 = PZ.ap.rearrange("p (b h g c) -> p b h g c", b=NS, h=NH, g=7)
    psr = A.f32(4, name="psr")
    obacc = A.f32(512, parts=P4, name="obacc")
    denacc = A.f32(4, parts=P4, name="denacc")

    assert A.top <= 36100, A.top
    for b in range(NS):
        S.dma('sp', pti.ap[b * NH:(b + 1) * NH, :], pt[b:b + 1, :].partition_broadcast(NH), writes=[pti])
    cp(ptf.ap, pti.ap, [pti], [ptf])
    ckr = ck.rearrange("n r h d -> (n r h) d")
    cvr = cv.rearrange("n r h d -> (n r h) d")
    for b in range(NS):
        ts(pr.ap[0:P4, b * 512:(b + 1) * 512], prq.ap[:, 0:512], ident_f.ap[0:P4, b:b + 1], None, ALU.mult, None, [prq, ident_f], [pr])
        pq = bank()
        mm(pq.ap, [(ones_f.ap[0:P4, :], pr.ap[0:P4, b * 512:(b + 1) * 512])], [ones_f, pr], [pq])
        cp(qbc.ap[:, b * 512:(b + 1) * 512], pq.ap, [pq], [qbc], e='act')
    tt(pr.ap[0:NPG, 0:2048], acc.ap[0:NPG, :], qbc.ap[0:NPG, :], ALU.mult, [acc, qbc], [pr])
    red(pgs.ap[0:NPG, :], pr.ap[0:NPG, 0:2048].rearrange("p (g d) -> p g d", g=16), ALU.add, [pr], [pgs])
    pq = bank()
    mm(pq.ap[0:16, 0:NB], [(pgs.ap[0:NPG, :], pairsel.ap[0:NPG, 0:NB])], [pgs, pairsel], [pq])
    cp(gT.ap[:, 0:NB], pq.ap[0:16, 0:NB], [pq], [gT])
    S.op('dve', lambda e: e.max(out=t8.ap, in_=gT.ap[:, 0:NB]), reads=[gT], writes=[t8])
    S.op('dve', lambda e: e.max_index(out=i8.ap, in_max=t8.ap, in_values=gT.ap[:, 0:NB]), reads=[gT, t8], writes=[i8])
    cp(i8f.ap, i8.ap, [i8], [i8f])
    ptf3 = ptf.ap.rearrange("p (n j) -> p n j", j=2)
    for k in range(3):
        ts(eqb.ap[:, 0:NB], iotaNB.ap[:, 0:NB], i8f.ap[:, k:k + 1], None, ALU.is_equal, None, [iotaNB, i8f], [eqb])
        for j in range(2):
            tt(eq2.ap[:, 0:NB], eqb.ap[:, 0:NB], ptf3[:, :, j], ALU.mult, [eqb, ptf], [eq2])
            red(physf.ap[:, k * 2 + j:k * 2 + j + 1], eq2.ap[:, 0:NB], ALU.add, [eq2], [physf])
    R163 = R16.ap.rearrange("p (g c) -> p g c", g=16)
    for g in range(16):
        ts(R163[:, g, :], physf.ap, ident_f.ap[0:16, g:g + 1], None, ALU.mult, None, [physf, ident_f], [R16])
    pq = bank()
    mm(pq.ap[:, 0:96], [(ones_f.ap[0:16, :], R16.ap)], [ones_f, R16], [pq])
    stt(idxf.ap, pq.ap[:, 0:96], 512.0, rowoff.ap, ALU.mult, ALU.add, [pq, rowoff], [idxf])
    ts(idxf.ap, idxf.ap, 0.0, float(NPHYS * 512 - 1), ALU.max, ALU.min, [idxf], [idxf])
    cp(idxi.ap, idxf.ap, [idxf], [idxi])
    mset(PZ.ap, 0.0, [PZ])
    for b in range(NS):
        mset(Kg4[:, :, 6, :], 0.0, [Kg])
        mset(Vg4[:, :, 6, :], 0.0, [Vg])
        S.dma('pool', Kg4[0:1, :, 6, :], prq.ap[b:b + 1, 512:1024].rearrange("p (h d) -> p h d", h=NH), reads=[prq], writes=[Kg])
        S.dma('pool', Vg4[0:1, :, 6, :], prq.ap[b:b + 1, 1024:1536].rearrange("p (h d) -> p h d", h=NH), reads=[prq], writes=[Vg])
        for h in range(NH):
            for kj in range(6):
                col = (b * NH + h) * 6 + kj
                S.dma('pool', Kg4[:, h, kj, :], ckr[:, :], writes=[Kg], reads=[idxi],
                      indirect=bass.IndirectOffsetOnAxis(ap=idxi.ap[:, col:col + 1].bitcast(U32), axis=0))
                S.dma('pool', Vg4[:, h, kj, :], cvr[:, :], writes=[Vg], reads=[idxi],
                      indirect=bass.IndirectOffsetOnAxis(ap=idxi.ap[:, col:col + 1].bitcast(U32), axis=0))
        qb_b = qbc.ap[:, b * 512:(b + 1) * 512].rearrange("p (h d) -> p h d", h=NH)
        for h in range(NH):
            tt(pr4[:, h], Kg4[:, h], bc(qb_b[:, h, :].unsqueeze(1), [128, 7, 128]), ALU.mult, [Kg, qbc], [pr])
        red(sl.ap, pr.ap.rearrange("p (g d) -> p g d", g=28), ALU.add, [pr], [sl])
        ts(sl3[:, :, 6], sl3[:, :, 6], negrow.ap, None, ALU.add, None, [sl, negrow], [sl])
        red(pmx.ap, sl3, ALU.max, [sl], [pmx])
        pq = bank()
        tr(pq.ap[0:4, 0:128], pmx.ap, ident_f.ap, [pmx], [pq])
        red(m4.ap, pq.ap[0:4, 0:128], ALU.max, [pq], [m4])
        ts(Rm.ap, ident_f.ap[0:4, 0:4], m4.ap, None, ALU.mult, None, [ident_f, m4], [Rm])
        pq = bank()
        mm(pq.ap[:, 0:4], [(ones_f.ap[0:4, :], Rm.ap)], [ones_f, Rm], [pq])
        ts(ngm.ap, pq.ap[:, 0:4], -SCALE, None, ALU.mult, None, [pq], [ngm])
        for h in range(NH):
            act(Ps3[:, h, :], sl3[:, h, :], AF.Exp, [sl, ngm], [Ps], scale=SCALE, bias=ngm.ap[:, h:h + 1])
        cp(PZ5[:, b, :, :, b], Ps3, [Ps], [PZ])
        red(psr.ap, Ps3, ALU.add, [Ps], [psr])
        pa = bank()
        for h in range(NH):
            mm(pa.ap[0:P4, h * 128:(h + 1) * 128], [(PZ5[:, b, h, g, :], Vg4[:, h, g, :]) for g in range(7)], [PZ, Vg], [pa])
        pd = bank()
        mm(pd.ap[0:P4, 0:4], [(oh[:, b, :], psr.ap)], [onehot4, psr], [pd])
        if b == 0:
            cp(obacc.ap, pa.ap[0:P4, :], [pa], [obacc])
            cp(denacc.ap, pd.ap[0:P4, 0:4], [pd], [denacc])
        else:
            tt(obacc.ap, obacc.ap, pa.ap[0:P4, :], ALU.add, [obacc, pa], [obacc])
            tt(denacc.ap, denacc.ap, pd.ap[0:P4, 0:4], ALU.add, [denacc, pd], [denacc])
    recip(denacc.ap, [denacc])
    tt(cats.ap[:, 512:1024].rearrange("p (h d) -> p h d", h=NH), obacc.ap.rearrange("p (h d) -> p h d", h=NH),
       bc(denacc.ap.unsqueeze(2), [P4, NH, 128]), ALU.mult, [obacc, denacc], [cats])
    cp(catb.ap, cats.ap, [cats], [catb])
    pb = bank()
    for c in range(KC):
        tr(pbf(pb)[:, c * NS:(c + 1) * NS], catb.ap[:, c * 128:(c + 1) * 128], ident_b.ap[0:P4, 0:P4], [catb], [pb], inc=(c == KC - 1))
    cp(catTs.ap, pbf(pb)[:, 0:KC * NS], [pb], [catTs])
    if _STOP == "A":
        S.finish()
        return nc
    S.barrier()
    rrl[0] = list(range(8))
    O_WO, O_WG, O_WU, O_VBT, O_BW, O_CW, O_WD = 2500, 6600, 17900, 29400, 17900, 29400, 36200
    A.top = O_VBT
    vbt = A.bf(NT * 512, name="vbt")
    vbt3 = vbt.ap.rearrange("p (m c) -> p m c", m=NT)
    assert A.top <= O_P2, A.top
    A.top = O_BW
    NSLOT = 2
    slots = []
    for si in range(NSLOT):
        d_ = {}
        d_['Pm'] = A.bf(T, name="Pm%d" % si)
        d_['PT'] = A.bf(NT * 128, name="PT%d" % si)
        d_['mx'] = A.f32(8, name="mx%d" % si)
        d_['negm'] = A.f32(1, name="negm%d" % si)
        d_['pbias'] = A.f32(8, name="pbias%d" % si)
        d_['rs'] = A.f32(12, name="rs%d" % si)
        d_['rinv'] = A.f32(1, name="rinv%d" % si)
        d_['dsb'] = A.f32(128, name="dsb%d" % si)
        d_['obb'] = A.bf(128, name="obb%d" % si)
        d_['banks'] = [PSB[4 * si + j] for j in range(4)]
        d_['rr'] = 0
        slots.append(d_)
    assert A.top <= O_VBT, A.top
    for m in range(NT):
        S.dma('pool', vbt3[:, m, :], vp[m * 128:(m + 1) * 128, :], writes=[vbt])
    A.top = O_WO
    wo = A.bf(KC * D, name="wo")
    A.top = O_WG
    wg = A.bf(KC * DFF, name="wg")
    assert A.top <= O_WU, A.top
    wo3 = wo.ap.rearrange("p (c n) -> p c n", c=KC)
    wg3 = wg.ap.rearrange("p (c n) -> p c n", c=KC)
    for c in range(KC):
        S.dma('pool', wo3[:, c, :], wod[c * 128:(c + 1) * 128, :], writes=[wo])
    for c in range(KC):
        for (a0, a1) in ((0, 1408), (1408, DFF)):
            S.dma('pool', wg3[:, c, a0:a1], wgd[c * 128:(c + 1) * 128, a0:a1], writes=[wg])

    def unitB(h, i, sl):
        Pm, PT_, mx, negm, pbias, rs, rinv, dsb, obb = (sl['Pm'], sl['PT'], sl['mx'], sl['negm'], sl['pbias'], sl['rs'],
                                                        sl['rinv'], sl['dsb'], sl['obb'])
        PT3 = PT_.ap.rearrange("p (j q) -> p j q", j=NT)

        def sbank():
            b_ = sl['banks'][sl['rr'] % 4]
            sl['rr'] += 1
            return b_
        tsl = slice(i * 128, (i + 1) * 128)
        nk = i + 1
        nbk = i // 2
        ng = (nk + 3) // 4
        lb = []
        for g in range(ng):
            w = min(512, nk * 128 - g * 512)
            pl = sbank()
            mm(pl.ap[:, 0:w], [(qbT3[:, h, tsl], kbT3[:, h, g * 512:g * 512 + w])], [qbT, kbT], [pl])
            lb.append((pl, w))
        yield
        for g, (pl, w) in enumerate(lb):
            red(mx.ap[:, g:g + 1], pl.ap[:, 0:w], ALU.max, [pl], [mx])
        red(negm.ap, mx.ap[:, 0:ng], ALU.max, [mx], [negm])
        ts(negm.ap, negm.ap, -SCALE, None, ALU.mult, None, [negm], [negm])
        if nbk > 0:
            ts(pbias.ap[:, 0:nbk], sel304[:, i, h, 0:nbk], negm.ap, NEG, ALU.add, ALU.add, [sel30, negm], [pbias])
        pl = lb[i // 4][0]
        c0 = (i % 4) * 128
        tt(dsb.ap, pl.ap[:, c0:c0 + 128], cmask.ap, ALU.add, [pl, cmask], [dsb])
        yield
        ncol = 0
        for n in range(nbk):
            pl = lb[n // 2][0]
            c0 = (n % 2) * 256
            act(Pm.ap[:, n * 256:(n + 1) * 256], pl.ap[:, c0:c0 + 256], AF.Exp, [pl, pbias], [Pm, rs],
                scale=SCALE, bias=pbias.ap[:, n:n + 1], accum_out=rs.ap[:, ncol:ncol + 1])
            ncol += 1
        if i % 2 == 1:
            j = i - 1
            pl = lb[j // 4][0]
            c0 = (j % 4) * 128
            act(Pm.ap[:, j * 128:(j + 1) * 128], pl.ap[:, c0:c0 + 128], AF.Exp, [pl, negm], [Pm, rs],
                scale=SCALE, bias=negm.ap, accum_out=rs.ap[:, ncol:ncol + 1])
            ncol += 1
        act(Pm.ap[:, i * 128:(i + 1) * 128], dsb.ap, AF.Exp, [dsb, negm], [Pm, rs],
            scale=SCALE, bias=negm.ap, accum_out=rs.ap[:, ncol:ncol + 1])
        ncol += 1
        yield
        red(rinv.ap, rs.ap[:, 0:ncol], ALU.add, [rs], [rinv])
        recip(rinv.ap, [rinv])
        for g in range((nk + 7) // 8):
            n8 = min(8, nk - g * 8)
            pq = sbank()
            for j in range(n8):
                jj = g * 8 + j
                tr(pbf(pq)[:, j * 128:(j + 1) * 128], Pm.ap[:, jj * 128:(jj + 1) * 128], ident_b.ap, [Pm], [pq], inc=(j == n8 - 1))
            cp(PT_.ap[:, g * 1024:g * 1024 + n8 * 128], pbf(pq)[:, 0:n8 * 128], [pq], [PT_], e=('act' if g == 0 else 'dve'))
            yield
        po = sbank()
        mm(po.ap[:, 0:128], [(PT3[:, j, :], vbt3[:, j, h * 128:(h + 1) * 128]) for j in range(nk)], [PT_, vbt], [po])
        yield
        ts(obb.ap, po.ap[:, 0:128], rinv.ap, None, ALU.mult, None, [po, rinv], [obb])
        pq = sbank()
        tr(pbf(pq)[:, 0:128], obb.ap, ident_b.ap, [obb], [pq])
        yield
        cp(obT3[:, h, tsl], pbf(pq)[:, 0:128], [pq], [obT], e='act')

    units = [(h, i) for i in range(NT) for h in range(NH)]
    order = []
    lo, hi = 0, len(units) - 1
    while lo <= hi:
        order.append(units[hi])
        if lo != hi:
            order.append(units[lo])
        lo += 1
        hi -= 1
    pend = list(order)
    live = [None] * NSLOT
    while pend or any(g_ is not None for g_ in live):
        for si in range(NSLOT):
            if live[si] is None and pend:
                h_, i_ = pend.pop(0)
                live[si] = unitB(h_, i_, slots[si])
            if live[si] is not None:
                try:
                    next(live[si])
                except StopIteration:
                    live[si] = None

    if _STOP == "B":
        S.finish()
        return nc
    S.barrier()
    rrl[0] = list(range(8))
    A.top = O_WU
    wu = A.bf(KC * DFF, name="wu")
    assert A.top <= O_CW, A.top
    wu3 = wu.ap.rearrange("p (c n) -> p c n", c=KC)
    for c in range(KC):
        for (a0, a1) in ((0, 1408), (1408, DFF)):
            S.dma('pool', wu3[:, c, a0:a1], wud[c * 128:(c + 1) * 128, a0:a1], writes=[wu])
    A.top = O_CW
    xr = [A.f32(D, name="xr0"), A.f32(D, name="xr1")]
    x2b = A.f32(D, name="x2b")
    hn = A.bf(D, name="hn")
    hT = A.bf(D, name="hT")
    sgt = A.bf(512, name="sgt")
    actT = A.bf(FC * 128, name="actT")
    yb = A.f32(D, name="yb")
    st = A.f32(4, name="st")
    assert A.top <= O_WD, A.top
    catTs3 = catTs.ap.rearrange("p (c b) -> p c b", c=KC)

    def tailA(x_, nt, catfn, dst):
        for n in range(2):
            pj = bank()
            mm(pj.ap[0:nt, :], [(catfn(c)[0], wo3[:, c, n * 512:(n + 1) * 512]) for c in range(KC)], [wo] + catfn(0)[1], [pj])
            tt(dst.ap[0:nt, n * 512:(n + 1) * 512], x_.ap[0:nt, n * 512:(n + 1) * 512], pj.ap[0:nt, :], ALU.add, [x_, pj], [dst])

    def tailB(x2_, nt, ydst):
        act(hn.ap[0:nt, :], x2_.ap[0:nt, :], AF.Square, [x2_], [hn, st], accum_out=st.ap[0:nt, 0:1])
        rsqrt(st.ap[0:nt, 0:1], st.ap[0:nt, 0:1], 1.0 / D, [st], [st])
        ts(hn.ap[0:nt, :], x2_.ap[0:nt, :], st.ap[0:nt, 0:1], None, ALU.mult, None, [x2_, st], [hn])
        pb_ = bank()
        for c in range(KC):
            tr(pbf(pb_)[:, c * nt:(c + 1) * nt], hn.ap[0:nt, c * 128:(c + 1) * 128], ident_b.ap[0:nt, 0:nt], [hn], [pb_], inc=(c == KC - 1))
        hTv = hT.ap[:, 0:KC * nt].rearrange("p (c t) -> p c t", c=KC)
        tt(hTv, pbf(pb_)[:, 0:KC * nt].rearrange("p (c t) -> p c t", c=KC), bc(n2.ap.unsqueeze(2), [128, KC, nt]), ALU.mult, [pb_, n2], [hT])
        aTv = actT.ap[:, 0:FC * nt].rearrange("p (f t) -> p f t", f=FC)
        for f0 in range(0, FC, 4):
            nf = min(4, FC - f0)
            pg_ = bank()
            pu_ = bank()
            for j in range(nf):
                f = f0 + j
                mm(pg_.ap[:, j * nt:(j + 1) * nt], [(wg3[:, c, f * 128:(f + 1) * 128], hTv[:, c, :]) for c in range(KC)], [wg, hT], [pg_])
                mm(pu_.ap[:, j * nt:(j + 1) * nt], [(wu3[:, c, f * 128:(f + 1) * 128], hTv[:, c, :]) for c in range(KC)], [wu, hT], [pu_])
            act(sgt.ap[:, 0:nf * nt], pg_.ap[:, 0:nf * nt], AF.Silu, [pg_], [sgt])
            tt(actT.ap[:, f0 * nt:(f0 + nf) * nt], sgt.ap[:, 0:nf * nt], pu_.ap[:, 0:nf * nt], ALU.mult, [sgt, pu_], [actT])
        for n in range(2):
            pj = bank()
            mm(pj.ap[0:nt, :], [(aTv[:, f, :], wd3[:, f, n * 512:(n + 1) * 512]) for f in range(FC)], [wd, actT], [pj])
            tt(x2_.ap[0:nt, n * 512:(n + 1) * 512], x2_.ap[0:nt, n * 512:(n + 1) * 512], pj.ap[0:nt, :], ALU.add, [x2_, pj], [x2_])
        act(hn.ap[0:nt, :], x2_.ap[0:nt, :], AF.Square, [x2_], [hn, st], accum_out=st.ap[0:nt, 1:2])
        rsqrt(st.ap[0:nt, 1:2], st.ap[0:nt, 1:2], 1.0 / D, [st], [st])
        stt(yb.ap[0:nt, :], x2_.ap[0:nt, :], st.ap[0:nt, 1:2], fnb.ap[0:nt, :], ALU.mult, ALU.mult, [x2_, st, fnb], [yb])
        S.dma('sp', ydst, yb.ap[0:nt, :], reads=[yb], is_output=True)

    for m in range(NT):
        tsl = slice(m * 128, (m + 1) * 128)
        xb = xr[m % 2]
        S.dma('sp', xb.ap, xp[tsl, :], writes=[xb])

        def catfn(c, tsl=tsl):
            if c < 4:
                return (oaT3[:, c, tsl], [oaT])
            return (obT3[:, c - 4, tsl], [obT])
        tailA(xb, 128, catfn, x2b)
        S.dma('sp', x2s[tsl, :], x2b.ap, reads=[x2b])
    S.barrier()
    A.top = O_WD
    wd = A.bf(FC * D, name="wd")
    xsb2 = A.f32(D, parts=NS, name="xsb2")
    x2sm = A.f32(D, parts=NS, name="x2sm")
    assert A.top <= 53200
    wd3 = wd.ap.rearrange("p (c n) -> p c n", c=FC)
    for c in range(FC):
        S.dma('pool', wd3[:, c, :], wdd[c * 128:(c + 1) * 128, :], writes=[wd])
    for m in range(NT):
        tsl = slice(m * 128, (m + 1) * 128)
        xb = xr[m % 2]
        S.dma('sp', xb.ap, x2s[tsl, :], writes=[xb])
        tailB(xb, 128, yp[tsl, :])
    S.dma('sp', xsb2.ap, xs[:, :], writes=[xsb2])
    tailA(xsb2, NS, lambda c: (catTs3[:, c, :], [catTs]), x2sm)
    tailB(x2sm, NS, ys[:, :])
    S.finish()
    return nc


_CACHE = {}


def _rot_tables(pos):
    half = 16
    inv = (np.float32(500000.0) ** (-(np.arange(half, dtype=np.float32)) / np.float32(half))).astype(np.float32)
    ang = pos.astype(np.float32)[:, None] * inv[None, :]
    cos = np.cos(ang).astype(np.float32)
    sin = np.sin(ang).astype(np.float32)
    tab = np.stack([np.broadcast_to(cos[:, None, :], (len(pos), 8, half)),
                    np.broadcast_to(sin[:, None, :], (len(pos), 8, half))], axis=1)
    return np.ascontiguousarray(tab.reshape(len(pos), 256), dtype=np.float32)


def kernel(x_prompt, x_sample, cache_k, cache_v, page_table, state_gdn, state_conv, norm1_w, w_in,
           conv_w, a_log, dt_bias, gdn_norm_w, w_out, norm2_w, w_gate, w_up, w_down, final_norm_w):
    f = lambda a: np.ascontiguousarray(np.asarray(a), dtype=np.float32)
    NPG = page_table.shape[1]
    PAST = NPG * 128
    if PAST not in _CACHE:
        _CACHE[PAST] = build(PAST)
    nc = _CACHE[PAST]
    ck = f(cache_k[0])
    cv = f(cache_v[0])
    shared = {
        "ck": ck, "cv": cv,
        "n1w": f(np.asarray(norm1_w[0]).reshape(KC, 128).T),
        "n2w": f(np.asarray(norm2_w[0]).reshape(KC, 128).T),
        "fnw": f(np.broadcast_to(np.asarray(final_norm_w)[None, :], (128, D))),
        "w_in": f(w_in[0]),
        "cw": f(np.asarray(conv_w[0]).T.reshape(12, 128, 4).transpose(1, 0, 2)),
        "cw4": f(np.broadcast_to(np.asarray(conv_w[0])[None], (NS, 4, GQ))),
        "alog": f(np.broadcast_to(np.asarray(a_log[0])[None, :], (128, NH))),
        "dtb": f(np.broadcast_to(np.asarray(dt_bias[0])[None, :], (128, NH))),
        "gnw": f(np.broadcast_to(np.asarray(gdn_norm_w[0])[None, :], (128, HD))),
        "wo": f(w_out[0]), "wg": f(w_gate[0]), "wu": f(w_up[0]), "wd": f(w_down[0]),
        "csp": _rot_tables(np.arange(T)),
        "css": _rot_tables(np.full((NS,), PAST)),
    }
    in_maps = []
    for c in range(NCORES):
        m = dict(shared)
        m["xp"] = f(x_prompt[c])
        m["xs"] = f(x_sample[NS * c:NS * (c + 1), 0, :])
        m["pt"] = np.ascontiguousarray(np.asarray(page_table[NS * c:NS * (c + 1)]), dtype=np.int32)
        m["sg"] = f(state_gdn[0, NS * c:NS * (c + 1)])
        m["sc"] = f(state_conv[0, NS * c:NS * (c + 1)])
        in_maps.append(m)
    res = run_bass_kernel_spmd(nc, in_maps, core_ids=list(range(NCORES)))
    R = res.results
    cat = lambda k: np.stack([np.asarray(R[c][k]) for c in range(NCORES)], axis=0)
    y_prompt = cat("yp")
    y_sample = cat("ys").reshape(NS * NCORES, 1, D)
    k_prompt = cat("kp").reshape(1, NCORES, T, NH, HD)
    v_prompt = cat("vp").reshape(1, NCORES, T, NH, HD)
    k_sample = cat("ks").reshape(1, NS * NCORES, 1, NH, HD)
    v_sample = cat("vs").reshape(1, NS * NCORES, 1, NH, HD)
    gdn_prompt = cat("gp").reshape(1, NCORES, NH, HD, HD)
    gdn_sample = cat("gs").reshape(1, NS * NCORES, NH, HD, HD)
    conv_prompt = cat("cp").reshape(1, NCORES, 3, GQ)
    conv_sample = cat("cs").reshape(1, NS * NCORES, 3, GQ)
    return (y_prompt, y_sample, k_prompt, v_prompt, k_sample, v_sample, gdn_prompt, gdn_sample, conv_prompt, conv_sample)
```

```python
import math
import os
import numpy as np
import concourse.bass as bass
import concourse.mybir as mybir
from concourse.bass_utils import run_bass_kernel_spmd

F32 = mybir.dt.float32
BF16 = mybir.dt.bfloat16
I32 = mybir.dt.int32
U32 = mybir.dt.uint32
AF = mybir.ActivationFunctionType
ALU = mybir.AluOpType
AX = mybir.AxisListType

T = 2048
NT = 16
D = 1024
KC = 8
HD = 128
NH = 4
GQ = 1536
INC = 3592
DFF = 2816
FC = 22
ZOFF, AOFF, BOFF, QBOFF, KBOFF, VBOFF = 1536, 2048, 2052, 2056, 2568, 3080
EPS = 1e-6
NEG = -30000.0
SCALE = HD ** -0.5
NS = 4
NCORES = 8
_STOP = os.environ.get("KSTOP", "")
_KNT = int(os.environ.get("KNT", "16"))
_NF32 = int(os.environ.get("KNF32", "1"))
_KSUB = int(os.environ.get("KSUB", "99"))


class Tile:
    def __init__(self, name):
        self.name = name
        self.w = None
        self.r = []
        self.dsem = None
        self.dcnt = 0
        self.excl = False


class Buf:
    def __init__(self, ap, name):
        self.ap = ap
        self.t = Tile(name)


class Sched:
    def __init__(self, nc):
        self.nc = nc
        self.eng = {'pe': nc.tensor, 'act': nc.scalar, 'dve': nc.vector, 'pool': nc.gpsimd, 'sp': nc.sync}
        self.sem = {}
        self.cnt = {}
        for k in ['pe', 'act', 'dve', 'pool']:
            self.sem[k] = nc.alloc_semaphore("sem_" + k)
            self.cnt[k] = 0
        self.waited = {k: {} for k in self.eng}
        self.out_stamps = []
        self.dtiles = []
        self.free_dsems = []
        self.ninst = 0

    def _wait(self, e, deps):
        best = {}
        for (sem, val) in deps:
            key = id(sem)
            if key not in best or best[key][1] < val:
                best[key] = (sem, val)
        for key, (sem, val) in best.items():
            if self.waited[e].get(key, 0) < val:
                self.eng[e].wait_ge(sem, val)
                self.waited[e][key] = val
                self.ninst += 1

    def _deps(self, e, reads, writes, skip_sem=None):
        deps = []
        mysem = self.sem.get(e)
        for t in reads:
            if t.w is not None:
                deps.append(t.w)
            if t.excl:
                for st in t.r:
                    if st[0] is not mysem:
                        deps.append(st)
        for t in writes:
            if t.w is not None and t.w[0] is not mysem and t.w[0] is not skip_sem:
                deps.append(t.w)
            for st in t.r:
                if st[0] is not mysem:
                    deps.append(st)
        return deps

    def op(self, e, fn, reads=(), writes=(), inc=True):
        reads = [b.t for b in reads]
        writes = [b.t for b in writes]
        self._wait(e, self._deps(e, reads, writes))
        inst = fn(self.eng[e])
        self.ninst += 1
        if inc:
            self.cnt[e] += 1
            inst.then_inc(self.sem[e], 1)
            stamp = (self.sem[e], self.cnt[e])
        else:
            stamp = (self.sem[e], self.cnt[e] + 1)
        for t in writes:
            t.w = stamp
            t.r = []
        for t in reads:
            t.r.append(stamp)
        return inst

    def dma(self, q, out, in_, reads=(), writes=(), is_output=False, indirect=None, owner=None, **kw):
        reads = [b.t for b in reads]
        writes = [b.t for b in writes]
        if owner is None:
            owner = writes[0] if writes else reads[0]
        else:
            owner = owner.t
        if owner.dsem is None:
            owner.dsem = self.nc.alloc_semaphore("d%d_%s" % (len(self.dtiles), owner.name))
            self.dtiles.append(owner)
        self._wait(q, self._deps(None, reads, writes, skip_sem=owner.dsem))
        if indirect is not None:
            inst = self.eng[q].indirect_dma_start(out=out, out_offset=None, in_=in_, in_offset=indirect)
        else:
            inst = self.eng[q].dma_start(out=out, in_=in_, **kw)
        self.ninst += 1
        owner.dcnt += 16
        inst.then_inc(owner.dsem, 16)
        stamp = (owner.dsem, owner.dcnt)
        for t in writes:
            t.w = stamp
            t.r = []
        for t in reads:
            t.r.append(stamp)
        if is_output:
            self.out_stamps.append(stamp)
        return inst

    def barrier(self):
        stamps = [(self.sem[k], self.cnt[k]) for k in self.sem if self.cnt[k] > 0]
        stamps += [(t.dsem, t.dcnt) for t in self.dtiles if t.dcnt > 0]
        for e in self.eng:
            self._wait(e, stamps)

    def finish(self):
        self._wait('sp', self.out_stamps)


class Arena:
    def __init__(self, nc, words):
        self.t = nc.alloc_sbuf_tensor("arena", [128, words], F32)
        self.top = 0
        self.words = words
        self.n = 0

    def mark(self):
        return self.top

    def reset(self, m):
        self.top = m

    def _take(self, w):
        o = self.top
        self.top += w
        assert self.top <= self.words, ("SBUF arena overflow", self.top, self.words)
        self.n += 1
        return o

    def f32(self, n, parts=128, name=None):
        o = self._take(n)
        return Buf(self.t[0:parts, o:o + n], name or ("b%d" % self.n))

    def bf(self, n, parts=128, name=None):
        w = (n + 1) // 2
        o = self._take(w)
        return Buf(self.t[0:parts, o:o + w].bitcast(BF16)[:, 0:n], name or ("b%d" % self.n))

    def i32(self, n, parts=128, name=None):
        o = self._take(n)
        return Buf(self.t[0:parts, o:o + n].bitcast(I32), name or ("b%d" % self.n))

    def u32(self, n, parts=128, name=None):
        o = self._take(n)
        return Buf(self.t[0:parts, o:o + n].bitcast(U32), name or ("b%d" % self.n))


def build(PAST):
    NPG = PAST // 128
    NB = NPG // 2
    NPHYS = NS * NCORES * NPG + -(-(NS * NCORES * NPG) // 4)
    nc = bass.Bass("TRN2", target_bir_lowering=False)

    def din(name, shape, dt=F32):
        return nc.dram_tensor(name, list(shape), dt, kind="ExternalInput").ap()

    def dout(name, shape, dt=F32):
        return nc.dram_tensor(name, list(shape), dt, kind="ExternalOutput").ap()

    xp = din("xp", [T, D])
    xs = din("xs", [NS, D])
    ck = din("ck", [NPHYS, 128, NH, HD])
    cv = din("cv", [NPHYS, 128, NH, HD])
    pt = din("pt", [NS, NPG], I32)
    sg = din("sg", [NS, NH, HD, HD])
    scd = din("sc", [NS, 3, GQ])
    n1w = din("n1w", [128, KC])
    n2w = din("n2w", [128, KC])
    fnw = din("fnw", [128, D])
    w_in = din("w_in", [D, INC])
    cwd = din("cw", [128, 12, 4])
    cw4d = din("cw4", [NS, 4, GQ])
    alogd = din("alog", [128, NH])
    dtbd = din("dtb", [128, NH])
    gnwd = din("gnw", [128, HD])
    wod = din("wo", [D, D])
    wgd = din("wg", [D, DFF])
    wud = din("wu", [D, DFF])
    wdd = din("wd", [DFF, D])
    cspd = din("csp", [T, 256])
    cssd = din("css", [NS, 256])

    yp = dout("yp", [T, D])
    ys = dout("ys", [NS, D])
    kp = dout("kp", [T, 512])
    vp = dout("vp", [T, 512])
    kso = dout("ks", [NS, 512])
    vso = dout("vs", [NS, 512])
    gp = dout("gp", [NH, HD, HD])
    gso = dout("gs", [NS * NH, HD, HD])
    cpo = dout("cp", [3, GQ])
    cso = dout("cs", [NS, 3, GQ])

    x2s = nc.dram_tensor("x2s", [T, D], F32, kind="Internal").ap()
    sqd = nc.dram_tensor("sqd", [NS, 1536], F32, kind="Internal").ap()
    soad = nc.dram_tensor("soad", [NS, 512], F32, kind="Internal").ap()

    S = Sched(nc)
    A = Arena(nc, 53200)
    PSB = []
    for i in range(8):
        PSB.append(Buf(nc.alloc_psum_tensor("ps%d" % i, [128, 512], F32)[:, :], "ps%d" % i))
        PSB[-1].t.excl = True
    rr = [0]
    rrl = [list(range(8))]

    def bank():
        b = PSB[rrl[0][rr[0] % len(rrl[0])]]
        rr[0] += 1
        return b

    def pbf(b):
        return b.ap[:, :].bitcast(BF16)

    def mm(out, pairs, reads, writes):
        n = len(pairs)
        for i, (l, r) in enumerate(pairs):
            S.op('pe', lambda e, l=l, r=r, i=i: e.matmul(out, lhsT=l, rhs=r, start=(i == 0), stop=(i == n - 1)),
                 reads=reads, writes=writes, inc=(i == n - 1))

    def tr(out, in_, ident, reads, writes, inc=True):
        S.op('pe', lambda e: e.transpose(out=out, in_=in_, identity=ident),
             reads=list(reads) + [ident_b if ident.dtype == BF16 else ident_f], writes=writes, inc=inc)

    def act(out, in_, func, reads, writes, **kw):
        S.op('act', lambda e: e.activation(out=out, in_=in_, func=func, **kw), reads=reads, writes=writes)

    def tt(out, in0, in1, op, reads, writes, e='dve'):
        S.op(e, lambda en: en.tensor_tensor(out=out, in0=in0, in1=in1, op=op), reads=reads, writes=writes)

    def ts(out, in0, s1, s2, op0, op1, reads, writes, e='dve'):
        if op1 is None:
            S.op(e, lambda en: en.tensor_scalar(out=out, in0=in0, scalar1=s1, scalar2=None, op0=op0), reads=reads, writes=writes)
        else:
            S.op(e, lambda en: en.tensor_scalar(out=out, in0=in0, scalar1=s1, scalar2=s2, op0=op0, op1=op1), reads=reads, writes=writes)

    def stt(out, in0, sc, in1, op0, op1, reads, writes):
        S.op('dve', lambda en: en.scalar_tensor_tensor(out=out, in0=in0, scalar=sc, in1=in1, op0=op0, op1=op1), reads=reads, writes=writes)

    def cp(out, in_, reads, writes, e='dve'):
        if e == 'act':
            S.op('act', lambda en: en.activation(out=out, in_=in_, func=AF.Copy), reads=reads, writes=writes)
        else:
            S.op(e, lambda en: en.tensor_copy(out=out, in_=in_), reads=reads, writes=writes)

    def red(out, in_, op, reads, writes):
        S.op('dve', lambda en: en.tensor_reduce(out=out, in_=in_, axis=AX.X, op=op), reads=reads, writes=writes)

    def mset(ap, val, writes, e='dve'):
        S.op(e, lambda en: en.memset(ap, val), writes=writes)

    def recip(ap, bufs):
        S.op('dve', lambda en: en.reciprocal(out=ap, in_=ap), reads=bufs, writes=bufs)

    def rsqrt(out, in_, scale, reads, writes):
        act(out, in_, AF.Sqrt, list(reads) + [epsb], writes, scale=scale, bias=epsb.ap[0:out.shape[0], :])
        recip(out, writes)

    def softplus(dst, tmp, src_bufs, n):
        stt(tmp, dst, -1.0, dst, ALU.mult, ALU.max, src_bufs, src_bufs)
        act(tmp, tmp, AF.Exp, src_bufs, src_bufs, scale=-1.0)
        act(tmp, tmp, AF.Ln, src_bufs + [ones_f], src_bufs, bias=ones_f.ap[0:n, 0:1])
        stt(dst, dst, 0.0, tmp, ALU.max, ALU.add, src_bufs, src_bufs)

    def bc(ap, shape):
        return ap.to_broadcast(list(shape))

    NBX = max(NB, 8)
    ident_f = A.f32(128, name="ident_f")
    ident_b = A.bf(128, name="ident_b")
    ones_f = A.f32(128, name="ones_f")
    ones_b = A.bf(128, name="ones_b")
    triU = A.f32(128, name="triU")
    maskU = A.f32(128, name="maskU")
    strictU = A.f32(128, name="strictU")
    cmask = A.f32(128, name="cmask")
    epsb = A.f32(1, name="epsb")
    negrow = A.f32(1, name="negrow")
    onehot4 = A.f32(NS * 4, name="onehot4")
    iotaNB = A.f32(NBX, parts=16, name="iotaNB")
    rowoff = A.f32(96, name="rowoff")
    pairsel = A.f32(NBX, name="pairsel")
    n1 = A.f32(KC, name="n1")
    n2 = A.f32(KC, name="n2")
    fnb = A.f32(D, name="fnb")
    cwt = A.f32(48, name="cwt")
    negA = A.f32(NH, name="negA")
    dtb = A.f32(NH, name="dtb")
    gnb = A.f32(HD, name="gnb")
    catTs = A.bf(KC * NS, name="catTs")
    assert A.top <= 2500, A.top

    def asel(buf, pattern, op, fill, base, cm):
        S.op('pool', lambda e: e.affine_select(out=buf.ap, in_=buf.ap, pattern=pattern, compare_op=op, fill=fill,
                                               base=base, channel_multiplier=cm), reads=[buf], writes=[buf])

    mset(ident_f.ap, 1.0, [ident_f], e='pool')
    asel(ident_f, [[-1, 128]], ALU.is_equal, 0.0, 0, 1)
    cp(ident_b.ap, ident_f.ap, [ident_f], [ident_b], e='pool')
    mset(ones_f.ap, 1.0, [ones_f], e='pool')
    mset(ones_b.ap, 1.0, [ones_b], e='pool')
    mset(triU.ap, 1.0, [triU], e='pool')
    asel(triU, [[1, 128]], ALU.is_ge, 0.0, 0, -1)
    mset(maskU.ap, 0.0, [maskU], e='pool')
    asel(maskU, [[1, 128]], ALU.is_ge, NEG, 0, -1)
    mset(strictU.ap, 1.0, [strictU], e='pool')
    asel(strictU, [[1, 128]], ALU.is_gt, 0.0, 0, -1)
    mset(cmask.ap, 0.0, [cmask], e='pool')
    asel(cmask, [[-1, 128]], ALU.is_ge, NEG, 0, 1)
    mset(epsb.ap, EPS, [epsb], e='pool')
    mset(negrow.ap, 0.0, [negrow], e='pool')
    asel(negrow, [[0, 1]], ALU.is_ge, NEG, 0, -1)
    mset(onehot4.ap, 1.0, [onehot4], e='pool')
    asel(onehot4, [[1, NS], [-1, 4]], ALU.is_equal, 0.0, 0, 0)
    S.op('pool', lambda e: e.iota(iotaNB.ap, pattern=[[1, NBX]], base=0, channel_multiplier=0,
                                  allow_small_or_imprecise_dtypes=True), writes=[iotaNB])
    S.op('pool', lambda e: e.iota(rowoff.ap, pattern=[[0, NS], [1, NH], [0, 6]], base=0, channel_multiplier=4,
                                  allow_small_or_imprecise_dtypes=True), writes=[rowoff])
    mset(pairsel.ap, 1.0, [pairsel], e='pool')
    asel(pairsel, [[-2, NBX]], ALU.is_ge, 0.0, 0, 1)
    asel(pairsel, [[2, NBX]], ALU.is_ge, 0.0, 1, -1)

    constb = Buf(None, "constgrp")
    cl = [(n1, n1w), (n2, n2w), (fnb, fnw), (cwt, cwd.rearrange("p c j -> p (c j)")), (negA, alogd), (dtb, dtbd), (gnb, gnwd)]
    for (b_, d_) in cl:
        S.dma('sp', b_.ap, d_, writes=[b_], owner=constb)
    for (b_, _) in cl:
        b_.t.w = (constb.t.dsem, constb.t.dcnt)
    act(negA.ap, negA.ap, AF.Exp, [negA], [negA])
    ts(negA.ap, negA.ap, -1.0, None, ALU.mult, None, [negA], [negA])

    O_WI, O_WORK, O_P2, O_OBT, O_OAT = 2500, 16900, 36100, 44900, 49000
    A.top = O_WI
    wi = A.bf(KC * INC, name="wi")
    assert A.top <= O_WORK
    wi3 = wi.ap.rearrange("p (c n) -> p c n", c=KC)
    for c in range(KC):
        for (a0, a1) in ((0, 1796), (1796, INC)):
            S.dma('pool', wi3[:, c, a0:a1], w_in[c * 128:(c + 1) * 128, a0:a1], writes=[wi])

    A.top = O_WORK
    P4 = NS
    xs_sb = A.f32(D, parts=P4, name="xs_sb")
    css = A.f32(256, parts=P4, name="css")
    prj = A.f32(INC, parts=P4, name="prj")
    cats = A.f32(D, parts=P4, name="cats")
    catb = A.bf(D, parts=P4, name="catb")
    s1 = A.f32(GQ, parts=P4, name="s1")
    s2 = A.f32(GQ, parts=P4, name="s2")
    sm = A.f32(96, parts=P4, name="sm")
    mark_sk = A.mark()
    bufA = A.f32(GQ, parts=P4, name="bufA")
    bufB = A.f32(GQ, parts=P4, name="bufB")
    qkvs = A.f32(GQ, parts=P4, name="qkvs")
    xns = A.bf(D, parts=P4, name="xns")
    xnTs = A.bf(KC * NS, name="xnTs")
    xnTs3 = xnTs.ap.rearrange("p (c b) -> p c b", c=KC)
    Ss = A.f32(16 * 128, name="Ss")
    Ss3 = Ss.ap.rearrange("p (g v) -> p g v", g=16)
    Sout = A.f32(16 * 128, name="Sout")
    Sout3 = Sout.ap.rearrange("p (g v) -> p g v", g=16)
    qkTs = A.f32(32, name="qkTs")
    qkTs4 = qkTs.ap.rearrange("p (s h b) -> p s h b", s=2, h=NH)
    kqS = A.f32(32, name="kqS")
    kqS3 = kqS.ap.rearrange("p (g s) -> p g s", g=16)
    kqtm = A.f32(1024, parts=P4, name="kqtm")
    kqtm4 = kqtm.ap.rearrange("p (s h v) -> p s h v", s=2, h=NH)
    vn = A.f32(512, parts=P4, name="vn")
    os_ = A.f32(512, parts=P4, name="os_")
    knm = A.f32(NS * 512, parts=P4, name="knm")
    Rg = A.f32(16, parts=P4, name="Rg")
    egbc = A.f32(16, name="egbc")

    S.dma('sp', xs_sb.ap, xs[:, :], writes=[xs_sb])
    S.dma('sp', css.ap, cssd[:, :], writes=[css])
    S.dma('sp', Ss3, sg.rearrange("b h k v -> k (b h) v"), writes=[Ss])
    ddum = Buf(None, "ddum")
    S.dma('sp', cso[:, 0:2, :], scd[:, 1:3, :], owner=ddum, is_output=True)
    act(s1.ap[:, 0:D], xs_sb.ap, AF.Square, [xs_sb], [s1, sm], accum_out=sm.ap[:, 0:1])
    rsqrt(sm.ap[:, 0:1], sm.ap[:, 0:1], 1.0 / D, [sm], [sm])
    ts(xns.ap, xs_sb.ap, sm.ap[:, 0:1], None, ALU.mult, None, [xs_sb, sm], [xns])
    pb = bank()
    for c in range(KC):
        tr(pbf(pb)[:, c * NS:(c + 1) * NS], xns.ap[:, c * 128:(c + 1) * 128], ident_b.ap[0:P4, 0:P4], [xns], [pb], inc=(c == KC - 1))
    tt(xnTs3, pbf(pb)[:, 0:KC * NS].rearrange("p (c b) -> p c b", c=KC), bc(n1.ap.unsqueeze(2), [128, KC, NS]), ALU.mult, [pb, n1], [xnTs])
    for j in range(8):
        c0 = j * 512
        w = min(512, INC - c0)
        pj = bank()
        mm(pj.ap[0:P4, 0:w], [(xnTs3[:, c, :], wi3[:, c, c0:c0 + w]) for c in range(KC)], [wi, xnTs], [pj])
        cp(prj.ap[:, c0:c0 + w], pj.ap[0:P4, 0:w], [pj], [prj], e=('act' if j % 2 == 0 else 'dve'))
    S.dma('pool', cso[:, 2, :], prj.ap[:, 0:GQ], reads=[prj], is_output=True)
    S.dma('sp', bufB.ap, cw4d[:, 3, :], writes=[bufB])
    tt(s1.ap, prj.ap[:, 0:GQ], bufB.ap, ALU.mult, [prj, bufB], [s1])
    for j in range(3):
        S.dma('sp', bufA.ap, scd[:, j, :], writes=[bufA])
        S.dma('sp', bufB.ap, cw4d[:, j, :], writes=[bufB])
        tt(s2.ap, bufA.ap, bufB.ap, ALU.mult, [bufA, bufB], [s2])
        tt(s1.ap, s1.ap, s2.ap, ALU.add, [s1, s2], [s1])
    act(qkvs.ap, s1.ap, AF.Silu, [s1], [qkvs])
    tt(s2.ap[:, 0:1024], qkvs.ap[:, 0:1024], qkvs.ap[:, 0:1024], ALU.mult, [qkvs], [s2])
    red(sm.ap[:, 8:16], s2.ap[:, 0:1024].rearrange("p (g d) -> p g d", g=8), ALU.add, [s2], [sm])
    rsqrt(sm.ap[:, 8:16], sm.ap[:, 8:16], 1.0, [sm], [sm])
    ts(sm.ap[:, 8:12], sm.ap[:, 8:12], SCALE, None, ALU.mult, None, [sm], [sm])
    qk8 = qkvs.ap[:, 0:1024].rearrange("p (g d) -> p g d", g=8)
    tt(qk8, qk8, bc(sm.ap[:, 8:16].unsqueeze(2), [P4, 8, 128]), ALU.mult, [qkvs, sm], [qkvs])
    qs3 = qkvs.ap[:, 0:512].rearrange("p (h d) -> p h d", h=NH)
    ks3 = qkvs.ap[:, 512:1024].rearrange("p (h d) -> p h d", h=NH)
    vs3 = qkvs.ap[:, 1024:1536].rearrange("p (h d) -> p h d", h=NH)
    sg0, sg1, sgg, sbeta, seg, sqk = (sm.ap[:, 16:20], sm.ap[:, 20:24], sm.ap[:, 24:28], sm.ap[:, 28:32],
                                      sm.ap[:, 32:36], sm.ap[:, 36:40])
    tt(sg0, prj.ap[:, AOFF:AOFF + 4], dtb.ap[0:P4, :], ALU.add, [prj, dtb], [sm])
    softplus(sg0, sg1, [sm], P4)
    tt(sgg, sg0, negA.ap[0:P4, :], ALU.mult, [sm, negA], [sm])
    act(sbeta, prj.ap[:, BOFF:BOFF + 4], AF.Sigmoid, [prj], [sm])
    act(seg, sgg, AF.Exp, [sm], [sm])
    pq = bank()
    for s_, src in enumerate((ks3, qs3)):
        for h in range(NH):
            tr(pq.ap[:, (s_ * NH + h) * NS:(s_ * NH + h + 1) * NS], src[:, h, :], ident_f.ap[0:P4, 0:P4], [qkvs], [pq],
               inc=(s_ == 1 and h == NH - 1))
    cp(qkTs.ap, pq.ap[:, 0:32], [pq], [qkTs])
    pq = bank()
    for b in range(NS):
        for h in range(NH):
            g = b * NH + h
            mm(pq.ap[:, g * 2:g * 2 + 2], [(Ss3[:, g, :], qkTs4[:, :, h, b])], [Ss, qkTs], [pq])
    cp(kqS.ap, pq.ap[:, 0:32], [pq], [kqS])
    kqS4 = kqS.ap.rearrange("p (b h s) -> p b h s", b=NS, h=NH)
    pk2 = [bank(), bank()]
    for s_ in range(2):
        for h in range(NH):
            tr(pk2[s_].ap[0:P4, h * 128:(h + 1) * 128], kqS4[:, :, h, s_], ident_f.ap, [kqS], [pk2[s_]], inc=(h == NH - 1))
        cp(kqtm.ap[:, s_ * 512:(s_ + 1) * 512], pk2[s_].ap[0:P4, :], [pk2[s_]], [kqtm], e=('act' if s_ == 0 else 'dve'))
    kS, qS = kqtm4[:, 0], kqtm4[:, 1]
    vn3 = vn.ap.rearrange("p (h v) -> p h v", h=NH)
    os3 = os_.ap.rearrange("p (h v) -> p h v", h=NH)

    def b4(ap):
        return bc(ap.unsqueeze(2), [P4, NH, 128])
    tt(vn3, kS, b4(seg), ALU.mult, [kqtm, sm], [vn])
    tt(vn3, vs3, vn3, ALU.subtract, [qkvs, vn], [vn])
    tt(vn3, vn3, b4(sbeta), ALU.mult, [vn, sm], [vn])
    s23 = s2.ap[:, 0:512].rearrange("p (h d) -> p h d", h=NH)
    tt(s23, qs3, ks3, ALU.mult, [qkvs], [s2])
    red(sqk, s23, ALU.add, [s2], [sm])
    tt(os3, qS, b4(seg), ALU.mult, [kqtm, sm], [os_])
    tt(s23, vn3, b4(sqk), ALU.mult, [vn, sm], [s2])
    tt(os_.ap, os_.ap, s2.ap[:, 0:512], ALU.add, [os_, s2], [os_])
    R3 = Rg.ap.rearrange("p (b h) -> p b h", b=NS)
    oh = onehot4.ap.rearrange("p (b c) -> p b c", b=NS)
    for b in range(NS):
        ts(R3[:, b, :], seg, ident_f.ap[0:P4, b:b + 1], None, ALU.mult, None, [sm, ident_f], [Rg])
        ts(knm.ap[:, b * 512:(b + 1) * 512], qkvs.ap[:, 512:1024], ident_f.ap[0:P4, b:b + 1], None, ALU.mult, None, [qkvs, ident_f], [knm])
    pq = bank()
    mm(pq.ap[:, 0:16], [(ones_f.ap[0:P4, :], Rg.ap)], [ones_f, Rg], [pq])
    cp(egbc.ap, pq.ap[:, 0:16], [pq], [egbc])
    for b in range(NS):
        pq = bank()
        for h in range(NH):
            mm(pq.ap[:, h * 128:(h + 1) * 128], [(knm.ap[:, b * 512 + h * 128:b * 512 + (h + 1) * 128], vn3[:, h, :])], [knm, vn], [pq])
        for h in range(NH):
            g = b * NH + h
            stt(Sout3[:, g, :], Ss3[:, g, :], egbc.ap[:, g:g + 1], pq.ap[:, h * 128:(h + 1) * 128], ALU.mult, ALU.add, [Ss, egbc, pq], [Sout])
    S.dma('pool', gso.rearrange("g k v -> k g v"), Sout3, reads=[Sout], is_output=True)
    tt(s23, os3, os3, ALU.mult, [os_], [s2])
    red(sm.ap[:, 40:44], s23, ALU.add, [s2], [sm])
    rsqrt(sm.ap[:, 40:44], sm.ap[:, 40:44], 1.0 / HD, [sm], [sm])
    tt(os3, os3, b4(sm.ap[:, 40:44]), ALU.mult, [os_, sm], [os_])
    tt(os3, os3, bc(gnb.ap[0:P4, :].unsqueeze(1), [P4, NH, 128]), ALU.mult, [os_, gnb], [os_])
    act(s2.ap[:, 0:512], prj.ap[:, ZOFF:ZOFF + 512], AF.Silu, [prj], [s2])
    tt(cats.ap[:, 0:512], os_.ap, s2.ap[:, 0:512], ALU.mult, [os_, s2], [cats])
    qb8 = prj.ap[:, QBOFF:QBOFF + 1024].rearrange("p (g d) -> p g d", g=8)
    css4 = css.ap.rearrange("p (s g f) -> p s g f", s=2, g=8)
    rts = s1.ap[:, 0:512].rearrange("p (k g f) -> p k g f", k=4, g=8)
    x1, x2 = qb8[:, :, 0:16], qb8[:, :, 16:32]
    tt(rts[:, 0], x1, css4[:, 0], ALU.mult, [prj, css], [s1])
    tt(rts[:, 1], x2, css4[:, 1], ALU.mult, [prj, css], [s1])
    tt(rts[:, 2], x2, css4[:, 0], ALU.mult, [prj, css], [s1])
    tt(rts[:, 3], x1, css4[:, 1], ALU.mult, [prj, css], [s1])
    tt(x1, rts[:, 0], rts[:, 1], ALU.subtract, [s1], [prj])
    tt(x2, rts[:, 2], rts[:, 3], ALU.add, [s1], [prj])
    S.dma('pool', kso[:, :], prj.ap[:, KBOFF:KBOFF + 512], reads=[prj], is_output=True)
    S.dma('pool', vso[:, :], prj.ap[:, VBOFF:VBOFF + 512], reads=[prj], is_output=True)

    if _STOP == "Sa":
        S.finish()
        return nc
    S.dma('sp', sqd[:, :], prj.ap[:, QBOFF:QBOFF + 1536], reads=[prj])
    S.dma('sp', soad[:, :], cats.ap[:, 0:512], reads=[cats])
    S.barrier()
    GR = 1
    NCH = 128 // GR
    O_PSI = 34810
    A.top = O_PSI
    ptc = A.i32(NS, name="ptc")
    ptcf = A.f32(NS, name="ptcf")
    iotaC = A.f32(NCH, name="iotaC")
    idxgf = A.f32(NS * NCH, name="idxgf")
    idxg = A.i32(NS * NCH, name="idxg")
    assert A.top <= 36100, A.top
    A.top = 44900
    acc = A.f32(NS * 512, name="acc")
    Gb = [A.f32(512, name="G%d" % i_) for i_ in range(4)]
    assert A.top <= 49000, A.top
    with nc.allow_non_contiguous_dma(reason="tiny page-table transpose"):
        S.dma('sp', ptc.ap[0:NPG, :], pt.rearrange("b n -> n b"), writes=[ptc])
    ckp = ck.rearrange("n (c r) h d -> (n c) (r h d)", r=GR)
    S.op('pool', lambda e: e.iota(iotaC.ap, pattern=[[1, NCH]], base=0, channel_multiplier=0,
                                  allow_small_or_imprecise_dtypes=True), writes=[iotaC])
    cp(ptcf.ap[0:NPG, :], ptc.ap[0:NPG, :], [ptc], [ptcf])
    stt(idxgf.ap[0:NPG, :].rearrange("p (b c) -> p b c", b=NS), bc(ptcf.ap[0:NPG, :].unsqueeze(2), [NPG, NS, NCH]), float(NCH),
        bc(iotaC.ap[0:NPG, :].unsqueeze(1), [NPG, NS, NCH]), ALU.mult, ALU.add, [ptcf, iotaC], [idxgf])
    ts(idxgf.ap[0:NPG, :], idxgf.ap[0:NPG, :], 0.0, float(NPHYS * NCH - 1), ALU.max, ALU.min, [idxgf], [idxgf])
    cp(idxg.ap[0:NPG, :], idxgf.ap[0:NPG, :], [idxgf], [idxg])
    mset(acc.ap, 0.0, [acc])
    acc3 = acc.ap.rearrange("p (b c) -> p b c", b=NS)
    ps_state = [0]

    ps_issued = [0]
    PS_TOT = NS * NCH

    def pagesum_chunks(n):
        for _ in range(n):
            k = ps_state[0]
            if k >= PS_TOT:
                return
            while ps_issued[0] < min(PS_TOT, k + 4):
                j_ = ps_issued[0]
                ps_issued[0] += 1
                Gj = Gb[j_ % 4]
                S.dma('pool', Gj.ap[0:NPG, :], ckp[:, :], writes=[Gj], reads=[idxg],
                      indirect=bass.IndirectOffsetOnAxis(ap=idxg.ap[0:NPG, j_:j_ + 1].bitcast(U32), axis=0))
            ps_state[0] += 1
            b = k // NCH
            G = Gb[k % 4]
            tt(acc3[0:NPG, b, :], acc3[0:NPG, b, :], G.ap[0:NPG, :], ALU.add, [acc, G], [acc], e='pool')

    if _STOP == "Sb":
        S.finish()
        return nc
    S.barrier()
    A.top = O_OAT
    oaT = A.bf(NH * T, name="oaT")
    A.top = O_OBT
    obT = A.bf(NH * T, name="obT")
    A.top = O_P2
    qbT = A.bf(NH * T, name="qbT")
    kbT = A.bf(NH * T, name="kbT")
    sel30 = A.f32(NT * 32, name="sel30")
    assert A.top <= O_OBT
    oaT3 = oaT.ap.rearrange("p (h t) -> p h t", h=NH)
    obT3 = obT.ap.rearrange("p (h t) -> p h t", h=NH)
    qbT3 = qbT.ap.rearrange("p (h t) -> p h t", h=NH)
    kbT3 = kbT.ap.rearrange("p (h t) -> p h t", h=NH)
    A.top = O_WORK
    ss = A.f32(1, name="ss")
    rstd = A.f32(1, name="rstd")
    xn = A.bf(D, name="xn")
    xnT = A.bf(D, name="xnT")
    xnT3 = xnT.ap.rearrange("p (c t) -> p c t", c=KC)
    raw = A.f32(12 * 131, name="raw")
    raw3 = raw.ap.rearrange("p (c t) -> p c t", c=12)
    cacc = A.f32(GQ, name="cacc")
    scr = A.f32(GQ, name="scr")
    xt = scr
    zs = A.bf(512, name="zs")
    tb = A.f32(GQ, name="tb")
    cst = A.f32(256, name="cst")
    absb = A.f32(8, name="absb")
    qkb = A.bf(1024, name="qkb")
    kmT = A.f32(NH * 8, name="kmT")
    kmTb = A.bf(NH * 8, name="kmTb")
    gsb = A.f32(32, name="gsb")
    top8 = A.f32(8, name="top8")
    gsm = A.f32(64, name="gsm")
    qkv = tb
    ssqk = A.f32(8, name="ssqk")
    ctab = A.f32(7 * 4, name="ctab")
    varb = A.bf(3 * 512, name="varb")
    varf = A.f32(4 * 512, name="varf")
    varTb = A.bf(3 * 512, name="varTb")
    qdTf = A.f32(512, name="qdTf")
    varb4 = varb.ap.rearrange("p (v h d) -> p v h d", v=3, h=NH)
    varf4 = varf.ap.rearrange("p (v h d) -> p v h d", v=4, h=NH)
    varTb4 = varTb.ap.rearrange("p (v h t) -> p v h t", v=3, h=NH)
    qdT3 = qdTf.ap.rearrange("p (h t) -> p h t", h=NH)
    qkT = A.f32(NH * 128, name="qkT")
    qkT3 = qkT.ap.rearrange("p (h t) -> p h t", h=NH)
    _mk = A.f32 if _NF32 else A.bf
    hsl = []
    for si in range(2):
        hsl.append({
            'dg': A.f32(128, name="dg%d" % si), 'decT': A.f32(128, name="decT%d" % si), 'decS': A.f32(128, name="decS%d" % si),
            'NTf': A.f32(128, name="NTf%d" % si),
            'Mm': [_mk(128, name="Mm0_%d" % si), _mk(128, name="Mm1_%d" % si)],
            'MT': [_mk(128, name="MT0_%d" % si), _mk(128, name="MT1_%d" % si)],
            'Qb': [_mk(128, name="Qb0_%d" % si), _mk(128, name="Qb1_%d" % si)],
            'banks': [PSB[3 * si + j_] for j_ in range(3)],
        })
    uu = A.f32(NH * 128, name="uu")
    wT = A.f32(NH * 128, name="wT")
    wT3 = wT.ap.rearrange("p (h t) -> p h t", h=NH)
    Sst = A.f32(NH * 128, name="Sst")
    vnew = A.f32(NH * 128, name="vnew")
    oss = A.f32(4, name="oss")
    print("phaseA top", A.top, O_P2)
    assert A.top <= O_P2, A.top

    mset(raw.ap, 0.0, [raw])
    mset(Sst.ap, 0.0, [Sst])
    mset(kmT.ap, 0.0, [kmT])
    mset(kmTb.ap, 0.0, [kmTb])
    mset(sel30.ap, 0.0, [sel30])
    cwt3 = cwt.ap.rearrange("p (c j) -> p c j", c=12)
    sel304 = sel30.ap.rearrange("p (m h n) -> p m h n", m=NT, h=NH)
    kmT3 = kmT.ap.rearrange("p (h n) -> p h n", h=NH)
    kmTb3 = kmTb.ap.rearrange("p (h n) -> p h n", h=NH)
    gsb3 = gsb.ap.rearrange("p (h n) -> p h n", h=NH)
    rrl[0] = [0, 1, 2, 3, 4, 5]
    pu, pw = PSB[6], PSB[7]
    c12 = lambda b_: b_.ap.rearrange("p (c t) -> p c t", c=12)

    for m in range(min(NT, _KNT)):
        tsl = slice(m * 128, (m + 1) * 128)
        xv = scr.ap[:, 0:D]
        S.dma('sp', xv, xp[tsl, :], writes=[scr])
        S.dma('sp', cst.ap, cspd[tsl, :], writes=[cst])
        act(xn.ap, xv, AF.Square, [scr], [xn, ss], accum_out=ss.ap)
        rsqrt(rstd.ap, ss.ap, 1.0 / D, [ss], [rstd])
        ts(xn.ap, xv, rstd.ap, None, ALU.mult, None, [scr, rstd], [xn])
        pb = bank()
        for c in range(KC):
            tr(pbf(pb)[:, c * 128:(c + 1) * 128], xn.ap[:, c * 128:(c + 1) * 128], ident_b.ap, [xn], [pb], inc=(c == KC - 1))
        tt(xnT3, pbf(pb).rearrange("p (c t) -> p c t", c=KC), bc(n1.ap.unsqueeze(2), [128, KC, 128]), ALU.mult, [pb, n1], [xnT])
        pagesum_chunks((NS * NCH + NT - 1) // NT)
        for g in range(3):
            pq = bank()
            for j in range(4):
                cc = g * 4 + j
                mm(pq.ap[:, j * 128:(j + 1) * 128], [(wi3[:, c, cc * 128:(cc + 1) * 128], xnT3[:, c, :]) for c in range(KC)], [wi, xnT], [pq])
            cp(raw3[:, g * 4:(g + 1) * 4, 3:131], pq.ap.rearrange("p (c t) -> p c t", c=4), [pq], [raw], e='act')
        tt(c12(cacc), raw3[:, :, 3:131], bc(cwt3[:, :, 3:4], [128, 12, 128]), ALU.mult, [raw, cwt], [cacc])
        for j in range(3):
            tt(c12(scr), raw3[:, :, j:j + 128], bc(cwt3[:, :, j:j + 1], [128, 12, 128]), ALU.mult, [raw, cwt], [scr])
            tt(cacc.ap, cacc.ap, scr.ap, ALU.add, [cacc, scr], [cacc])
        cp(raw3[:, :, 0:3], raw3[:, :, 128:131], [raw], [raw])
        pz = bank()
        mm(pz.ap, [(xnT3[:, c, :], wi3[:, c, ZOFF:ZOFF + 512]) for c in range(KC)], [wi, xnT], [pz])
        act(zs.ap, pz.ap, AF.Silu, [pz], [zs])
        for j, off in enumerate((QBOFF, KBOFF, VBOFF)):
            pj = bank()
            mm(pj.ap, [(xnT3[:, c, :], wi3[:, c, off:off + 512]) for c in range(KC)], [wi, xnT], [pj])
            cp(tb.ap[:, j * 512:(j + 1) * 512], pj.ap, [pj], [tb], e='act')
        pa = bank()
        mm(pa.ap[:, 0:8], [(xnT3[:, c, :], wi3[:, c, AOFF:AOFF + 8]) for c in range(KC)], [wi, xnT], [pa])
        cp(absb.ap, pa.ap[:, 0:8], [pa], [absb], e='act')
        if _KSUB <= 1:
            continue
        act(cacc.ap, cacc.ap, AF.Silu, [cacc], [cacc])
        if _KSUB <= 2:
            continue
        tb3 = tb.ap[:, 0:1024].rearrange("p (g d) -> p g d", g=8)
        cs_m = cst.ap.rearrange("p (s g f) -> p s g f", s=2, g=8)
        rt3 = scr.ap[:, 0:512].rearrange("p (k g f) -> p k g f", k=4, g=8)
        x1, x2 = tb3[:, :, 0:16], tb3[:, :, 16:32]
        tt(rt3[:, 0], x1, cs_m[:, 0], ALU.mult, [tb, cst], [scr])
        tt(rt3[:, 1], x2, cs_m[:, 1], ALU.mult, [tb, cst], [scr])
        tt(rt3[:, 2], x2, cs_m[:, 0], ALU.mult, [tb, cst], [scr])
        tt(rt3[:, 3], x1, cs_m[:, 1], ALU.mult, [tb, cst], [scr])
        tt(x1, rt3[:, 0], rt3[:, 1], ALU.subtract, [scr], [tb])
        tt(x2, rt3[:, 2], rt3[:, 3], ALU.add, [scr], [tb])
        S.dma('sp', kp[tsl, :], tb.ap[:, 512:1024], reads=[tb], is_output=True)
        S.dma('sp', vp[tsl, :], tb.ap[:, 1024:1536], reads=[tb], is_output=True)
        cp(qkb.ap, tb.ap[:, 0:1024], [tb], [qkb])
        pt_ = bank()
        for g in range(8):
            tr(pbf(pt_)[:, g * 128:(g + 1) * 128], qkb.ap[:, g * 128:(g + 1) * 128], ident_b.ap, [qkb], [pt_], inc=(g == 7))
        ptv = pbf(pt_).rearrange("p (g t) -> p g t", g=8)
        cp(qbT3[:, :, tsl], ptv[:, 0:4, :], [pt_], [qbT])
        cp(kbT3[:, :, tsl], ptv[:, 4:8, :], [pt_], [kbT], e='act')
        if _KSUB <= 3:
            continue
        nbk = m // 2
        pk = bank()
        for h in range(NH):
            mm(pk.ap[:, h:h + 1], [(qkb.ap[:, 512 + h * 128:512 + (h + 1) * 128], ones_b.ap[:, 0:1])], [qkb, ones_b], [pk])
        if m % 2 == 0:
            ts(kmT3[:, :, nbk], pk.ap[:, 0:4], 1.0 / 256, None, ALU.mult, None, [pk], [kmT])
        else:
            stt(kmT3[:, :, nbk], pk.ap[:, 0:4], 1.0 / 256, kmT3[:, :, nbk], ALU.mult, ALU.add, [pk, kmT], [kmT])
        if nbk >= 1:
            pg = bank()
            for h in range(NH):
                mm(pg.ap[:, h * 8:(h + 1) * 8], [(qbT3[:, h, tsl], kmTb3[:, h, :])], [qbT, kmTb], [pg])
            cp(gsb.ap, pg.ap[:, 0:32], [pg], [gsb])
            if nbk < 8:
                mset(gsb3[:, :, nbk:8], NEG, [gsb])
            for h in range(NH):
                S.op('dve', lambda e, h=h: e.max(out=top8.ap, in_=gsb3[:, h, :]), reads=[gsb], writes=[top8])
                ts(sel304[:, m, h, :], gsb3[:, h, :], top8.ap[:, 2:3], -NEG, ALU.is_ge, ALU.mult, [gsb, top8], [sel30])
        if m % 2 == 1:
            cp(kmTb.ap, kmT.ap, [kmT], [kmTb])
        if _KSUB <= 4:
            continue
        a_, b_ = absb.ap[:, 0:4], absb.ap[:, 4:8]
        g0 = gsm.ap[:, 0:4]
        g1 = gsm.ap[:, 4:8]
        gg = gsm.ap[:, 8:12]
        beta = gsm.ap[:, 12:16]
        gcl = gsm.ap[:, 16:24]
        eg = gsm.ap[:, 24:28]
        egl = gsm.ap[:, 28:32]
        eglast = gsm.ap[:, 32:36]
        tt(g0, a_, dtb.ap, ALU.add, [absb, dtb], [gsm])
        softplus(g0, g1, [gsm], 128)
        tt(gg, g0, negA.ap, ALU.mult, [gsm, negA], [gsm])
        act(beta, b_, AF.Sigmoid, [absb], [gsm])
        pgc = bank()
        mm(pgc.ap[:, 0:4], [(triU.ap, gg)], [triU, gsm], [pgc])
        mm(pgc.ap[:, 4:8], [(ones_f.ap, gg)], [ones_f, gsm], [pgc])
        cp(gcl, pgc.ap[:, 0:8], [pgc], [gsm])
        gcum, glast = gcl[:, 0:4], gcl[:, 4:8]
        act(eg, gcum, AF.Exp, [gsm], [gsm])
        tt(egl, glast, gcum, ALU.subtract, [gsm], [gsm])
        act(egl, egl, AF.Exp, [gsm], [gsm])
        act(eglast, glast, AF.Exp, [gsm], [gsm])
        cacc3 = c12(cacc)
        for g in range(3):
            pq = bank()
            for j in range(4):
                tr(pq.ap[:, j * 128:(j + 1) * 128], cacc3[:, g * 4 + j, :], ident_f.ap, [cacc], [pq], inc=(j == 3))
            cp(qkv.ap[:, g * 512:(g + 1) * 512], pq.ap, [pq], [qkv], e=('act' if g % 2 == 0 else 'dve'))
        if m == NT - 1:
            for j in range(3):
                pc = bank()
                mm(pc.ap[0:3, :], [(xnT3[:, c, 125:128], wi3[:, c, j * 512:(j + 1) * 512]) for c in range(KC)], [wi, xnT], [pc])
                cp(cacc.ap[0:3, j * 512:(j + 1) * 512], pc.ap[0:3, :], [pc], [cacc])
            S.dma('sp', cpo[:, :], cacc.ap[0:3, :], reads=[cacc], is_output=True)
        sqb = scr.ap[:, 0:1024]
        tt(sqb, qkv.ap[:, 0:1024], qkv.ap[:, 0:1024], ALU.mult, [qkv], [scr])
        red(ssqk.ap, sqb.rearrange("p (g d) -> p g d", g=8), ALU.add, [scr], [ssqk])
        rsqrt(ssqk.ap, ssqk.ap, 1.0, [ssqk], [ssqk])
        rq, rk = ssqk.ap[:, 0:4], ssqk.ap[:, 4:8]
        ct = ctab.ap.rearrange("p (v h) -> p v h", v=7)
        ts(ct[:, 0], rq, SCALE, None, ALU.mult, None, [ssqk], [ctab])
        tt(ct[:, 1], ct[:, 0], eg, ALU.mult, [ctab, gsm], [ctab])
        cp(ct[:, 2], rk, [ssqk], [ctab])
        tt(ct[:, 3], rk, beta, ALU.mult, [ssqk, gsm], [ctab])
        tt(ct[:, 4], ct[:, 3], eg, ALU.mult, [ctab, gsm], [ctab])
        tt(ct[:, 5], rk, egl, ALU.mult, [ssqk, gsm], [ctab])
        cp(ct[:, 6], beta, [gsm], [ctab])
        q3 = qkv.ap[:, 0:512].rearrange("p (h d) -> p h d", h=NH)
        k3 = qkv.ap[:, 512:1024].rearrange("p (h d) -> p h d", h=NH)
        v3 = qkv.ap[:, 1024:1536].rearrange("p (h d) -> p h d", h=NH)

        def cb(i):
            return bc(ct[:, i].unsqueeze(2), [128, NH, 128])
        tt(varb4[:, 0], q3, cb(0), ALU.mult, [qkv, ctab], [varb])
        tt(varb4[:, 1], k3, cb(2), ALU.mult, [qkv, ctab], [varb])
        tt(varb4[:, 2], k3, cb(3), ALU.mult, [qkv, ctab], [varb])
        tt(varf4[:, 0], q3, cb(1), ALU.mult, [qkv, ctab], [varf])
        tt(varf4[:, 1], k3, cb(4), ALU.mult, [qkv, ctab], [varf])
        tt(varf4[:, 2], k3, cb(5), ALU.mult, [qkv, ctab], [varf])
        tt(varf4[:, 3], v3, cb(6), ALU.mult, [qkv, ctab], [varf])
        for g in range(2):
            pq = bank()
            n8 = 8 if g == 0 else 4
            for j in range(n8):
                v_, h = (g * 8 + j) // 4, (g * 8 + j) % 4
                tr(pbf(pq)[:, j * 128:(j + 1) * 128], varb4[:, v_, h, :], ident_b.ap, [varb], [pq], inc=(j == n8 - 1))
            cp(varTb.ap[:, g * 1024:g * 1024 + n8 * 128], pbf(pq)[:, 0:n8 * 128], [pq], [varTb], e=('act' if g == 0 else 'dve'))
        pq = bank()
        for h in range(NH):
            tr(pq.ap[:, h * 128:(h + 1) * 128], varf4[:, 0, h, :], ident_f.ap, [varf], [pq], inc=(h == NH - 1))
        cp(qdTf.ap, pq.ap, [pq], [qdTf], e='act')
        qnT, knT, kbTv = varTb4[:, 0], varTb4[:, 1], varTb4[:, 2]
        kbg, kdv, vbv = varf4[:, 1], varf4[:, 2], varf4[:, 3]
        if _KSUB <= 6:
            continue
        def chain(h, sl):
            dg, decT, decS, NTf, Mm, MT, Qb = sl['dg'], sl['decT'], sl['decS'], sl['NTf'], sl['Mm'], sl['MT'], sl['Qb']
            pj = sl['banks'][0]
            pns = sl['banks'][1:3]
            ts(dg.ap, ident_f.ap, gcum[:, h:h + 1], None, ALU.mult, None, [ident_f, gsm], [dg])
            mm(pj.ap[:, 0:128], [(ones_f.ap, dg.ap)], [ones_f, dg], [pj])
            yield
            stt(dg.ap, pj.ap[:, 0:128], gcum[:, h:h + 1], maskU.ap, ALU.subtract, ALU.min, [pj, gsm, maskU], [dg])
            act(decT.ap, dg.ap, AF.Exp, [dg], [decT])
            tt(decS.ap, decT.ap, strictU.ap, ALU.mult, [decT, strictU], [decS])
            mm(pj.ap[:, 128:256], [(knT[:, h, :], qnT[:, h, :])], [varTb], [pj])
            mm(pj.ap[:, 256:384], [(knT[:, h, :], kbTv[:, h, :])], [varTb], [pj])
            yield
            tt(qkT3[:, h, :], pj.ap[:, 128:256], decT.ap, ALU.mult, [pj, decT], [qkT])
            stt(NTf.ap, pj.ap[:, 256:384], -1.0, decS.ap, ALU.mult, ALU.mult, [pj, decS], [NTf])
            cp(MT[0].ap, NTf.ap, [NTf], [MT[0]])
            pn = pns[0]
            tr(pn.ap[:, 0:128], NTf.ap, ident_f.ap, [NTf], [pn])
            yield
            cp(Mm[0].ap, pn.ap[:, 0:128], [pn], [Mm[0]], e='act')
            tt(NTf.ap, NTf.ap, ident_f.ap, ALU.add, [NTf, ident_f], [NTf])
            cp(Qb[0].ap, NTf.ap, [NTf], [Qb[0]])
            yield
            cur = 0
            for lv in range(6):
                nx = 1 - cur
                pn = pns[(lv + 1) % 2]
                mm(pn.ap[:, 0:128], [(MT[cur].ap, Mm[cur].ap)], [MT[cur], Mm[cur]], [pn])
                mm(pn.ap[:, 128:256], [(Mm[cur].ap, MT[cur].ap)], [MT[cur], Mm[cur]], [pn])
                yield
                cp(Mm[nx].ap, pn.ap[:, 0:128], [pn], [Mm[nx]], e='act')
                if lv < 5:
                    cp(MT[nx].ap, pn.ap[:, 128:256], [pn], [MT[nx]])
                yield
                mm(pn.ap[:, 256:384], [(Mm[nx].ap, Qb[cur].ap)], [Mm[nx], Qb[cur]], [pn])
                yield
                tt(NTf.ap, NTf.ap, pn.ap[:, 256:384], ALU.add, [NTf, pn], [NTf])
                if lv < 5:
                    cp(Qb[nx].ap, NTf.ap, [NTf], [Qb[nx]])
                yield
                cur = nx
            mm(pu.ap[:, h * 128:(h + 1) * 128], [(NTf.ap, vbv[:, h, :])], [NTf, varf], [pu])
            mm(pw.ap[:, h * 128:(h + 1) * 128], [(kbg[:, h, :], NTf.ap)], [NTf, varf], [pw])

        for hp in ((0, 1), (2, 3)):
            gens = [chain(hp[0], hsl[0]), chain(hp[1], hsl[1])]
            alive = [True, True]
            while any(alive):
                for gi_ in range(2):
                    if alive[gi_]:
                        try:
                            next(gens[gi_])
                        except StopIteration:
                            alive[gi_] = False
        cp(uu.ap, pu.ap, [pu], [uu], e='act')
        cp(wT.ap, pw.ap, [pw], [wT])
        if _KSUB <= 7:
            continue
        pv = bank()
        for h in range(NH):
            mm(pv.ap[:, h * 128:(h + 1) * 128], [(wT3[:, h, :], Sst.ap[:, h * 128:(h + 1) * 128])], [wT, Sst], [pv])
        tt(vnew.ap, uu.ap, pv.ap, ALU.subtract, [uu, pv], [vnew])
        po = bank()
        for h in range(NH):
            mm(po.ap[:, h * 128:(h + 1) * 128], [(qdT3[:, h, :], Sst.ap[:, h * 128:(h + 1) * 128]),
                                                 (qkT3[:, h, :], vnew.ap[:, h * 128:(h + 1) * 128])], [qdTf, Sst, qkT, vnew], [po])
        psn = bank()
        for h in range(NH):
            mm(psn.ap[:, h * 128:(h + 1) * 128], [(kdv[:, h, :], vnew.ap[:, h * 128:(h + 1) * 128])], [varf, vnew], [psn])
        Sst3 = Sst.ap.rearrange("p (h d) -> p h d", h=NH)
        tt(Sst3, Sst3, bc(eglast.unsqueeze(2), [128, NH, 128]), ALU.mult, [Sst, gsm], [Sst])
        tt(Sst.ap, Sst.ap, psn.ap, ALU.add, [Sst, psn], [Sst])
        if _KSUB <= 8:
            continue
        osb = scr.ap[:, 0:512]
        osq = scr.ap[:, 512:1024]
        oab = qkb.ap[:, 0:512]
        cp(osb, po.ap, [po], [scr], e='act')
        tt(osq, osb, osb, ALU.mult, [scr], [scr])
        red(oss.ap, osq.rearrange("p (h d) -> p h d", h=NH), ALU.add, [scr], [oss])
        rsqrt(oss.ap, oss.ap, 1.0 / HD, [oss], [oss])
        osb3 = osb.rearrange("p (h d) -> p h d", h=NH)
        tt(osb3, osb3, bc(oss.ap.unsqueeze(2), [128, NH, 128]), ALU.mult, [scr, oss], [scr])
        tt(osb3, osb3, bc(gnb.ap.unsqueeze(1), [128, NH, 128]), ALU.mult, [scr, gnb], [scr])
        tt(oab, osb, zs.ap, ALU.mult, [scr, zs], [qkb])
        pq = bank()
        for h in range(NH):
            tr(pbf(pq)[:, h * 128:(h + 1) * 128], oab[:, h * 128:(h + 1) * 128], ident_b.ap, [qkb], [pq], inc=(h == NH - 1))
        cp(oaT3[:, :, tsl], pbf(pq)[:, 0:512].rearrange("p (h t) -> p h t", h=NH), [pq], [oaT])
    S.dma('sp', gp.rearrange("h k v -> k h v"), Sst.ap.rearrange("p (h v) -> p h v", h=NH), reads=[Sst], is_output=True)

    pagesum_chunks(NS * NCH)
    S.barrier()
    rrl[0] = list(range(8))
    A.top = O_WORK
    P4 = NS
    prq = A.f32(1536, parts=P4, name="prq")
    cats = A.f32(D, parts=P4, name="cats")
    catb = A.bf(D, parts=P4, name="catb")
    S.dma('sp', prq.ap, sqd[:, :], writes=[prq])
    S.dma('sp', cats.ap[:, 0:512], soad[:, :], writes=[cats])
    pti = A.i32(NPG, parts=16, name="pti")
    ptf = A.f32(NPG, parts=16, name="ptf")
    qbc = A.f32(NS * 512, name="qbc")
    pr = A.f32(NH * 7 * 128, name="pr")
    pgs = A.f32(16, name="pgs")
    gT = A.f32(NBX, parts=16, name="gT")
    t8 = A.f32(8, parts=16, name="t8")
    i8 = A.u32(8, parts=16, name="i8")
    i8f = A.f32(8, parts=16, name="i8f")
    eqb = A.f32(NBX, parts=16, name="eqb")
    eq2 = A.f32(NBX, parts=16, name="eq2")
    physf = A.f32(6, parts=16, name="physf")
    R16 = A.f32(96, parts=16, name="R16")
    idxf = A.f32(96, name="idxf")
    idxi = A.i32(96, name="idxi")
    Kg = A.f32(NH * 7 * 128, name="Kg")
    Vg = A.f32(NH * 7 * 128, name="Vg")
    Kg4 = Kg.ap.rearrange("p (h g d) -> p h g d", h=NH, g=7)
    Vg4 = Vg.ap.rearrange("p (h g d) -> p h g d", h=NH, g=7)
    pr4 = pr.ap.rearrange("p (h g d) -> p h g d", h=NH, g=7)
    sl = A.f32(28, name="sl")
    sl3 = sl.ap.rearrange("p (h g) -> p h g", h=NH)
    pmx = A.f32(4, name="pmx")
    m4 = A.f32(1, parts=4, name="m4")
    Rm = A.f32(4, parts=4, name="Rm")
    ngm = A.f32(4, name="ngm")
    Ps = A.f32(28, name="Ps")
    Ps3 = Ps.ap.rearrange("p (h g) -> p h g", h=NH)
    PZ = A.f32(NS * 28 * 4, name="PZ")
    PZ5 = PZ.ap.rearrange("p (b h g c) -> p b h g c", b=NS, h=NH, g=7)
    psr = A.f32(4, name="psr")
    obacc = A.f32(512, parts=P4, name="obacc")
    denacc = A.f32(4, parts=P4, name="denacc")

    assert A.top <= 36100, A.top
    for b in range(NS):
        S.dma('sp', pti.ap[b * NH:(b + 1) * NH, :], pt[b:b + 1, :].partition_broadcast(NH), writes=[pti])
    cp(ptf.ap, pti.ap, [pti], [ptf])
    ckr = ck.rearrange("n r h d -> (n r h) d")
    cvr = cv.rearrange("n r h d -> (n r h) d")
    for b in range(NS):
        ts(pr.ap[0:P4, b * 512:(b + 1) * 512], prq.ap[:, 0:512], ident_f.ap[0:P4, b:b + 1], None, ALU.mult, None, [prq, ident_f], [pr])
        pq = bank()
        mm(pq.ap, [(ones_f.ap[0:P4, :], pr.ap[0:P4, b * 512:(b + 1) * 512])], [ones_f, pr], [pq])
        cp(qbc.ap[:, b * 512:(b + 1) * 512], pq.ap, [pq], [qbc], e='act')
    tt(pr.ap[0:NPG, 0:2048], acc.ap[0:NPG, :], qbc.ap[0:NPG, :], ALU.mult, [acc, qbc], [pr])
    red(pgs.ap[0:NPG, :], pr.ap[0:NPG, 0:2048].rearrange("p (g d) -> p g d", g=16), ALU.add, [pr], [pgs])
    pq = bank()
    mm(pq.ap[0:16, 0:NB], [(pgs.ap[0:NPG, :], pairsel.ap[0:NPG, 0:NB])], [pgs, pairsel], [pq])
    cp(gT.ap[:, 0:NB], pq.ap[0:16, 0:NB], [pq], [gT])
    S.op('dve', lambda e: e.max(out=t8.ap, in_=gT.ap[:, 0:NB]), reads=[gT], writes=[t8])
    S.op('dve', lambda e: e.max_index(out=i8.ap, in_max=t8.ap, in_values=gT.ap[:, 0:NB]), reads=[gT, t8], writes=[i8])
    cp(i8f.ap, i8.ap, [i8], [i8f])
    ptf3 = ptf.ap.rearrange("p (n j) -> p n j", j=2)
    for k in range(3):
        ts(eqb.ap[:, 0:NB], iotaNB.ap[:, 0:NB], i8f.ap[:, k:k + 1], None, ALU.is_equal, None, [iotaNB, i8f], [eqb])
        for j in range(2):
            tt(eq2.ap[:, 0:NB], eqb.ap[:, 0:NB], ptf3[:, :, j], ALU.mult, [eqb, ptf], [eq2])
            red(physf.ap[:, k * 2 + j:k * 2 + j + 1], eq2.ap[:, 0:NB], ALU.add, [eq2], [physf])
    R163 = R16.ap.rearrange("p (g c) -> p g c", g=16)
    for g in range(16):
        ts(R163[:, g, :], physf.ap, ident_f.ap[0:16, g:g + 1], None, ALU.mult, None, [physf, ident_f], [R16])
    pq = bank()
    mm(pq.ap[:, 0:96], [(ones_f.ap[0:16, :], R16.ap)], [ones_f, R16], [pq])
    stt(idxf.ap, pq.ap[:, 0:96], 512.0, rowoff.ap, ALU.mult, ALU.add, [pq, rowoff], [idxf])
    ts(idxf.ap, idxf.ap, 0.0, float(NPHYS * 512 - 1), ALU.max, ALU.min, [idxf], [idxf])
    cp(idxi.ap, idxf.ap, [idxf], [idxi])
    mset(PZ.ap, 0.0, [PZ])
    for b in range(NS):
        mset(Kg4[:, :, 6, :], 0.0, [Kg])
        mset(Vg4[:, :, 6, :], 0.0, [Vg])
        S.dma('pool', Kg4[0:1, :, 6, :], prq.ap[b:b + 1, 512:1024].rearrange("p (h d) -> p h d", h=NH), reads=[prq], writes=[Kg])
        S.dma('pool', Vg4[0:1, :, 6, :], prq.ap[b:b + 1, 1024:1536].rearrange("p (h d) -> p h d", h=NH), reads=[prq], writes=[Vg])
        for h in range(NH):
            for kj in range(6):
                col = (b * NH + h) * 6 + kj
                S.dma('pool', Kg4[:, h, kj, :], ckr[:, :], writes=[Kg], reads=[idxi],
                      indirect=bass.IndirectOffsetOnAxis(ap=idxi.ap[:, col:col + 1].bitcast(U32), axis=0))
                S.dma('pool', Vg4[:, h, kj, :], cvr[:, :], writes=[Vg], reads=[idxi],
                      indirect=bass.IndirectOffsetOnAxis(ap=idxi.ap[:, col:col + 1].bitcast(U32), axis=0))
        qb_b = qbc.ap[:, b * 512:(b + 1) * 512].rearrange("p (h d) -> p h d", h=NH)
        for h in range(NH):
            tt(pr4[:, h], Kg4[:, h], bc(qb_b[:, h, :].unsqueeze(1), [128, 7, 128]), ALU.mult, [Kg, qbc], [pr])
        red(sl.ap, pr.ap.rearrange("p (g d) -> p g d", g=28), ALU.add, [pr], [sl])
        ts(sl3[:, :, 6], sl3[:, :, 6], negrow.ap, None, ALU.add, None, [sl, negrow], [sl])
        red(pmx.ap, sl3, ALU.max, [sl], [pmx])
        pq = bank()
        tr(pq.ap[0:4, 0:128], pmx.ap, ident_f.ap, [pmx], [pq])
        red(m4.ap, pq.ap[0:4, 0:128], ALU.max, [pq], [m4])
        ts(Rm.ap, ident_f.ap[0:4, 0:4], m4.ap, None, ALU.mult, None, [ident_f, m4], [Rm])
        pq = bank()
        mm(pq.ap[:, 0:4], [(ones_f.ap[0:4, :], Rm.ap)], [ones_f, Rm], [pq])
        ts(ngm.ap, pq.ap[:, 0:4], -SCALE, None, ALU.mult, None, [pq], [ngm])
        for h in range(NH):
            act(Ps3[:, h, :], sl3[:, h, :], AF.Exp, [sl, ngm], [Ps], scale=SCALE, bias=ngm.ap[:, h:h + 1])
        cp(PZ5[:, b, :, :, b], Ps3, [Ps], [PZ])
        red(psr.ap, Ps3, ALU.add, [Ps], [psr])
        pa = bank()
        for h in range(NH):
            mm(pa.ap[0:P4, h * 128:(h + 1) * 128], [(PZ5[:, b, h, g, :], Vg4[:, h, g, :]) for g in range(7)], [PZ, Vg], [pa])
        pd = bank()
        mm(pd.ap[0:P4, 0:4], [(oh[:, b, :], psr.ap)], [onehot4, psr], [pd])
        if b == 0:
            cp(obacc.ap, pa.ap[0:P4, :], [pa], [obacc])
            cp(denacc.ap, pd.ap[0:P4, 0:4], [pd], [denacc])
        else:
            tt(obacc.ap, obacc.ap, pa.ap[0:P4, :], ALU.add, [obacc, pa], [obacc])
            tt(denacc.ap, denacc.ap, pd.ap[0:P4, 0:4], ALU.add, [denacc, pd], [denacc])
    recip(denacc.ap, [denacc])
    tt(cats.ap[:, 512:1024].rearrange("p (h d) -> p h d", h=NH), obacc.ap.rearrange("p (h d) -> p h d", h=NH),
       bc(denacc.ap.unsqueeze(2), [P4, NH, 128]), ALU.mult, [obacc, denacc], [cats])
    cp(catb.ap, cats.ap, [cats], [catb])
    pb = bank()
    for c in range(KC):
        tr(pbf(pb)[:, c * NS:(c + 1) * NS], catb.ap[:, c * 128:(c + 1) * 128], ident_b.ap[0:P4, 0:P4], [catb], [pb], inc=(c == KC - 1))
    cp(catTs.ap, pbf(pb)[:, 0:KC * NS], [pb], [catTs])
    if _STOP == "A":
        S.finish()
        return nc
    S.barrier()
    rrl[0] = list(range(8))
    O_WO, O_WG, O_WU, O_VBT, O_BW, O_CW, O_WD = 2500, 6600, 17900, 29400, 17900, 29400, 36200
    A.top = O_VBT
    vbt = A.bf(NT * 512, name="vbt")
    vbt3 = vbt.ap.rearrange("p (m c) -> p m c", m=NT)
    assert A.top <= O_P2, A.top
    A.top = O_BW
    NSLOT = 2
    slots = []
    for si in range(NSLOT):
        d_ = {}
        d_['Pm'] = A.bf(T, name="Pm%d" % si)
        d_['PT'] = A.bf(NT * 128, name="PT%d" % si)
        d_['mx'] = A.f32(8, name="mx%d" % si)
        d_['negm'] = A.f32(1, name="negm%d" % si)
        d_['pbias'] = A.f32(8, name="pbias%d" % si)
        d_['rs'] = A.f32(12, name="rs%d" % si)
        d_['rinv'] = A.f32(1, name="rinv%d" % si)
        d_['dsb'] = A.f32(128, name="dsb%d" % si)
        d_['obb'] = A.bf(128, name="obb%d" % si)
        d_['banks'] = [PSB[4 * si + j] for j in range(4)]
        d_['rr'] = 0
        slots.append(d_)
    assert A.top <= O_VBT, A.top
    for m in range(NT):
        S.dma('pool', vbt3[:, m, :], vp[m * 128:(m + 1) * 128, :], writes=[vbt])
    A.top = O_WO
    wo = A.bf(KC * D, name="wo")
    A.top = O_WG
    wg = A.bf(KC * DFF, name="wg")
    assert A.top <= O_WU, A.top
    wo3 = wo.ap.rearrange("p (c n) -> p c n", c=KC)
    wg3 = wg.ap.rearrange("p (c n) -> p c n", c=KC)
    for c in range(KC):
        S.dma('pool', wo3[:, c, :], wod[c * 128:(c + 1) * 128, :], writes=[wo])
    for c in range(KC):
        for (a0, a1) in ((0, 1408), (1408, DFF)):
            S.dma('pool', wg3[:, c, a0:a1], wgd[c * 128:(c + 1) * 128, a0:a1], writes=[wg])

    def unitB(h, i, sl):
        Pm, PT_, mx, negm, pbias, rs, rinv, dsb, obb = (sl['Pm'], sl['PT'], sl['mx'], sl['negm'], sl['pbias'], sl['rs'],
                                                        sl['rinv'], sl['dsb'], sl['obb'])
        PT3 = PT_.ap.rearrange("p (j q) -> p j q", j=NT)

        def sbank():
            b_ = sl['banks'][sl['rr'] % 4]
            sl['rr'] += 1
            return b_
        tsl = slice(i * 128, (i + 1) * 128)
        nk = i + 1
        nbk = i // 2
        ng = (nk + 3) // 4
        lb = []
        for g in range(ng):
            w = min(512, nk * 128 - g * 512)
            pl = sbank()
            mm(pl.ap[:, 0:w], [(qbT3[:, h, tsl], kbT3[:, h, g * 512:g * 512 + w])], [qbT, kbT], [pl])
            lb.append((pl, w))
        yield
        for g, (pl, w) in enumerate(lb):
            red(mx.ap[:, g:g + 1], pl.ap[:, 0:w], ALU.max, [pl], [mx])
        red(negm.ap, mx.ap[:, 0:ng], ALU.max, [mx], [negm])
        ts(negm.ap, negm.ap, -SCALE, None, ALU.mult, None, [negm], [negm])
        if nbk > 0:
            ts(pbias.ap[:, 0:nbk], sel304[:, i, h, 0:nbk], negm.ap, NEG, ALU.add, ALU.add, [sel30, negm], [pbias])
        pl = lb[i // 4][0]
        c0 = (i % 4) * 128
        tt(dsb.ap, pl.ap[:, c0:c0 + 128], cmask.ap, ALU.add, [pl, cmask], [dsb])
        yield
        ncol = 0
        for n in range(nbk):
            pl = lb[n // 2][0]
            c0 = (n % 2) * 256
            act(Pm.ap[:, n * 256:(n + 1) * 256], pl.ap[:, c0:c0 + 256], AF.Exp, [pl, pbias], [Pm, rs],
                scale=SCALE, bias=pbias.ap[:, n:n + 1], accum_out=rs.ap[:, ncol:ncol + 1])
            ncol += 1
        if i % 2 == 1:
            j = i - 1
            pl = lb[j // 4][0]
            c0 = (j % 4) * 128
            act(Pm.ap[:, j * 128:(j + 1) * 128], pl.ap[:, c0:c0 + 128], AF.Exp, [pl, negm], [Pm, rs],
                scale=SCALE, bias=negm.ap, accum_out=rs.ap[:, ncol:ncol + 1])
            ncol += 1
        act(Pm.ap[:, i * 128:(i + 1) * 128], dsb.ap, AF.Exp, [dsb, negm], [Pm, rs],
            scale=SCALE, bias=negm.ap, accum_out=rs.ap[:, ncol:ncol + 1])
        ncol += 1
        yield
        red(rinv.ap, rs.ap[:, 0:ncol], ALU.add, [rs], [rinv])
        recip(rinv.ap, [rinv])
        for g in range((nk + 7) // 8):
            n8 = min(8, nk - g * 8)
            pq = sbank()
            for j in range(n8):
                jj = g * 8 + j
                tr(pbf(pq)[:, j * 128:(j + 1) * 128], Pm.ap[:, jj * 128:(jj + 1) * 128], ident_b.ap, [Pm], [pq], inc=(j == n8 - 1))
            cp(PT_.ap[:, g * 1024:g * 1024 + n8 * 128], pbf(pq)[:, 0:n8 * 128], [pq], [PT_], e=('act' if g == 0 else 'dve'))
            yield
        po = sbank()
        mm(po.ap[:, 0:128], [(PT3[:, j, :], vbt3[:, j, h * 128:(h + 1) * 128]) for j in range(nk)], [PT_, vbt], [po])
        yield
        ts(obb.ap, po.ap[:, 0:128], rinv.ap, None, ALU.mult, None, [po, rinv], [obb])
        pq = sbank()
        tr(pbf(pq)[:, 0:128], obb.ap, ident_b.ap, [obb], [pq])
        yield
        cp(obT3[:, h, tsl], pbf(pq)[:, 0:128], [pq], [obT], e='act')

    units = [(h, i) for i in range(NT) for h in range(NH)]
    order = []
    lo, hi = 0, len(units) - 1
    while lo <= hi:
        order.append(units[hi])
        if lo != hi:
            order.append(units[lo])
        lo += 1
        hi -= 1
    pend = list(order)
    live = [None] * NSLOT
    while pend or any(g_ is not None for g_ in live):
        for si in range(NSLOT):
            if live[si] is None and pend:
                h_, i_ = pend.pop(0)
                live[si] = unitB(h_, i_, slots[si])
            if live[si] is not None:
                try:
                    next(live[si])
                except StopIteration:
                    live[si] = None

    if _STOP == "B":
        S.finish()
        return nc
    S.barrier()
    rrl[0] = list(range(8))
    A.top = O_WU
    wu = A.bf(KC * DFF, name="wu")
    assert A.top <= O_CW, A.top
    wu3 = wu.ap.rearrange("p (c n) -> p c n", c=KC)
    for c in range(KC):
        for (a0, a1) in ((0, 1408), (1408, DFF)):
            S.dma('pool', wu3[:, c, a0:a1], wud[c * 128:(c + 1) * 128, a0:a1], writes=[wu])
    A.top = O_CW
    xr = [A.f32(D, name="xr0"), A.f32(D, name="xr1")]
    x2b = A.f32(D, name="x2b")
    hn = A.bf(D, name="hn")
    hT = A.bf(D, name="hT")
    sgt = A.bf(512, name="sgt")
    actT = A.bf(FC * 128, name="actT")
    yb = A.f32(D, name="yb")
    st = A.f32(4, name="st")
    assert A.top <= O_WD, A.top
    catTs3 = catTs.ap.rearrange("p (c b) -> p c b", c=KC)

    def tailA(x_, nt, catfn, dst):
        for n in range(2):
            pj = bank()
            mm(pj.ap[0:nt, :], [(catfn(c)[0], wo3[:, c, n * 512:(n + 1) * 512]) for c in range(KC)], [wo] + catfn(0)[1], [pj])
            tt(dst.ap[0:nt, n * 512:(n + 1) * 512], x_.ap[0:nt, n * 512:(n + 1) * 512], pj.ap[0:nt, :], ALU.add, [x_, pj], [dst])

    def tailB(x2_, nt, ydst):
        act(hn.ap[0:nt, :], x2_.ap[0:nt, :], AF.Square, [x2_], [hn, st], accum_out=st.ap[0:nt, 0:1])
        rsqrt(st.ap[0:nt, 0:1], st.ap[0:nt, 0:1], 1.0 / D, [st], [st])
        ts(hn.ap[0:nt, :], x2_.ap[0:nt, :], st.ap[0:nt, 0:1], None, ALU.mult, None, [x2_, st], [hn])
        pb_ = bank()
        for c in range(KC):
            tr(pbf(pb_)[:, c * nt:(c + 1) * nt], hn.ap[0:nt, c * 128:(c + 1) * 128], ident_b.ap[0:nt, 0:nt], [hn], [pb_], inc=(c == KC - 1))
        hTv = hT.ap[:, 0:KC * nt].rearrange("p (c t) -> p c t", c=KC)
        tt(hTv, pbf(pb_)[:, 0:KC * nt].rearrange("p (c t) -> p c t", c=KC), bc(n2.ap.unsqueeze(2), [128, KC, nt]), ALU.mult, [pb_, n2], [hT])
        aTv = actT.ap[:, 0:FC * nt].rearrange("p (f t) -> p f t", f=FC)
        for f0 in range(0, FC, 4):
            nf = min(4, FC - f0)
            pg_ = bank()
            pu_ = bank()
            for j in range(nf):
                f = f0 + j
                mm(pg_.ap[:, j * nt:(j + 1) * nt], [(wg3[:, c, f * 128:(f + 1) * 128], hTv[:, c, :]) for c in range(KC)], [wg, hT], [pg_])
                mm(pu_.ap[:, j * nt:(j + 1) * nt], [(wu3[:, c, f * 128:(f + 1) * 128], hTv[:, c, :]) for c in range(KC)], [wu, hT], [pu_])
            act(sgt.ap[:, 0:nf * nt], pg_.ap[:, 0:nf * nt], AF.Silu, [pg_], [sgt])
            tt(actT.ap[:, f0 * nt:(f0 + nf) * nt], sgt.ap[:, 0:nf * nt], pu_.ap[:, 0:nf * nt], ALU.mult, [sgt, pu_], [actT])
        for n in range(2):
            pj = bank()
            mm(pj.ap[0:nt, :], [(aTv[:, f, :], wd3[:, f, n * 512:(n + 1) * 512]) for f in range(FC)], [wd, actT], [pj])
            tt(x2_.ap[0:nt, n * 512:(n + 1) * 512], x2_.ap[0:nt, n * 512:(n + 1) * 512], pj.ap[0:nt, :], ALU.add, [x2_, pj], [x2_])
        act(hn.ap[0:nt, :], x2_.ap[0:nt, :], AF.Square, [x2_], [hn, st], accum_out=st.ap[0:nt, 1:2])
        rsqrt(st.ap[0:nt, 1:2], st.ap[0:nt, 1:2], 1.0 / D, [st], [st])
        stt(yb.ap[0:nt, :], x2_.ap[0:nt, :], st.ap[0:nt, 1:2], fnb.ap[0:nt, :], ALU.mult, ALU.mult, [x2_, st, fnb], [yb])
        S.dma('sp', ydst, yb.ap[0:nt, :], reads=[yb], is_output=True)

    for m in range(NT):
        tsl = slice(m * 128, (m + 1) * 128)
        xb = xr[m % 2]
        S.dma('sp', xb.ap, xp[tsl, :], writes=[xb])

        def catfn(c, tsl=tsl):
            if c < 4:
                return (oaT3[:, c, tsl], [oaT])
            return (obT3[:, c - 4, tsl], [obT])
        tailA(xb, 128, catfn, x2b)
        S.dma('sp', x2s[tsl, :], x2b.ap, reads=[x2b])
    S.barrier()
    A.top = O_WD
    wd = A.bf(FC * D, name="wd")
    xsb2 = A.f32(D, parts=NS, name="xsb2")
    x2sm = A.f32(D, parts=NS, name="x2sm")
    assert A.top <= 53200
    wd3 = wd.ap.rearrange("p (c n) -> p c n", c=FC)
    for c in range(FC):
        S.dma('pool', wd3[:, c, :], wdd[c * 128:(c + 1) * 128, :], writes=[wd])
    for m in range(NT):
        tsl = slice(m * 128, (m + 1) * 128)
        xb = xr[m % 2]
        S.dma('sp', xb.ap, x2s[tsl, :], writes=[xb])
        tailB(xb, 128, yp[tsl, :])
    S.dma('sp', xsb2.ap, xs[:, :], writes=[xsb2])
    tailA(xsb2, NS, lambda c: (catTs3[:, c, :], [catTs]), x2sm)
    tailB(x2sm, NS, ys[:, :])
    S.finish()
    return nc


_CACHE = {}


def _rot_tables(pos):
    half = 16
    inv = (np.float32(500000.0) ** (-(np.arange(half, dtype=np.float32)) / np.float32(half))).astype(np.float32)
    ang = pos.astype(np.float32)[:, None] * inv[None, :]
    cos = np.cos(ang).astype(np.float32)
    sin = np.sin(ang).astype(np.float32)
    tab = np.stack([np.broadcast_to(cos[:, None, :], (len(pos), 8, half)),
                    np.broadcast_to(sin[:, None, :], (len(pos), 8, half))], axis=1)
    return np.ascontiguousarray(tab.reshape(len(pos), 256), dtype=np.float32)


def kernel(x_prompt, x_sample, cache_k, cache_v, page_table, state_gdn, state_conv, norm1_w, w_in,
           conv_w, a_log, dt_bias, gdn_norm_w, w_out, norm2_w, w_gate, w_up, w_down, final_norm_w):
    f = lambda a: np.ascontiguousarray(np.asarray(a), dtype=np.float32)
    NPG = page_table.shape[1]
    PAST = NPG * 128
    if PAST not in _CACHE:
        _CACHE[PAST] = build(PAST)
    nc = _CACHE[PAST]
    ck = f(cache_k[0])
    cv = f(cache_v[0])
    shared = {
        "ck": ck, "cv": cv,
        "n1w": f(np.asarray(norm1_w[0]).reshape(KC, 128).T),
        "n2w": f(np.asarray(norm2_w[0]).reshape(KC, 128).T),
        "fnw": f(np.broadcast_to(np.asarray(final_norm_w)[None, :], (128, D))),
        "w_in": f(w_in[0]),
        "cw": f(np.asarray(conv_w[0]).T.reshape(12, 128, 4).transpose(1, 0, 2)),
        "cw4": f(np.broadcast_to(np.asarray(conv_w[0])[None], (NS, 4, GQ))),
        "alog": f(np.broadcast_to(np.asarray(a_log[0])[None, :], (128, NH))),
        "dtb": f(np.broadcast_to(np.asarray(dt_bias[0])[None, :], (128, NH))),
        "gnw": f(np.broadcast_to(np.asarray(gdn_norm_w[0])[None, :], (128, HD))),
        "wo": f(w_out[0]), "wg": f(w_gate[0]), "wu": f(w_up[0]), "wd": f(w_down[0]),
        "csp": _rot_tables(np.arange(T)),
        "css": _rot_tables(np.full((NS,), PAST)),
    }
    in_maps = []
    for c in range(NCORES):
        m = dict(shared)
        m["xp"] = f(x_prompt[c])
        m["xs"] = f(x_sample[NS * c:NS * (c + 1), 0, :])
        m["pt"] = np.ascontiguousarray(np.asarray(page_table[NS * c:NS * (c + 1)]), dtype=np.int32)
        m["sg"] = f(state_gdn[0, NS * c:NS * (c + 1)])
        m["sc"] = f(state_conv[0, NS * c:NS * (c + 1)])
        in_maps.append(m)
    res = run_bass_kernel_spmd(nc, in_maps, core_ids=list(range(NCORES)))
    R = res.results
    cat = lambda k: np.stack([np.asarray(R[c][k]) for c in range(NCORES)], axis=0)
    y_prompt = cat("yp")
    y_sample = cat("ys").reshape(NS * NCORES, 1, D)
    k_prompt = cat("kp").reshape(1, NCORES, T, NH, HD)
    v_prompt = cat("vp").reshape(1, NCORES, T, NH, HD)
    k_sample = cat("ks").reshape(1, NS * NCORES, 1, NH, HD)
    v_sample = cat("vs").reshape(1, NS * NCORES, 1, NH, HD)
    gdn_prompt = cat("gp").reshape(1, NCORES, NH, HD, HD)
    gdn_sample = cat("gs").reshape(1, NS * NCORES, NH, HD, HD)
    conv_prompt = cat("cp").reshape(1, NCORES, 3, GQ)
    conv_sample = cat("cs").reshape(1, NS * NCORES, 3, GQ)
    return (y_prompt, y_sample, k_prompt, v_prompt, k_sample, v_sample, gdn_prompt, gdn_sample, conv_prompt, conv_sample)
```
